# Optimizing a Trainium2 kernel written in Bass

```python
import math
import jax, jax.numpy as jnp
from jax import lax
import numpy as np

D_MODEL = 1024
BATCH = 8
SEQ = 4096
DEPTH = 2

N_META = 16
N_MIXERS = 2
EXPAND = 2
D_INNER = EXPAND * D_MODEL
RMS_EPS = 1e-6
N_RWKV_LAYERS = (DEPTH + 1) // 2
N_MLA_LAYERS = DEPTH // 2

RWKV_HEAD = 64
RWKV_HEADS = D_INNER // RWKV_HEAD
DECAY_LORA = 64
AAA_LORA = 64
RWKV_WIDTHS = (D_INNER, D_INNER, D_INNER, D_INNER, DECAY_LORA, AAA_LORA)
N_LERP = len(RWKV_WIDTHS)
RWKV_IN = sum(RWKV_WIDTHS)
GN_EPS = 64e-5

MLA_HEADS = 16
QK_NOPE = 128
QK_ROPE = 64
V_HEAD = 128
Q_LORA = 512
KV_LORA = 256
MLA_IN = Q_LORA + KV_LORA + QK_ROPE + D_INNER
ROPE_THETA = 10000.0
Q_BLOCK = 128

kernel_name = "hybrid_rwkv7_mla_meta_sandwich"


def _rms_norm(x, g):
    xf = x.astype(jnp.float32)
    y = xf * lax.rsqrt(jnp.mean(xf * xf, axis=-1, keepdims=True) + RMS_EPS)
    return (y * g.astype(jnp.float32)).astype(x.dtype)


def _rope(x, pos):
    half = x.shape[-1] // 2
    inv_freq = jnp.exp(-math.log(ROPE_THETA) * jnp.arange(half, dtype=jnp.float32) / half)
    ang = pos[:, None] * inv_freq[None, :]
    ang = ang.reshape((1, ang.shape[0]) + (1,) * (x.ndim - 3) + (half,))
    cos, sin = jnp.cos(ang), jnp.sin(ang)
    xf = x.astype(jnp.float32)
    x1, x2 = xf[..., :half], xf[..., half:]
    return jnp.concatenate([x1 * cos - x2 * sin, x2 * cos + x1 * sin], axis=-1).astype(x.dtype)


def _wkv7_scan(r, decay, k, v, a_vec, b_vec):
    B, L, H, N = r.shape
    xs = tuple(jnp.moveaxis(t.astype(jnp.float32), 1, 0) for t in (r, decay, k, v, a_vec, b_vec))

    def step(S, inp):
        r_t, w_t, k_t, v_t, a_t, b_t = inp
        sa = jnp.einsum('bhij,bhj->bhi', S, a_t)
        S = S * w_t[:, :, None, :] + sa[..., None] * b_t[:, :, None, :] + v_t[..., None] * k_t[:, :, None, :]
        return S, jnp.einsum('bhij,bhj->bhi', S, r_t)

    S0 = jnp.zeros((B, H, N, N), jnp.float32)
    _, ys = lax.scan(step, S0, xs)
    return jnp.moveaxis(ys, 0, 1)


def _rwkv7_mixer(x, mu, w_in, w0, w2, a0, a2, k_k, k_a, r_k, ln_w, ln_b, w_out):
    B, L, _ = x.shape
    H, N = RWKV_HEADS, RWKV_HEAD
    x_prev = jnp.pad(x, ((0, 0), (1, 0), (0, 0)))[:, :-1]
    dx = x_prev - x
    widths = np.array(RWKV_WIDTHS)
    mu_cols = jnp.repeat(mu.T, widths, axis=1, total_repeat_length=RWKV_IN)
    w_comb = jnp.concatenate([w_in, mu_cols * w_in], axis=0)
    h = jnp.concatenate([x, dx], axis=-1) @ w_comb
    r, k, v, g, w_lo, a_lo = jnp.split(h, list(np.cumsum(widths)[:-1]), axis=-1)
    w = -jax.nn.softplus(-(w0 + jnp.tanh(w_lo) @ w2)) - 0.5
    a = jax.nn.sigmoid(a0 + a_lo @ a2)
    heads = lambda t: t.reshape(B, L, H, N)
    r, k, v, w, a = heads(r), heads(k), heads(v), heads(w), heads(a)
    k_k, k_a, r_k = k_k.reshape(H, N), k_a.reshape(H, N), r_k.reshape(H, N)
    kk = (k * k_k).astype(jnp.float32)
    kk = kk / jnp.maximum(jnp.linalg.norm(kk, axis=-1, keepdims=True), 1e-12)
    k = k * (1 + (a - 1) * k_a)
    decay = jnp.exp(-jnp.exp(w.astype(jnp.float32)))
    y = _wkv7_scan(r, decay, k, v, -kk, kk * a.astype(jnp.float32))
    mean = jnp.mean(y, axis=-1, keepdims=True)
    var = jnp.mean(jnp.square(y - mean), axis=-1, keepdims=True)
    y = ((y - mean) * lax.rsqrt(var + GN_EPS)).reshape(B, L, D_INNER)
    y = y * ln_w.astype(jnp.float32) + ln_b.astype(jnp.float32)
    bonus = (jnp.sum(r * k * r_k, axis=-1, keepdims=True) * v).reshape(B, L, D_INNER)
    y = (y + bonus.astype(jnp.float32)).astype(x.dtype) * jax.nn.silu(g)
    return y @ w_out


def _causal_block_attention(q_nope, q_rope, k_nope, k_rope, v):
    L = q_nope.shape[1]
    scale = (QK_NOPE + QK_ROPE) ** -0.5
    neg = jnp.finfo(jnp.float32).min
    bounds = [(0, N_META)] + [(s, min(s + Q_BLOCK, L)) for s in range(N_META, L, Q_BLOCK)]
    outs = []
    for s, e in bounds:
        sc = (jnp.einsum('bqhd,bkhd->bhqk', q_nope[:, s:e], k_nope[:, :e])
              + jnp.einsum('bqhr,bkr->bhqk', q_rope[:, s:e], k_rope[:, :e])).astype(jnp.float32) * scale
        mask = jnp.arange(s, e)[:, None] >= jnp.arange(e)[None, :]
        p = jax.nn.softmax(jnp.where(mask, sc, neg), axis=-1).astype(v.dtype)
        outs.append(jnp.einsum('bhqk,bkhd->bqhd', p, v[:, :e]))
    return jnp.concatenate(outs, axis=1)


def _mla_mixer(x, pos, w_in, q_norm, w_q_up, kv_norm, w_kv_up, w_out):
    B, L, _ = x.shape
    H = MLA_HEADS
    h = x @ w_in
    c_q, c_kv, k_rope, g = jnp.split(h, [Q_LORA, Q_LORA + KV_LORA, Q_LORA + KV_LORA + QK_ROPE], axis=-1)
    q = (_rms_norm(c_q, q_norm) @ w_q_up).reshape(B, L, H, QK_NOPE + QK_ROPE)
    q_nope, q_rope = q[..., :QK_NOPE], _rope(q[..., QK_NOPE:], pos)
    kv = (_rms_norm(c_kv, kv_norm) @ w_kv_up).reshape(B, L, H, QK_NOPE + V_HEAD)
    k_nope, v = kv[..., :QK_NOPE], kv[..., QK_NOPE:]
    k_rope = _rope(k_rope, pos)
    o = _causal_block_attention(q_nope, q_rope, k_nope, k_rope, v).reshape(B, L, D_INNER)
    return (o * jax.nn.silu(g)) @ w_out


def setup_inputs(seed: int = 0) -> dict:
    key = jax.random.key(seed)
    ks = jax.random.split(key, 24)
    f32 = jnp.float32
    nrm = lambda k, shape, s: jax.random.normal(k, shape, f32) * s
    NR, NM = N_RWKV_LAYERS, N_MLA_LAYERS
    return {
        "x": nrm(ks[0], (BATCH, SEQ, D_MODEL), 1.0),
        "meta_tokens": nrm(ks[1], (N_META, D_MODEL), 1.0),
        "norm_pre": 1.0 + nrm(ks[2], (DEPTH, D_MODEL), 0.02),
        "norm_post": 1.0 + nrm(ks[3], (DEPTH, D_MODEL), 0.02),
        "rwkv_mu": jax.random.uniform(ks[4], (NR, N_LERP, D_MODEL), f32),
        "rwkv_w_in": nrm(ks[5], (NR, D_MODEL, RWKV_IN), D_MODEL ** -0.5),
        "rwkv_w0": -2.0 + nrm(ks[6], (NR, D_INNER), 0.5),
        "rwkv_w2": nrm(ks[7], (NR, DECAY_LORA, D_INNER), 0.5 * DECAY_LORA ** -0.5),
        "rwkv_a0": nrm(ks[8], (NR, D_INNER), 0.1),
        "rwkv_a2": nrm(ks[9], (NR, AAA_LORA, D_INNER), 0.5 * AAA_LORA ** -0.5),
        "rwkv_k_k": 0.85 + nrm(ks[10], (NR, D_INNER), 0.05),
        "rwkv_k_a": 1.0 + nrm(ks[11], (NR, D_INNER), 0.05),
        "rwkv_r_k": nrm(ks[12], (NR, D_INNER), 0.1),
        "rwkv_ln_w": 1.0 + nrm(ks[13], (NR, D_INNER), 0.02),
        "rwkv_ln_b": nrm(ks[14], (NR, D_INNER), 0.02),
        "rwkv_w_out": nrm(ks[15], (NR, D_INNER, D_MODEL), D_INNER ** -0.5),
        "mla_w_in": nrm(ks[16], (NM, D_MODEL, MLA_IN), D_MODEL ** -0.5),
        "mla_q_norm": 1.0 + nrm(ks[17], (NM, Q_LORA), 0.02),
        "mla_w_q_up": nrm(ks[18], (NM, Q_LORA, MLA_HEADS * (QK_NOPE + QK_ROPE)), Q_LORA ** -0.5),
        "mla_kv_norm": 1.0 + nrm(ks[19], (NM, KV_LORA), 0.02),
        "mla_w_kv_up": nrm(ks[20], (NM, KV_LORA, MLA_HEADS * (QK_NOPE + V_HEAD)), KV_LORA ** -0.5),
        "mla_w_out": nrm(ks[21], (NM, D_INNER, D_MODEL), D_INNER ** -0.5),
    }


def reference(x, meta_tokens, norm_pre, norm_post,
              rwkv_mu, rwkv_w_in, rwkv_w0, rwkv_w2, rwkv_a0, rwkv_a2,
              rwkv_k_k, rwkv_k_a, rwkv_r_k, rwkv_ln_w, rwkv_ln_b, rwkv_w_out,
              mla_w_in, mla_q_norm, mla_w_q_up, mla_kv_norm, mla_w_kv_up, mla_w_out):
    B = x.shape[0]
    meta = jnp.broadcast_to(meta_tokens.astype(x.dtype)[None], (B, N_META, D_MODEL))
    h = jnp.concatenate([meta, x], axis=1)
    pos = jnp.arange(h.shape[1], dtype=jnp.float32)
    for i in range(DEPTH):
        j = i // N_MIXERS
        u = _rms_norm(h, norm_pre[i])
        if i % N_MIXERS == 0:
            m = _rwkv7_mixer(u, rwkv_mu[j], rwkv_w_in[j], rwkv_w0[j], rwkv_w2[j], rwkv_a0[j], rwkv_a2[j],
                             rwkv_k_k[j], rwkv_k_a[j], rwkv_r_k[j], rwkv_ln_w[j], rwkv_ln_b[j], rwkv_w_out[j])
        else:
            m = _mla_mixer(u, pos, mla_w_in[j], mla_q_norm[j], mla_w_q_up[j], mla_kv_norm[j],
                           mla_w_kv_up[j], mla_w_out[j])
        h = h + _rms_norm(m, norm_post[i])
    return h[:, N_META:]
```

```python
from contextlib import ExitStack
import math
import contextlib
import numpy as np
import concourse.bass as bass
import concourse.mybir as mybir
from concourse.bass_utils import run_bass_kernel_spmd

F32 = mybir.dt.float32
BF16 = mybir.dt.bfloat16
I32 = mybir.dt.int32
ALU = mybir.AluOpType
AF = mybir.ActivationFunctionType
AX = mybir.AxisListType


class Buf:
    __slots__ = ("name", "w", "r")

    def __init__(self, name=""):
        self.name = name
        self.w = None
        self.r = {}


class _Rec:
    def __init__(self):
        self.call = None

    def __getattr__(self, name):
        def f(*a, **k):
            self.call = (name, a, k)
            return self
        return f


class Sched:
    ENGS = ("pe", "act", "dve", "pool", "sp")
    EPOCH = 24000
    NSLOT = {"sp": 24, "pool": 12, "act": 8}
    STRICT = ("act", "dve", "pool")

    def __init__(self, nc, stack):
        self.nc = nc
        self.stack = stack
        self.ops = {e: [] for e in self.ENGS}
        self.cnt = {e: 0 for e in self.ENGS}
        self.sems = {}
        self.dcnt = {q: 0 for q in self.NSLOT}
        self.seen = {e: {} for e in self.ENGS}
        self.nwait = 0

    def sem(self, key):
        s = self.sems.get(key)
        if s is None:
            s = self.stack.enter_context(self.nc.semaphore("s_" + "_".join(str(k) for k in key)))
            self.sems[key] = s
        return s

    def _deps(self, eng, reads, writes, strict=False):
        toks = {}

        def add(tok, same_ok):
            if tok is None:
                return
            key, val = tok
            if same_ok and not strict and eng not in self.STRICT and key[0] == "e" and key[1] == eng:
                return
            if toks.get(key, -1) < val:
                toks[key] = val
        for b in reads:
            add(b.w, False)
        for b in writes:
            add(b.w, True)
            for k, v in b.r.items():
                add((k, v), True)
        out = []
        seen = self.seen[eng]
        for key, val in toks.items():
            if seen.get(key, -1) >= val:
                continue
            seen[key] = val
            out.append((key, val))
        return out

    def _mark(self, tok, reads, writes):
        key, val = tok
        for b in reads:
            if b.r.get(key, -1) < val:
                b.r[key] = val
        for b in writes:
            b.w = tok
            b.r = {}

    def op(self, eng, fn, reads=(), writes=()):
        waits = self._deps(eng, reads, writes)
        self.cnt[eng] += 1
        c = self.cnt[eng]
        key = ("e", eng, c // self.EPOCH)
        val = c % self.EPOCH
        if val == 0:
            self.cnt[eng] += 1
            c = self.cnt[eng]
            val = c % self.EPOCH
        self.sem(key)
        rec = _Rec()
        fn(rec)
        name, a, k = rec.call
        self.ops[eng].append((waits, (lambda e, name=name, a=a, k=k: getattr(e, name)(*a, **k)), key, 1))
        self._mark((key, val), reads, writes)
        self.nwait += len(waits)

    def dma(self, q, out, in_, reads=(), writes=(), **kw):
        n = self.NSLOT[q]
        idx = self.dcnt[q]
        self.dcnt[q] += 1
        slot = idx % n
        val = 16 * (idx // n + 1)
        key = ("d", q, slot)
        self.sem(key)
        waits = self._deps(q, reads, writes, strict=True)
        if val > 16:
            seen = self.seen[q]
            if seen.get(key, -1) < val - 16:
                seen[key] = val - 16
                waits.append((key, val - 16))
        self.ops[q].append((waits, (lambda e, out=out, in_=in_, kw=kw: e.dma_start(out=out, in_=in_, **kw)), key, 16))
        self._mark((key, val), reads, writes)
        self.nwait += len(waits)

    def _finals(self):
        waits = []
        for q, n in self.NSLOT.items():
            tot = self.dcnt[q]
            for slot in range(min(n, tot)):
                uses = (tot - slot + n - 1) // n
                waits.append((("d", q, slot), 16 * uses))
        for e in ("pe", "act", "dve", "pool"):
            c = self.cnt[e]
            if c > 0:
                waits.append((("e", e, c // self.EPOCH), c % self.EPOCH))
        return waits

    def run(self, barrier=True):
        nc = self.nc
        sched = self
        finals = self._finals() if barrier else []
        plan = {}
        for name in self.ENGS:
            seen = self.seen[name]
            w = []
            for k, v in finals:
                if seen.get(k, -1) < v:
                    seen[k] = v
                    w.append((k, v))
            plan[name] = w
        ops = self.ops
        self.ops = {e: [] for e in self.ENGS}

        def replay(name, eng):
            for waits, fn, key, inc in ops[name]:
                for k, v in waits:
                    eng.wait_ge(sched.sems[k], v)
                ins = fn(eng)
                ins.then_inc(sched.sems[key], inc)
            for k, v in plan[name]:
                eng.wait_ge(sched.sems[k], v)

        with nc.Block() as block:
            @block.tensor
            def _(e):
                replay("pe", e)

            @block.scalar
            def _(e):
                replay("act", e)

            @block.vector
            def _(e):
                replay("dve", e)

            @block.gpsimd
            def _(e):
                replay("pool", e)

            @block.sync
            def _(e):
                replay("sp", e)


class StopBuild(Exception):
    pass


C0 = float(np.exp(-0.5))
GN_EPS = 64e-5
RMS_EPS = 1e-6


def supertiles(NT):
    out = []
    t = 0
    while t < NT:
        n = min(4, NT - t)
        out.append((t, n))
        t += n
    return out


def consts(C):
    nc, S = C.nc, C.S
    sb = lambda n, s, d: C.stack.enter_context(nc.sbuf_tensor(n, s, d))
    C.b_const = Buf("const")
    bc = C.b_const
    C.identf = sb("identf", [128, 128], F32)
    C.ident = sb("ident", [128, 128], BF16)
    C.blkf = sb("blkf", [128, 128], F32)
    C.blk = sb("blk", [128, 128], BF16)
    C.mU = sb("mU", [128, 128], F32)
    C.mUi = sb("mUi", [128, 128], F32)
    C.mL = sb("mL", [128, 128], F32)
    C.mask4 = sb("mask4", [128, 4, 128], F32)
    C.cvals = sb("cvals", [128, 8], F32)
    C.ones_s = sb("ones_s", [128, 4, 128], F32)
    for t, pat_op, base in ((C.mU, ALU.is_gt, 0), (C.mUi, ALU.is_ge, 0), (C.mL, ALU.is_gt, 0)):
        pass
    S.op("pool", lambda e: e.memset(C.identf[:], 1.0), writes=[bc])
    S.op("pool", lambda e: e.affine_select(out=C.identf[:], in_=C.identf[:], pattern=[[-1, 128]],
                                           compare_op=ALU.is_equal, fill=0.0, base=0, channel_multiplier=1),
         reads=[bc], writes=[bc])
    S.op("pool", lambda e: e.tensor_copy(out=C.ident[:], in_=C.identf[:]), reads=[bc], writes=[bc])
    S.op("pool", lambda e: e.memset(C.mU[:], 1.0), writes=[bc])
    S.op("pool", lambda e: e.affine_select(out=C.mU[:], in_=C.mU[:], pattern=[[1, 128]],
                                           compare_op=ALU.is_gt, fill=0.0, base=0, channel_multiplier=-1),
         reads=[bc], writes=[bc])
    S.op("pool", lambda e: e.memset(C.mUi[:], 1.0), writes=[bc])
    S.op("pool", lambda e: e.affine_select(out=C.mUi[:], in_=C.mUi[:], pattern=[[1, 128]],
                                           compare_op=ALU.is_ge, fill=0.0, base=0, channel_multiplier=-1),
         reads=[bc], writes=[bc])
    S.op("pool", lambda e: e.memset(C.mL[:], 1.0), writes=[bc])
    S.op("pool", lambda e: e.affine_select(out=C.mL[:], in_=C.mL[:], pattern=[[-1, 128]],
                                           compare_op=ALU.is_gt, fill=0.0, base=0, channel_multiplier=1),
         reads=[bc], writes=[bc])
    for k in range(4):
        src = C.mU if k % 2 == 0 else C.mUi
        S.op("pool", lambda e, k=k, src=src: e.tensor_copy(out=C.mask4[:, k, :], in_=src[:]), reads=[bc], writes=[bc])
    S.op("pool", lambda e: e.memset(C.blkf[:], 0.0), writes=[bc])
    S.op("pool", lambda e: e.memset(C.blkf[0:64, 0:64], 1.0), writes=[bc])
    S.op("pool", lambda e: e.memset(C.blkf[64:128, 64:128], 1.0), writes=[bc])
    S.op("pool", lambda e: e.tensor_copy(out=C.blk[:], in_=C.blkf[:]), reads=[bc], writes=[bc])
    S.op("pool", lambda e: e.memset(C.cvals[:, 0:1], RMS_EPS), writes=[bc])
    S.op("pool", lambda e: e.memset(C.cvals[:, 1:2], GN_EPS), writes=[bc])
    S.op("pool", lambda e: e.memset(C.cvals[:, 2:3], 0.0), writes=[bc])
    S.op("pool", lambda e: e.memset(C.cvals[:, 3:4], 1.0), writes=[bc])
    S.op("pool", lambda e: e.memset(C.ones_s[:], 1.0), writes=[bc])
    S.op("pool", lambda e: e.memset(C.ones_s[:, :, 0:1], 0.0), writes=[bc])
    C.psum = [C.stack.enter_context(nc.psum_tensor(f"ps{i}", [128, 512], F32)) for i in range(8)]
    C.b_ps = [Buf(f"ps{i}") for i in range(8)]


def rmsnorm_stats(C, src_ap, ncols, ss_ap, rs_ap, junk_ap, reads, b_stat):
    S = C.S
    S.op("act", lambda e: e.activation(out=junk_ap, in_=src_ap, func=AF.Square, bias=C.cvals[:, 2:3], scale=1.0, accum_out=ss_ap),
         reads=reads + [C.b_const], writes=[b_stat])
    S.op("act", lambda e: e.activation(out=rs_ap, in_=ss_ap, func=AF.Sqrt, bias=C.cvals[:, 0:1], scale=1.0 / ncols),
         reads=[b_stat, C.b_const], writes=[b_stat])
    S.op("dve", lambda e: e.reciprocal(out=rs_ap, in_=rs_ap), reads=[b_stat], writes=[b_stat])


def layer0(C):
    nc, S = C.nc, C.S
    NT = C.NT
    L = NT * 128
    STS = supertiles(NT)
    ps, bps = C.psum, C.b_ps
    D = C.dram
    with ExitStack() as l0:
        sb0 = lambda n, s, d: l0.enter_context(nc.sbuf_tensor(n, s, d))
        pp = sb0("pp0_sb", [128, 160], F32)
        b_pp = Buf("pp")
        tw = sb0("tw", [64, L], BF16)
        al = sb0("al", [64, L], BF16)
        b_tw, b_al = Buf("tw"), Buf("al")
        S.dma("sp", pp[:], D["pp0"], writes=[b_pp])

        with ExitStack() as pa:
            sa = lambda n, s, d: pa.enter_context(nc.sbuf_tensor(n, s, d))
            uT = sa("uT", [128, 8, 1 + L], BF16)
            b_uT = Buf("uT")
            gpre = sa("gpre", [128, 1024], F32)
            b_g = Buf("gpre")
            S.dma("sp", gpre[:], D["gpre0"].partition_broadcast(128), writes=[b_g])
            S.op("pool", lambda e: e.memset(uT[:, :, 0:1], 0.0), writes=[b_uT])
            with ExitStack() as pA:
                sA = lambda n, s, d: pA.enter_context(nc.sbuf_tensor(n, s, d))
                xin = [sA(f"xin{i}", [128, 1024], F32) for i in range(2)]
                b_xin = [Buf() for _ in range(2)]
                ub = [sA(f"ub{i}", [128, 1024], BF16) for i in range(2)]
                b_ub = [Buf() for _ in range(2)]
                junk = sA("junkA", [128, 1024], BF16)
                b_junk = Buf()
                st_ss = sA("ssA", [128, NT], F32)
                st_rs = sA("rsA", [128, NT], F32)
                b_st = [Buf() for _ in range(NT)]
                for tt in range(NT):
                    x, bx = xin[tt % 2], b_xin[tt % 2]
                    u, bu = ub[tt % 2], b_ub[tt % 2]
                    S.dma("sp", x[:], D["h0"][tt * 128:(tt + 1) * 128, :], writes=[bx])
                    S.op("pool", lambda e, tt=tt: e.memset(st_ss[:, tt:tt + 1], 0.0), writes=[b_st[tt]])
                    rmsnorm_stats(C, x[:], 1024, st_ss[:, tt:tt + 1], st_rs[:, tt:tt + 1], junk[:], [bx, b_junk], b_st[tt])
                    S.op("dve", lambda e, x=x, u=u, tt=tt: e.scalar_tensor_tensor(
                        out=u[:], in0=x[:], scalar=st_rs[:, tt:tt + 1], in1=gpre[:], op0=ALU.mult, op1=ALU.mult),
                        reads=[bx, b_st[tt], b_g], writes=[bu])
                    pb = ps[tt % 2][:].bitcast(BF16)
                    for c in range(8):
                        S.op("pe", lambda e, c=c, u=u, pb=pb: e.transpose(out=pb[:, c * 128:(c + 1) * 128], in_=u[:, c * 128:(c + 1) * 128], identity=C.ident[:]),
                             reads=[bu, C.b_const], writes=[bps[tt % 2]])
                    eng = "act" if tt % 2 == 0 else "dve"
                    dst = uT[:, :, 1 + tt * 128: 1 + (tt + 1) * 128]
                    src = pb.rearrange("p (c k) -> p c k", c=8)
                    if eng == "act":
                        S.op("act", lambda e, dst=dst, src=src: e.copy(out=dst, in_=src), reads=[bps[tt % 2]], writes=[b_uT])
                    else:
                        S.op("dve", lambda e, dst=dst, src=src: e.tensor_copy(out=dst, in_=src), reads=[bps[tt % 2]], writes=[b_uT])
                S.run()

            with ExitStack() as pBC:
                sB = lambda n, s, d: pBC.enter_context(nc.sbuf_tensor(n, s, d))
                dxt = sB("dxt", [128, 8, 512], F32)
                tmpl = sB("tmpl", [128, 8, 512], F32)
                b_dx, b_tmpl = Buf(), Buf()
                xg = [sB(f"xg{i}", [128, 8, 512], BF16) for i in range(2)]
                b_xg = [Buf() for _ in range(2)]
                lerp_cnt = [0]

                def lerp(g, t0, n):
                    i = lerp_cnt[0] % 2
                    lerp_cnt[0] += 1
                    cur = uT[:, :, 1 + t0: 1 + t0 + n]
                    prev = uT[:, :, t0: t0 + n]
                    mu_bc = pp[:, g * 8:(g + 1) * 8].unsqueeze(2).to_broadcast([128, 8, n])
                    S.op("dve", lambda e: e.tensor_tensor(out=dxt[:, :, :n], in0=prev, in1=cur, op=ALU.subtract),
                         reads=[b_uT], writes=[b_dx])
                    S.op("pool", lambda e: e.tensor_tensor(out=tmpl[:, :, :n], in0=dxt[:, :, :n], in1=mu_bc, op=ALU.mult),
                         reads=[b_dx, b_pp], writes=[b_tmpl])
                    S.op("dve", lambda e: e.tensor_tensor(out=xg[i][:, :, :n], in0=tmpl[:, :, :n], in1=cur, op=ALU.add),
                         reads=[b_tmpl, b_uT], writes=[b_xg[i]])
                    return xg[i], b_xg[i]

                with ExitStack() as pB:
                    sBb = lambda n, s, d: pB.enter_context(nc.sbuf_tensor(n, s, d))
                    wlf = sBb("wlf", [128, 8, 128], F32)
                    wl = sBb("wl", [128, 8, 128], BF16)
                    b_wl = Buf()
                    S.dma("sp", wlf[:], D["w_in0"][:, 8192:8320].rearrange("(c p) n -> p c n", p=128), writes=[b_wl])
                    S.op("pool", lambda e: e.tensor_copy(out=wl[:], in_=wlf[:]), reads=[b_wl], writes=[b_wl])
                    for si, (tt0, nt) in enumerate(STS):
                        t0, n = tt0 * 128, nt * 128
                        xw, bxw = lerp(4, t0, n)
                        xa, bxa = lerp(5, t0, n)
                        pw, pa_ = ps[2 + (si % 2) * 2], ps[3 + (si % 2) * 2]
                        bpw, bpa = bps[2 + (si % 2) * 2], bps[3 + (si % 2) * 2]
                        for c in range(8):
                            S.op("pe", lambda e, c=c, xw=xw, pw=pw: e.matmul(out=pw[0:64, :n], lhsT=wl[:, c, 0:64], rhs=xw[:, c, :n], start=(c == 0), stop=(c == 7)),
                                 reads=[b_wl, bxw], writes=[bpw])
                        for c in range(8):
                            S.op("pe", lambda e, c=c, xa=xa, pa_=pa_: e.matmul(out=pa_[0:64, :n], lhsT=wl[:, c, 64:128], rhs=xa[:, c, :n], start=(c == 0), stop=(c == 7)),
                                 reads=[b_wl, bxa], writes=[bpa])
                        S.op("act", lambda e, pw=pw, t0=t0, n=n: e.activation(out=tw[:, t0:t0 + n], in_=pw[0:64, :n], func=AF.Tanh, bias=C.cvals[0:64, 2:3], scale=1.0),
                             reads=[bpw, C.b_const], writes=[b_tw])
                        S.op("dve", lambda e, pa_=pa_, t0=t0, n=n: e.tensor_copy(out=al[:, t0:t0 + n], in_=pa_[0:64, :n]),
                             reads=[bpa], writes=[b_al])
                    S.run()

                with ExitStack() as pC:
                    sC = lambda n, s, d: pC.enter_context(nc.sbuf_tensor(n, s, d))
                    wst = [sC(f"wst{i}", [128, 2048], F32) for i in range(2)]
                    b_wst = [Buf() for _ in range(2)]
                    wg = sC("wg", [128, 8, 2048], BF16)
                    b_wg = Buf()
                    stg = [sC(f"stg{i}", [128, 512], F32) for i in range(4)]
                    b_stg = [Buf() for _ in range(4)]
                    k = 0
                    for g in range(4):
                        for c in range(8):
                            S.dma("sp", wst[c % 2][:], D["w_in0"][c * 128:(c + 1) * 128, g * 2048:(g + 1) * 2048], writes=[b_wst[c % 2]])
                            if c % 2 == 0:
                                S.op("pool", lambda e, c=c: e.tensor_copy(out=wg[:, c, :], in_=wst[c % 2][:]), reads=[b_wst[c % 2]], writes=[b_wg])
                            else:
                                S.op("act", lambda e, c=c: e.copy(out=wg[:, c, :], in_=wst[c % 2][:]), reads=[b_wst[c % 2]], writes=[b_wg])
                        scr = D[("R_s", "K_s", "V_s", "G_s")[g]]
                        for si, (tt0, nt) in enumerate(STS):
                            t0, n = tt0 * 128, nt * 128
                            x_, bx_ = lerp(g, t0, n)
                            for oc in range(16):
                                pi = k % 4
                                p_, bp_ = ps[4 + pi], bps[4 + pi]
                                s_, bs_ = stg[pi], b_stg[pi]
                                k += 1
                                for c in range(8):
                                    S.op("pe", lambda e, c=c, oc=oc, x_=x_, p_=p_: e.matmul(out=p_[:, :n], lhsT=wg[:, c, oc * 128:(oc + 1) * 128], rhs=x_[:, c, :n], start=(c == 0), stop=(c == 7)),
                                         reads=[b_wg, bx_], writes=[bp_])
                                if g == 3:
                                    sv = s_[:].bitcast(BF16)[:, :n]
                                    S.op("act", lambda e, sv=sv, p_=p_: e.activation(out=sv, in_=p_[:, :n], func=AF.Silu, bias=C.cvals[:, 2:3], scale=1.0),
                                         reads=[bp_, C.b_const], writes=[bs_])
                                else:
                                    sv = s_[:, :n]
                                    if oc % 2 == 0:
                                        S.op("act", lambda e, sv=sv, p_=p_: e.copy(out=sv, in_=p_[:, :n]), reads=[bp_], writes=[bs_])
                                    else:
                                        S.op("dve", lambda e, sv=sv, p_=p_: e.tensor_copy(out=sv, in_=p_[:, :n]), reads=[bp_], writes=[bs_])
                                S.dma("sp", scr[tt0:tt0 + nt, :, oc, :].rearrange("t p k -> p t k"),
                                      sv.rearrange("p (t k) -> p t k", t=nt), reads=[bs_], writes=[Buf()])
                    S.run()
        C.l0_keep = (pp, b_pp, tw, al, b_tw, b_al)
        if getattr(C, "stop_after", None) == "l0c":
            raise StopBuild()
        layer0_wkv(C)
        if getattr(C, "stop_after", None) == "l0":
            raise StopBuild()


def layer0_wkv(C):
    nc, S = C.nc, C.S
    NT = C.NT
    L = NT * 128
    ps, bps = C.psum, C.b_ps
    D = C.dram
    pp, b_pp, tw, al, b_tw, b_al = C.l0_keep
    PW0, PA0, PKK, PKA, PRK, PLW, PLB = [48 + 16 * j for j in range(7)]
    with ExitStack() as pd:
        sb = lambda n, s, d: pd.enter_context(nc.sbuf_tensor(n, s, d))
        w2b = sb("w2b", [64, 2048], BF16)
        a2b = sb("a2b", [64, 2048], BF16)
        b_w = Buf("wres")
        with ExitStack() as pl:
            wst = [pl.enter_context(nc.sbuf_tensor(f"wst2_{i}", [128, 2048], F32)) for i in range(2)]
            b_wst = [Buf(), Buf()]
            S.dma("sp", wst[0][0:64, :], D["w2"], writes=[b_wst[0]])
            S.op("pool", lambda e: e.tensor_copy(out=w2b[:], in_=wst[0][0:64, :]), reads=[b_wst[0]], writes=[b_w])
            S.dma("sp", wst[1][0:64, :], D["a2"], writes=[b_wst[1]])
            S.op("pool", lambda e: e.tensor_copy(out=a2b[:], in_=wst[1][0:64, :]), reads=[b_wst[1]], writes=[b_w])
            S.run()
        ST = sb("ST", [128, 16, 64], F32)
        STb = sb("STb", [128, 16, 64], BF16)
        b_ST = [Buf(f"ST{g}") for g in range(4)]
        S.op("pool", lambda e: e.memset(ST[:], 0.0), writes=b_ST)
        S.op("pool", lambda e: e.memset(STb[:], 0.0), writes=b_ST)
        ARt = sb("ARt", [128, 16, 2, 128], BF16)
        BKt = sb("BKt", [128, 16, 2, 128], BF16)
        BhT = sb("BhT", [128, 2048], BF16)
        KhT = sb("KhT", [128, 2048], BF16)
        VT = sb("VT", [128, 2048], BF16)
        bonT = [sb(f"bonT{i}", [128, 16, 128], BF16) for i in range(2)]
        wc = sb("wc", [128, 16], F32)
        b_op = [Buf(f"op{q}") for q in range(4)]
        Rq = [sb(f"Rq{i}", [128, 4, 128], F32) for i in range(2)]
        Kq = [sb(f"Kq{i}", [128, 4, 128], F32) for i in range(2)]
        Vq = [sb(f"Vq{i}", [128, 4, 128], F32) for i in range(2)]
        b_in = [[Buf(), Buf(), Buf()], [Buf(), Buf(), Buf()]]
        NTMP = 12
        tmp = [sb(f"tmpD{i}", [128, 4, 128], F32) for i in range(NTMP)]
        b_tmp = [Buf() for _ in range(NTMP)]
        tmpb = [sb(f"tmpDb{i}", [128, 4, 128], BF16) for i in range(4)]
        b_tmpb = [Buf() for _ in range(4)]
        M3 = [sb(f"M3_{i}", [128, 16, 3, 128], BF16) for i in range(2)]
        Tm = [sb(f"Tm_{i}", [128, 16, 128], BF16) for i in range(2)]
        b_M3 = [[Buf() for _ in range(4)] for _ in range(2)]
        Ab = [sb(f"A_{i}", [128, 16, 128], BF16) for i in range(2)]
        ATb = [sb(f"AT_{i}", [128, 16, 128], BF16) for i in range(2)]
        Tb = [sb(f"T_{i}", [128, 16, 128], BF16) for i in range(2)]
        b_A = [[Buf() for _ in range(4)] for _ in range(2)]
        b_AT = [[Buf() for _ in range(4)] for _ in range(2)]
        b_T = [[Buf() for _ in range(4)] for _ in range(2)]
        XTb = [sb(f"XTb{i}", [128, 512], BF16) for i in range(2)]
        UTb = [sb(f"UTb{i}", [128, 512], BF16) for i in range(2)]
        b_X, b_U = [Buf(), Buf()], [Buf(), Buf()]
        yf = sb("yf", [128, 8, 64], F32)
        ysq = sb("ysq", [128, 8, 64], F32)
        b_y, b_ysq = Buf(), Buf()
        gst = sb("gst", [128, 6, 32], F32)
        b_gst = Buf()
        yn = sb("yn", [128, 2048], BF16)
        b_yn = [Buf() for _ in range(4)]
        Gt = sb("Gt0", [128, 16, 128], BF16)
        b_G = Buf()
        zf = sb("zf", [128, 8, 128], F32)
        b_zf = Buf()
        zb = [sb(f"zbw{i}", [128, 16, 128], BF16) for i in range(2)]
        b_zb = [Buf(), Buf()]

        def bc4(col0, q):
            return pp[:, col0 + 4 * q: col0 + 4 * q + 4].unsqueeze(2).to_broadcast([128, 4, 128])

        tctr = [0]

        def T_(n=1):
            i = tctr[0] % NTMP
            tctr[0] += 1
            return tmp[i], b_tmp[i]

        rot = [0]

        def bank():
            i = rot[0] % 6
            rot[0] += 1
            return ps[i], bps[i]

        def D_quarter(tt, q):
            t0 = tt * 128
            ib = (tt * 4 + q) % 2
            R, K, V, bi = Rq[ib], Kq[ib], Vq[ib], b_in[ib]
            S.dma("sp", R[:], D["R_s"][tt, :, 4 * q:4 * q + 4, :], writes=[bi[0]])
            S.dma("sp", K[:], D["K_s"][tt, :, 4 * q:4 * q + 4, :], writes=[bi[1]])
            S.dma("sp", V[:], D["V_s"][tt, :, 4 * q:4 * q + 4, :], writes=[bi[2]])
            bo = b_op[q]
            pw, bpw = bank()
            pa_, bpa = bank()
            for j in range(4):
                hp = 4 * q + j
                S.op("pe", lambda e, j=j, hp=hp: e.matmul(out=pw[:, j * 128:(j + 1) * 128], lhsT=w2b[:, hp * 128:(hp + 1) * 128], rhs=tw[:, t0:t0 + 128], start=True, stop=True),
                     reads=[b_w, b_tw], writes=[bpw])
            for j in range(4):
                hp = 4 * q + j
                S.op("pe", lambda e, j=j, hp=hp: e.matmul(out=pa_[:, j * 128:(j + 1) * 128], lhsT=a2b[:, hp * 128:(hp + 1) * 128], rhs=al[:, t0:t0 + 128], start=True, stop=True),
                     reads=[b_w, b_al], writes=[bpa])
            sw, bsw = T_()
            sa_, bsa = T_()
            for j in range(4):
                hp = 4 * q + j
                S.op("act", lambda e, j=j, hp=hp, sw=sw: e.activation(out=sw[:, j, :], in_=pw[:, j * 128:(j + 1) * 128], func=AF.Sigmoid, bias=pp[:, PW0 + hp:PW0 + hp + 1], scale=1.0),
                     reads=[bpw, b_pp], writes=[bsw])
                S.op("act", lambda e, j=j, hp=hp, sa_=sa_: e.activation(out=sa_[:, j, :], in_=pa_[:, j * 128:(j + 1) * 128], func=AF.Sigmoid, bias=pp[:, PA0 + hp:PA0 + hp + 1], scale=1.0),
                     reads=[bpa, b_pp], writes=[bsa])
            yield
            cs, bcs = T_()
            S.op("dve", lambda e, cs=cs, sw=sw: e.tensor_tensor_scan(out=cs[:].rearrange("p a b -> p (a b)"), data0=C.ones_s[:].rearrange("p a b -> p (a b)"), data1=sw[:].rearrange("p a b -> p (a b)"), initial=0.0, op0=ALU.mult, op1=ALU.add),
                 reads=[bsw, C.b_const], writes=[bcs])
            cp_, bcp = T_()
            S.op("pool", lambda e, cp_=cp_, cs=cs, sw=sw: e.tensor_tensor(out=cp_[:], in0=cs[:], in1=sw[:], op=ALU.subtract), reads=[bcs, bsw], writes=[bcp])
            ce, bce = T_()
            S.op("pool", lambda e, ce=ce, cs=cs: e.tensor_tensor(out=ce[:], in0=cs[:, :, 127:128].to_broadcast([128, 4, 128]), in1=cs[:], op=ALU.subtract), reads=[bcs], writes=[bce])
            epos, bepos = T_()
            eneg, beneg = T_()
            S.op("act", lambda e, epos=epos, cs=cs: e.activation(out=epos[:], in_=cs[:], func=AF.Exp, bias=C.cvals[:, 2:3], scale=-C0), reads=[bcs, C.b_const], writes=[bepos])
            S.op("act", lambda e, eneg=eneg, cs=cs: e.activation(out=eneg[:], in_=cs[:], func=AF.Exp, bias=C.cvals[:, 2:3], scale=C0), reads=[bcs, C.b_const], writes=[beneg])
            S.op("act", lambda e, cp_=cp_: e.activation(out=cp_[:], in_=cp_[:], func=AF.Exp, bias=C.cvals[:, 2:3], scale=-C0), reads=[bcp, C.b_const], writes=[bcp])
            S.op("act", lambda e, ce=ce: e.activation(out=ce[:], in_=ce[:], func=AF.Exp, bias=C.cvals[:, 2:3], scale=-C0), reads=[bce, C.b_const], writes=[bce])
            S.op("dve", lambda e, epos=epos, q=q: e.tensor_copy(out=wc[:, 4 * q:4 * q + 4], in_=epos[:, :, 127]), reads=[bepos], writes=[bo])
            yield
            kkn, bkkn = T_()
            S.op("pool", lambda e, kkn=kkn, K=K, q=q: e.tensor_tensor(out=kkn[:], in0=K[:], in1=bc4(PKK, q), op=ALU.mult), reads=[*bi, b_pp], writes=[bkkn])
            sq, bsq = tmpb[0], b_tmpb[0]
            S.op("dve", lambda e, kkn=kkn: e.tensor_tensor(out=sq[:], in0=kkn[:], in1=kkn[:], op=ALU.mult), reads=[bkkn], writes=[bsq])
            yield
            pn, bpn = bank()
            S.op("pe", lambda e: e.matmul(out=pn[:], lhsT=C.blk[:], rhs=sq[:].rearrange("p a b -> p (a b)"), start=True, stop=True), reads=[bsq, C.b_const], writes=[bpn])
            rn, brn = T_()
            S.op("act", lambda e, rn=rn: e.activation(out=rn[:].rearrange("p a b -> p (a b)"), in_=pn[:], func=AF.Sqrt, bias=C.cvals[:, 2:3], scale=1.0), reads=[bpn, C.b_const], writes=[brn])
            S.op("dve", lambda e, rn=rn: e.tensor_scalar(out=rn[:], in0=rn[:], scalar1=1e-12, scalar2=None, op0=ALU.max), reads=[brn], writes=[brn])
            S.op("dve", lambda e, rn=rn: e.reciprocal(out=rn[:], in_=rn[:]), reads=[brn], writes=[brn])
            S.op("pool", lambda e, kkn=kkn, rn=rn: e.tensor_tensor(out=kkn[:], in0=kkn[:], in1=rn[:], op=ALU.mult), reads=[bkkn, brn], writes=[bkkn])
            kk, bkk = kkn, bkkn
            yield
            bb, bbb = rn, brn
            S.op("dve", lambda e, bb=bb, kk=kk, sa_=sa_: e.tensor_tensor(out=bb[:], in0=kk[:], in1=sa_[:], op=ALU.mult), reads=[bkk, bsa, brn], writes=[bbb])
            t1, bt1 = T_()
            S.op("dve", lambda e, t1=t1, sa_=sa_, q=q: e.scalar_tensor_tensor(out=t1[:], in0=sa_[:], scalar=-1.0, in1=bc4(PKA, q), op0=ALU.add, op1=ALU.mult), reads=[bsa, b_pp], writes=[bt1])
            kp, bkp = t1, bt1
            S.op("dve", lambda e, t1=t1, K=K: e.scalar_tensor_tensor(out=t1[:], in0=t1[:], scalar=1.0, in1=K[:], op0=ALU.add, op1=ALU.mult), reads=[bt1, *bi], writes=[bt1])
            hs = slice(4 * q, 4 * q + 4)
            yield
            S.op("dve", lambda e, kk=kk, cp_=cp_, hs=hs: e.scalar_tensor_tensor(out=ARt[:, hs, 0, :], in0=kk[:], scalar=-1.0, in1=cp_[:], op0=ALU.mult, op1=ALU.mult), reads=[bkk, bcp], writes=[bo])
            S.op("pool", lambda e, R=R, epos=epos, hs=hs: e.tensor_tensor(out=ARt[:, hs, 1, :], in0=R[:], in1=epos[:], op=ALU.mult), reads=[*bi, bepos], writes=[bo])
            S.op("dve", lambda e, bb=bb, eneg=eneg, hs=hs: e.tensor_tensor(out=BKt[:, hs, 0, :], in0=bb[:], in1=eneg[:], op=ALU.mult), reads=[bbb, beneg], writes=[bo])
            S.op("pool", lambda e, kp=kp, eneg=eneg, hs=hs: e.tensor_tensor(out=BKt[:, hs, 1, :], in0=kp[:], in1=eneg[:], op=ALU.mult), reads=[bkp, beneg], writes=[bo])
            yield
            bh, bbh = tmpb[1], b_tmpb[1]
            kh, bkh = tmpb[2], b_tmpb[2]
            vb, bvb = tmpb[3], b_tmpb[3]
            S.op("dve", lambda e, bb=bb, ce=ce: e.tensor_tensor(out=bh[:], in0=bb[:], in1=ce[:], op=ALU.mult), reads=[bbb, bce], writes=[bbh])
            S.op("pool", lambda e, kp=kp, ce=ce: e.tensor_tensor(out=kh[:], in0=kp[:], in1=ce[:], op=ALU.mult), reads=[bkp, bce], writes=[bkh])
            S.op("act", lambda e, V=V: e.copy(out=vb[:], in_=V[:]), reads=[*bi], writes=[bvb])
            yield
            ptr, bptr = bank()
            ptb = ptr[:].bitcast(BF16)
            for j in range(4):
                S.op("pe", lambda e, j=j: e.transpose(out=ptb[:, j * 128:(j + 1) * 128], in_=bh[:, j, :], identity=C.ident[:]), reads=[bbh, C.b_const], writes=[bptr])
            for j in range(4):
                S.op("pe", lambda e, j=j: e.transpose(out=ptb[:, 512 + j * 128:512 + (j + 1) * 128], in_=kh[:, j, :], identity=C.ident[:]), reads=[bkh, C.b_const], writes=[bptr])
            S.op("act", lambda e, q=q: e.copy(out=BhT[:, q * 512:(q + 1) * 512], in_=ptb[:, 0:512]), reads=[bptr], writes=[bo, bptr])
            S.op("dve", lambda e, q=q: e.tensor_copy(out=KhT[:, q * 512:(q + 1) * 512], in_=ptb[:, 512:1024]), reads=[bptr], writes=[bo])
            ptr2, bptr2 = bank()
            ptb2 = ptr2[:].bitcast(BF16)
            for j in range(4):
                S.op("pe", lambda e, j=j: e.transpose(out=ptb2[:, j * 128:(j + 1) * 128], in_=vb[:, j, :], identity=C.ident[:]), reads=[bvb, C.b_const], writes=[bptr2])
            S.op("act", lambda e, q=q: e.copy(out=VT[:, q * 512:(q + 1) * 512], in_=ptb2[:, 0:512]), reads=[bptr2], writes=[bo])
            yield
            rk, brk = T_()
            S.op("pool", lambda e, rk=rk, R=R, kp=kp: e.tensor_tensor(out=rk[:], in0=R[:], in1=kp[:], op=ALU.mult), reads=[*bi, bkp], writes=[brk])
            rkb, brkb = tmpb[0], b_tmpb[0]
            S.op("dve", lambda e, rk=rk, q=q: e.tensor_tensor(out=rkb[:], in0=rk[:], in1=bc4(PRK, q), op=ALU.mult), reads=[brk, b_pp], writes=[brkb])
            yield
            pbn, bpbn = bank()
            S.op("pe", lambda e: e.matmul(out=pbn[:], lhsT=C.blk[:], rhs=rkb[:].rearrange("p a b -> p (a b)"), start=True, stop=True), reads=[brkb, C.b_const], writes=[bpbn])
            S.op("dve", lambda e, V=V, hs=hs: e.tensor_tensor(out=bonT[tt % 2][:, hs, :], in0=pbn[:].rearrange("p (a b) -> p a b", a=4), in1=V[:], op=ALU.mult), reads=[bpbn, *bi], writes=[bo])


        def G_pass(tt, p):
            gi = p
            for qd in range(4):
                bo = b_op[2 * p + qd // 2]
                for hh in range(4):
                    l16 = qd * 4 + hh
                    h = 16 * p + l16
                    hp, par = h // 2, h % 2
                    pr = slice(par * 64, par * 64 + 64)
                    pg, bpg = bank()
                    arhs = ARt[pr, hp, :, :].rearrange("p a b -> p (a b)")
                    S.op("pe", lambda e: e.matmul(out=pg[:, 0:256], lhsT=BKt[pr, hp, 0, :], rhs=arhs, start=True, stop=True), reads=[bo], writes=[bpg])
                    S.op("pe", lambda e: e.matmul(out=pg[:, 256:512], lhsT=BKt[pr, hp, 1, :], rhs=arhs, start=True, stop=True), reads=[bo], writes=[bpg])
                    S.op("dve", lambda e: e.tensor_tensor(out=Ab[0][:, l16, :], in0=pg[:, 0:128], in1=C.mU[:], op=ALU.mult), reads=[bpg, C.b_const], writes=[b_A[0][qd]])
                    S.op("dve", lambda e: e.tensor_tensor(out=M3[gi][:, l16, :, :], in0=pg[:, 128:512].rearrange("p (a b) -> p a b", a=3), in1=C.mask4[:, 1:4, :], op=ALU.mult), reads=[bpg, C.b_const], writes=[b_M3[gi][qd]])
                q4 = slice(qd * 4, qd * 4 + 4)
                if qd % 2 == 1:
                    qp = qd // 2
                    ATv = ATb[0][:].rearrange("p (h two) t -> p h two t", two=2)
                    for par in range(2):
                        pt_, bpt_ = bank()
                        pr = slice(par * 64, par * 64 + 64)
                        for k4 in range(4):
                            h = 16 * p + qp * 8 + 2 * k4 + par
                            hp = h // 2
                            S.op("pe", lambda e: e.matmul(out=pt_[:, k4 * 128:(k4 + 1) * 128], lhsT=ARt[pr, hp, 0, :], rhs=BKt[pr, hp, 0, :], start=True, stop=True), reads=[b_op[2 * p + qp]], writes=[bpt_])
                        S.op("dve", lambda e: e.tensor_tensor(out=ATv[:, qp * 4:(qp + 1) * 4, par, :], in0=pt_[:].rearrange("p (a b) -> p a b", a=4), in1=C.mL[:].unsqueeze(1).to_broadcast([128, 4, 128]), op=ALU.mult), reads=[bpt_, C.b_const], writes=[b_AT[0][qd - 1], b_AT[0][qd]])
                S.op("pool", lambda e: e.tensor_tensor(out=Tb[0][:, q4, :], in0=Ab[0][:, q4, :], in1=C.ident[:].unsqueeze(1).to_broadcast([128, 4, 128]), op=ALU.add), reads=[b_A[0][qd], C.b_const], writes=[b_T[0][qd]])
                yield

        def N_level(p, lv):
            gi = p
            i0, i1 = (lv - 1) % 2, lv % 2
            last = (lv == 6)
            pend = []
            for qd in range(4):
                q4 = slice(qd * 4, qd * 4 + 4)
                pAT, bpAT = bank()
                for hh in range(4):
                    l16 = qd * 4 + hh
                    S.op("pe", lambda e: e.matmul(out=pAT[:, hh * 128:(hh + 1) * 128], lhsT=Ab[i0][:, l16, :], rhs=ATb[i0][:, l16, :], start=True, stop=True), reads=[b_A[i0][qd], b_AT[i0][qd]], writes=[bpAT])
                S.op("act", lambda e: e.copy(out=ATb[i1][:, q4, :].rearrange("p a b -> p (a b)"), in_=pAT[:]), reads=[bpAT], writes=[b_AT[i1][qd]])
                if not last:
                    pA_, bpA = bank()
                    for hh in range(4):
                        l16 = qd * 4 + hh
                        S.op("pe", lambda e: e.matmul(out=pA_[:, hh * 128:(hh + 1) * 128], lhsT=ATb[i0][:, l16, :], rhs=Ab[i0][:, l16, :], start=True, stop=True), reads=[b_A[i0][qd], b_AT[i0][qd]], writes=[bpA])
                    S.op("act", lambda e: e.copy(out=Ab[i1][:, q4, :].rearrange("p a b -> p (a b)"), in_=pA_[:]), reads=[bpA], writes=[b_A[i1][qd]])
            for qd in range(4):
                q4 = slice(qd * 4, qd * 4 + 4)
                pT, bpT = bank()
                for hh in range(4):
                    l16 = qd * 4 + hh
                    S.op("pe", lambda e: e.matmul(out=pT[:, hh * 128:(hh + 1) * 128], lhsT=ATb[i1][:, l16, :], rhs=Tb[i0][:, l16, :], start=True, stop=True), reads=[b_AT[i1][qd], b_T[i0][qd]], writes=[bpT])
                if last:
                    S.op("dve", lambda e: e.tensor_tensor(out=Tm[gi][:, q4, :], in0=pT[:].rearrange("p (a b) -> p a b", a=4), in1=Tb[i0][:, q4, :], op=ALU.add), reads=[bpT, b_T[i0][qd]], writes=[b_M3[gi][qd]])
                else:
                    S.op("dve", lambda e: e.tensor_tensor(out=Tb[i1][:, q4, :], in0=pT[:].rearrange("p (a b) -> p a b", a=4), in1=Tb[i0][:, q4, :], op=ALU.add), reads=[bpT, b_T[i0][qd]], writes=[b_T[i1][qd]])

        def state_stages(tt, p):
            gi = p
            t0 = tt * 128
            stages = []

            def stage_X():
                for g2 in range(2):
                    grp = 2 * p + g2
                    bo = b_op[grp]
                    bm = [b_M3[gi][2 * g2], b_M3[gi][2 * g2 + 1]]
                    pX, bpX = ps[6 + g2], bps[6 + g2]
                    for l8 in range(8):
                        h = grp * 8 + l8
                        l16 = g2 * 8 + l8
                        hp, par = h // 2, h % 2
                        pr = slice(par * 64, par * 64 + 64)
                        S.op("pe", lambda e: e.matmul(out=pX[:, l8 * 64:(l8 + 1) * 64], lhsT=ARt[pr, hp, 0, :], rhs=STb[pr, hp, :], start=True, stop=False), reads=[bo, b_ST[grp]], writes=[bpX])
                        S.op("pe", lambda e: e.matmul(out=pX[:, l8 * 64:(l8 + 1) * 64], lhsT=M3[gi][:, l16, 1, :], rhs=VT[:, h * 64:(h + 1) * 64], start=False, stop=True), reads=bm + [bo], writes=[bpX])
                    S.op("act", lambda e: e.copy(out=XTb[g2][:], in_=pX[:]), reads=[bpX], writes=[b_X[g2]])

            def stage_U():
                for g2 in range(2):
                    bm = [b_M3[gi][2 * g2], b_M3[gi][2 * g2 + 1]]
                    pU, bpU = ps[6 + g2], bps[6 + g2]
                    for l8 in range(8):
                        l16 = g2 * 8 + l8
                        S.op("pe", lambda e: e.matmul(out=pU[:, l8 * 64:(l8 + 1) * 64], lhsT=Tm[gi][:, l16, :], rhs=XTb[g2][:, l8 * 64:(l8 + 1) * 64], start=True, stop=True), reads=bm + [b_X[g2]], writes=[bpU])
                    S.op("act", lambda e: e.copy(out=UTb[g2][:], in_=pU[:]), reads=[bpU], writes=[b_U[g2]])

            def stage_S():
                for g2 in range(2):
                    grp = 2 * p + g2
                    bo = b_op[grp]
                    pS, bpS = ps[6 + g2], bps[6 + g2]
                    for l8 in range(8):
                        h = grp * 8 + l8
                        hp = h // 2
                        o = pS[:, l8 * 64:(l8 + 1) * 64]
                        S.op("pe", lambda e: e.matmul(out=o, lhsT=BhT[:, hp * 128:(hp + 1) * 128], rhs=UTb[g2][:, l8 * 64:(l8 + 1) * 64], start=True, stop=False), reads=[bo, b_U[g2]], writes=[bpS])
                        S.op("pe", lambda e: e.matmul(out=o, lhsT=KhT[:, hp * 128:(hp + 1) * 128], rhs=VT[:, h * 64:(h + 1) * 64], start=False, stop=True), reads=[bo], writes=[bpS])
                    hps = slice(grp * 4, grp * 4 + 4)
                    S.op("pool", lambda e: e.tensor_tensor(out=ST[:, hps, :], in0=ST[:, hps, :], in1=wc[:, hps].unsqueeze(2).to_broadcast([128, 4, 64]), op=ALU.mult), reads=[b_ST[grp], bo], writes=[b_ST[grp]])
                    for par in range(2):
                        pr = slice(par * 64, par * 64 + 64)
                        srcp = pS[pr, :].rearrange("p (a two b) -> p a two b", a=4, two=2)[:, :, par, :]
                        S.op("dve", lambda e: e.tensor_tensor(out=ST[pr, hps, :], in0=ST[pr, hps, :], in1=srcp, op=ALU.add), reads=[b_ST[grp], bpS], writes=[b_ST[grp]])
                    S.op("pool", lambda e: e.tensor_copy(out=STb[:, hps, :], in_=ST[:, hps, :]), reads=[b_ST[grp]], writes=[b_ST[grp]])

            def stage_Y():
                for g2 in range(2):
                    grp = 2 * p + g2
                    bo = b_op[grp]
                    bm = [b_M3[gi][2 * g2], b_M3[gi][2 * g2 + 1]]
                    pY, bpY = ps[6 + g2], bps[6 + g2]
                    for l8 in range(8):
                        h = grp * 8 + l8
                        l16 = g2 * 8 + l8
                        hp, par = h // 2, h % 2
                        pr = slice(par * 64, par * 64 + 64)
                        o = pY[:, l8 * 64:(l8 + 1) * 64]
                        S.op("pe", lambda e: e.matmul(out=o, lhsT=ARt[pr, hp, 1, :], rhs=STb[pr, hp, :], start=True, stop=False), reads=[bo, b_ST[grp]], writes=[bpY])
                        S.op("pe", lambda e: e.matmul(out=o, lhsT=M3[gi][:, l16, 0, :], rhs=UTb[g2][:, l8 * 64:(l8 + 1) * 64], start=False, stop=False), reads=bm + [b_U[g2]], writes=[bpY])
                        S.op("pe", lambda e: e.matmul(out=o, lhsT=M3[gi][:, l16, 2, :], rhs=VT[:, h * 64:(h + 1) * 64], start=False, stop=True), reads=bm + [bo], writes=[bpY])
                    S.op("act", lambda e: e.copy(out=yf[:].rearrange("p a b -> p (a b)"), in_=pY[:]), reads=[bpY], writes=[b_y])
                    S.op("pool", lambda e: e.tensor_tensor(out=ysq[:], in0=yf[:], in1=yf[:], op=ALU.mult), reads=[b_y], writes=[b_ysq])
                    g8 = slice(grp * 8, grp * 8 + 8)
                    S.op("dve", lambda e: e.tensor_reduce(out=gst[:, 0, g8], in_=yf[:], axis=AX.X, op=ALU.add), reads=[b_y], writes=[b_gst])
                    S.op("dve", lambda e: e.tensor_reduce(out=gst[:, 1, g8], in_=ysq[:], axis=AX.X, op=ALU.add), reads=[b_ysq], writes=[b_gst])
                    S.op("dve", lambda e: e.tensor_scalar(out=gst[:, 2, g8], in0=gst[:, 0, g8], scalar1=1.0 / 64, scalar2=None, op0=ALU.mult), reads=[b_gst], writes=[b_gst])
                    S.op("dve", lambda e: e.tensor_tensor(out=gst[:, 3, g8], in0=gst[:, 2, g8], in1=gst[:, 2, g8], op=ALU.mult), reads=[b_gst], writes=[b_gst])
                    S.op("dve", lambda e: e.scalar_tensor_tensor(out=gst[:, 4, g8], in0=gst[:, 1, g8], scalar=1.0 / 64, in1=gst[:, 3, g8], op0=ALU.mult, op1=ALU.subtract), reads=[b_gst], writes=[b_gst])
                    S.op("act", lambda e: e.activation(out=gst[:, 5, g8], in_=gst[:, 4, g8], func=AF.Sqrt, bias=C.cvals[:, 1:2], scale=1.0), reads=[b_gst, C.b_const], writes=[b_gst])
                    S.op("dve", lambda e: e.reciprocal(out=gst[:, 5, g8], in_=gst[:, 5, g8]), reads=[b_gst], writes=[b_gst])
                    S.op("dve", lambda e: e.tensor_tensor(out=yf[:], in0=yf[:], in1=gst[:, 2, g8].unsqueeze(2).to_broadcast([128, 8, 64]), op=ALU.subtract), reads=[b_y, b_gst], writes=[b_y])
                    S.op("pool", lambda e: e.tensor_tensor(out=yn[:, grp * 512:(grp + 1) * 512].rearrange("p (a b) -> p a b", a=8), in0=yf[:], in1=gst[:, 5, g8].unsqueeze(2).to_broadcast([128, 8, 64]), op=ALU.mult), reads=[b_y, b_gst], writes=[b_yn[grp]])

            return [stage_X, stage_U, stage_Y, stage_S]

        def F2a(tt):
            S.dma("sp", Gt[:], D["G_s"][tt], writes=[b_G])
            z_, bz_ = zb[tt % 2], b_zb[tt % 2]
            for hf in range(2):
                pz, bpz = bank()
                pzb = pz[:].bitcast(BF16)
                for j in range(8):
                    hp = hf * 8 + j
                    S.op("pe", lambda e: e.transpose(out=pzb[:, j * 128:(j + 1) * 128], in_=yn[:, hp * 128:(hp + 1) * 128], identity=C.ident[:]), reads=[b_yn[hp // 4], C.b_const], writes=[bpz])
                h8 = slice(hf * 8, hf * 8 + 8)
                lw_bc = pp[:, PLW + hf * 8:PLW + hf * 8 + 8].unsqueeze(2).to_broadcast([128, 8, 128])
                lb_bc = pp[:, PLB + hf * 8:PLB + hf * 8 + 8].unsqueeze(2).to_broadcast([128, 8, 128])
                S.op("dve", lambda e: e.tensor_tensor(out=zf[:], in0=pzb.rearrange("p (a b) -> p a b", a=8), in1=lw_bc, op=ALU.mult), reads=[bpz, b_pp], writes=[b_zf])
                S.op("pool", lambda e: e.tensor_tensor(out=zf[:], in0=zf[:], in1=lb_bc, op=ALU.add), reads=[b_zf, b_pp], writes=[b_zf])
                S.op("pool", lambda e: e.tensor_tensor(out=zf[:], in0=zf[:], in1=bonT[tt % 2][:, h8, :], op=ALU.add), reads=[b_zf] + b_op, writes=[b_zf])
                S.op("pool", lambda e: e.tensor_tensor(out=z_[:, h8, :], in0=zf[:], in1=Gt[:, h8, :], op=ALU.mult), reads=[b_zf, b_G], writes=[bz_])
            S.dma("sp", D["Z_s"][tt], z_[:], reads=[bz_], writes=[Buf()])

        passes = [(tt, p) for tt in range(NT) for p in range(2)]
        pendF = []
        for g in (D_quarter(0, 0), D_quarter(0, 1)):
            for _ in g:
                pass
        for _ in G_pass(0, 0):
            pass
        for i, (tt, p) in enumerate(passes):
            gens = []
            if i + 1 < len(passes):
                tn, pn_ = passes[i + 1]
                gens = [D_quarter(tn, 2 * pn_), D_quarter(tn, 2 * pn_ + 1)]
            plan = {1: (0, 4), 2: (0, 3), 3: (0, 3), 4: (1, 4), 5: (1, 3), 6: (1, 3)}
            for lv in range(1, 7):
                N_level(p, lv)
                if lv == 1 and pendF:
                    pendF.pop(0)()
                if gens:
                    gi_, nch = plan[lv]
                    for _ in range(nch):
                        next(gens[gi_], None)
            for g in gens:
                for _ in g:
                    pass
            gnext = G_pass(*passes[i + 1]) if i + 1 < len(passes) else iter(())
            for st_fn in state_stages(tt, p):
                st_fn()
                next(gnext, None)
            for _ in gnext:
                pass
            if p == 1:
                pendF.append(lambda tt=tt: F2a(tt))
        while pendF:
            pendF.pop(0)()
        S.run()


def phase_x(C):
    nc, S = C.nc, C.S
    NT = C.NT
    ps, bps = C.psum, C.b_ps
    D = C.dram
    with ExitStack() as px:
        sb = lambda n, s, d: px.enter_context(nc.sbuf_tensor(n, s, d))
        wo = sb("wo", [128, 16, 1024], BF16)
        gpost = sb("gpost", [128, 1024], F32)
        b_w = Buf("wres2")
        with ExitStack() as pl:
            wst = [pl.enter_context(nc.sbuf_tensor(f"wst3_{i}", [128, 2048], F32)) for i in range(2)]
            b_wst = [Buf(), Buf()]
            for c in range(8):
                S.dma("sp", wst[c % 2][:].rearrange("p (j n) -> p j n", j=2),
                      D["w_out0"][c * 256:(c + 1) * 256, :].rearrange("(j p) n -> p j n", p=128), writes=[b_wst[c % 2]])
                S.op("pool", lambda e: e.tensor_copy(out=wo[:, 2 * c:2 * c + 2, :], in_=wst[c % 2][:].rearrange("p (j n) -> p j n", j=2)),
                     reads=[b_wst[c % 2]], writes=[b_w])
            S.dma("sp", gpost[:], D["gpost0"].partition_broadcast(128), writes=[b_w])
            S.run()
        zin = [sb(f"zin{i}", [128, 16, 128], BF16) for i in range(2)]
        b_zin = [Buf(), Buf()]
        hin = [sb(f"hinX{i}", [128, 1024], F32) for i in range(2)]
        b_hin = [Buf(), Buf()]
        hout = [sb(f"houtX{i}", [128, 1024], F32) for i in range(2)]
        b_hout = [Buf(), Buf()]
        junk = sb("junkF", [128, 512], BF16)
        b_junk = Buf()
        pst = sb("pst", [128, 2, 4], F32)
        b_pst = [Buf(), Buf()]
        if getattr(C, "uT1", None) is not None:
            ub1 = [sb(f"ubX{i}", [128, 1024], BF16) for i in range(2)]
            b_ub1 = [Buf(), Buf()]
            junk1 = sb("junkX1", [128, 1024], BF16)
            b_junk1 = Buf()
            st1 = sb("ssX1", [128, NT], F32)
            rs1 = sb("rsX1", [128, NT], F32)
            b_st1 = [Buf() for _ in range(NT)]
        for tt in range(NT):
            t0 = tt * 128
            i2 = tt % 2
            S.dma("sp", zin[i2][:], D["Z_s"][tt], writes=[b_zin[i2]])
            S.dma("sp", hin[i2][:], D["h0"][t0:t0 + 128, :], writes=[b_hin[i2]])
            pm = [ps[2 * i2], ps[2 * i2 + 1]]
            bpm = [bps[2 * i2], bps[2 * i2 + 1]]
            for nh in range(2):
                for hp in range(16):
                    S.op("pe", lambda e: e.matmul(out=pm[nh][:], lhsT=zin[i2][:, hp, :], rhs=wo[:, hp, nh * 512:(nh + 1) * 512], start=(hp == 0), stop=(hp == 15)), reads=[b_zin[i2], b_w], writes=[bpm[nh]])
            st_ = pst[:, i2, :]
            S.op("pool", lambda e: e.memset(st_, 0.0), writes=[b_pst[i2]])
            for nh in range(2):
                S.op("act", lambda e: e.activation(out=junk[:], in_=pm[nh][:], func=AF.Square, bias=C.cvals[:, 2:3], scale=1.0, accum_out=st_[:, nh:nh + 1]), reads=[bpm[nh], C.b_const, b_junk], writes=[b_pst[i2]])
            S.op("dve", lambda e: e.tensor_tensor(out=st_[:, 2:3], in0=st_[:, 0:1], in1=st_[:, 1:2], op=ALU.add), reads=[b_pst[i2]], writes=[b_pst[i2]])
            S.op("act", lambda e: e.activation(out=st_[:, 3:4], in_=st_[:, 2:3], func=AF.Sqrt, bias=C.cvals[:, 0:1], scale=1.0 / 1024), reads=[b_pst[i2], C.b_const], writes=[b_pst[i2]])
            S.op("dve", lambda e: e.reciprocal(out=st_[:, 3:4], in_=st_[:, 3:4]), reads=[b_pst[i2]], writes=[b_pst[i2]])
            ho, bho = hout[i2], b_hout[i2]
            for nh in range(2):
                cs_ = slice(nh * 512, (nh + 1) * 512)
                S.op("dve", lambda e: e.scalar_tensor_tensor(out=ho[:, cs_], in0=pm[nh][:], scalar=st_[:, 3:4], in1=gpost[:, cs_], op0=ALU.mult, op1=ALU.mult), reads=[bpm[nh], b_pst[i2], b_w], writes=[bho])
            S.op("dve", lambda e: e.tensor_tensor(out=ho[:], in0=ho[:], in1=hin[i2][:], op=ALU.add), reads=[bho, b_hin[i2]], writes=[bho])
            S.dma("sp", D["H1"][t0:t0 + 128, :], ho[:], reads=[bho], writes=[Buf()])
            if getattr(C, "uT1", None) is not None:
                uT1, b_uT1, gpre1 = C.uT1, C.b_uT1, C.gpre1
                S.op("pool", lambda e: e.memset(st1[:, tt:tt + 1], 0.0), writes=[b_st1[tt]])
                rmsnorm_stats(C, ho[:], 1024, st1[:, tt:tt + 1], rs1[:, tt:tt + 1], junk1[:], [bho, b_junk1], b_st1[tt])
                u1, bu1 = ub1[i2], b_ub1[i2]
                S.op("dve", lambda e: e.scalar_tensor_tensor(out=u1[:], in0=ho[:], scalar=rs1[:, tt:tt + 1], in1=gpre1[:], op0=ALU.mult, op1=ALU.mult), reads=[bho, b_st1[tt], C.b_gpre1], writes=[bu1])
                pb = ps[4 + i2][:].bitcast(BF16)
                for c in range(8):
                    S.op("pe", lambda e: e.transpose(out=pb[:, c * 128:(c + 1) * 128], in_=u1[:, c * 128:(c + 1) * 128], identity=C.ident[:]), reads=[bu1, C.b_const], writes=[bps[4 + i2]])
                S.op("act", lambda e: e.copy(out=uT1[:, :, t0:t0 + 128], in_=pb.rearrange("p (c k) -> p c k", c=8)), reads=[bps[4 + i2]], writes=[b_uT1])
        S.run()


SCALE = 192.0 ** -0.5
TWO_PI = 2.0 * math.pi


def layer1(C):
    nc, S = C.nc, C.S
    NT = C.NT
    L = NT * 128
    STS = supertiles(NT)
    ps, bps = C.psum, C.b_ps
    D = C.dram
    cv = C.cvals
    bc = C.b_const
    with ExitStack() as l1:
        sb1 = lambda n, s, d: l1.enter_context(nc.sbuf_tensor(n, s, d))
        pp = sb1("pp1_sb", [128, 8], F32)
        b_pp = Buf("pp1")
        S.dma("sp", pp[:], D["pp1"], writes=[b_pp])
        KR = sb1("KR", [128, L], BF16)
        b_KR = Buf("KR")
        mx = sb1("mx", [128, 8], F32)
        b_mx = Buf("mx")
        S.op("pool", lambda e: e.memset(mx[:], 0.0), writes=[b_mx])
        onesb = sb1("onesb", [128, 128], BF16)
        S.op("pool", lambda e: e.memset(onesb[:], 1.0), writes=[bc])
        fr = sb1("fr", [64, 2], F32)
        fri = sb1("fri", [64, 2], I32)
        S.op("pool", lambda e: e.iota(out=fri[:, 0:1], pattern=[[0, 1]], base=0, channel_multiplier=1), writes=[bc])
        S.op("dve", lambda e: e.tensor_single_scalar(out=fri[:, 1:2], in_=fri[:, 0:1], scalar=31, op=ALU.bitwise_and), reads=[bc], writes=[bc])
        S.op("pool", lambda e: e.tensor_copy(out=fr[:, 0:1], in_=fri[:, 1:2]), reads=[bc], writes=[bc])
        S.op("act", lambda e: e.activation(out=fr[:, 1:2], in_=fr[:, 0:1], func=AF.Exp, bias=cv[0:64, 2:3], scale=-math.log(10000.0) / 32.0), reads=[bc], writes=[bc])
        S.run()

        with ExitStack() as pa:
            sa = lambda n, s, d: pa.enter_context(nc.sbuf_tensor(n, s, d))
            fusedA = getattr(C, "uT1", None) is not None
            if fusedA:
                uT, b_uT = C.uT1, C.b_uT1
            else:
                uT = sa("uT1", [128, 8, L], BF16)
                b_uT = Buf("uT1")
            with ExitStack() as pA:
                if fusedA:
                    NTA = 0
                else:
                    NTA = NT
                sA = lambda n, s, d: pA.enter_context(nc.sbuf_tensor(n, s, d))
                gpre = sA("gpre1_sb", [128, 1024], F32)
                b_g = Buf()
                S.dma("sp", gpre[:], D["gpre1"].partition_broadcast(128), writes=[b_g])
                xin = [sA(f"xin1_{i}", [128, 1024], F32) for i in range(2)]
                b_xin = [Buf() for _ in range(2)]
                ub = [sA(f"ub1_{i}", [128, 1024], BF16) for i in range(2)]
                b_ub = [Buf() for _ in range(2)]
                junk = sA("junkA1", [128, 1024], BF16)
                b_junk = Buf()
                st_ss = sA("ssA1", [128, NT], F32)
                st_rs = sA("rsA1", [128, NT], F32)
                b_st = [Buf() for _ in range(NT)]
                for tt in range(NTA):
                    x, bx = xin[tt % 2], b_xin[tt % 2]
                    u, bu = ub[tt % 2], b_ub[tt % 2]
                    S.dma("sp", x[:], D["H1"][tt * 128:(tt + 1) * 128, :], writes=[bx])
                    S.op("pool", lambda e: e.memset(st_ss[:, tt:tt + 1], 0.0), writes=[b_st[tt]])
                    rmsnorm_stats(C, x[:], 1024, st_ss[:, tt:tt + 1], st_rs[:, tt:tt + 1], junk[:], [bx, b_junk], b_st[tt])
                    S.op("dve", lambda e: e.scalar_tensor_tensor(out=u[:], in0=x[:], scalar=st_rs[:, tt:tt + 1], in1=gpre[:], op0=ALU.mult, op1=ALU.mult),
                         reads=[bx, b_st[tt], b_g], writes=[bu])
                    pb = ps[tt % 2][:].bitcast(BF16)
                    for c in range(8):
                        S.op("pe", lambda e: e.transpose(out=pb[:, c * 128:(c + 1) * 128], in_=u[:, c * 128:(c + 1) * 128], identity=C.ident[:]),
                             reads=[bu, bc], writes=[bps[tt % 2]])
                    dst = uT[:, :, tt * 128:(tt + 1) * 128]
                    src = pb.rearrange("p (c k) -> p c k", c=8)
                    if tt % 2 == 0:
                        S.op("act", lambda e: e.copy(out=dst, in_=src), reads=[bps[tt % 2]], writes=[b_uT])
                    else:
                        S.op("dve", lambda e: e.tensor_copy(out=dst, in_=src), reads=[bps[tt % 2]], writes=[b_uT])
                S.run()

            with ExitStack() as pB:
                sB = lambda n, s, d: pB.enter_context(nc.sbuf_tensor(n, s, d))
                wst = [sB(f"wstB{i}", [128, 2048], F32) for i in range(2)]
                b_wst = [Buf(), Buf()]
                wcnt = [0]

                def load_w(dst_ap, src_ap, rows, cols, bdst):
                    i = wcnt[0] % 2
                    wcnt[0] += 1
                    S.dma("sp", wst[i][0:rows, 0:cols], src_ap, writes=[b_wst[i]])
                    if i == 0:
                        S.op("pool", lambda e: e.tensor_copy(out=dst_ap, in_=wst[i][0:rows, 0:cols]), reads=[b_wst[i]], writes=[bdst])
                    else:
                        S.op("act", lambda e: e.copy(out=dst_ap, in_=wst[i][0:rows, 0:cols]), reads=[b_wst[i]], writes=[bdst])

                stg = [sB(f"stgB{i}", [128, 512], BF16) for i in range(4)]
                b_stg = [Buf() for _ in range(4)]
                sqb_all = [sB(f"sqB{i}", [128, 512], BF16) for i in range(4)]
                b_sqb_all = [Buf() for _ in range(4)]
                sqb, b_sqb = sqb_all[0:2], b_sqb_all[0:2]
                nmc = [0]
                red = sB("redB", [128, 4], F32)
                b_red = Buf()
                cnt = [0]

                pend_norm = []

                def flush_norm():
                    while pend_norm:
                        a_, c_ = pend_norm.pop(0)
                        norm_max(a_, c_)

                def norm_max(sq_list, col):
                    nmc[0] += 1
                    pn, bpn = ps[6 + nmc[0] % 2], bps[6 + nmc[0] % 2]
                    n = sq_list[0][0].shape[-1]
                    for i, (ap, K, b) in enumerate(sq_list):
                        S.op("pe", lambda e: e.matmul(out=pn[:, :n], lhsT=onesb[0:K, :], rhs=ap, start=(i == 0), stop=(i == len(sq_list) - 1)), reads=[b, bc], writes=[bpn])
                    S.op("dve", lambda e: e.tensor_reduce(out=red[:, 0:1], in_=pn[:, :n], axis=AX.X, op=ALU.max), reads=[bpn], writes=[b_red])
                    S.op("dve", lambda e: e.tensor_tensor(out=mx[:, col:col + 1], in0=mx[:, col:col + 1], in1=red[:, 0:1], op=ALU.max), reads=[b_red, b_mx], writes=[b_mx])

                def feat_rmsnorm(src, bsrc, nch, n, gcol0, dst, bdst, nfeat):
                    pn, bpn = ps[6], bps[6]
                    for c in range(nch):
                        sq, bsq = sqb[c % 2], b_sqb[c % 2]
                        S.op("act", lambda e: e.activation(out=sq[:, :n], in_=src[:, c, :n], func=AF.Square, bias=cv[:, 2:3], scale=1.0), reads=[bsrc, bc], writes=[bsq])
                        S.op("pe", lambda e: e.matmul(out=pn[:, :n], lhsT=onesb[:], rhs=sq[:, :n], start=(c == 0), stop=(c == nch - 1)), reads=[bsq, bc], writes=[bpn])
                    rs, brs = rstd, b_rstd
                    S.op("act", lambda e: e.activation(out=rs[:, :n], in_=pn[:, :n], func=AF.Sqrt, bias=cv[:, 0:1], scale=1.0 / nfeat), reads=[bpn, bc], writes=[brs])
                    S.op("dve", lambda e: e.reciprocal(out=rs[:, :n], in_=rs[:, :n]), reads=[brs], writes=[brs])
                    for c in range(nch):
                        S.op("dve", lambda e: e.scalar_tensor_tensor(out=dst[:, c, :n], in0=src[:, c, :n], scalar=pp[:, gcol0 + c:gcol0 + c + 1], in1=rs[:, :n], op0=ALU.mult, op1=ALU.mult),
                             reads=[bsrc, brs, b_pp], writes=[bdst])

                rstd = sB("rstdB", [128, 512], F32)
                b_rstd = Buf()
                cosT = sB("cosT", [64, 512], F32)
                sinT = sB("sinT", [64, 512], F32)
                b_tab = Buf()
                angi = sB("angi", [64, 512], I32)
                angf = sB("angf", [64, 512], F32)
                angn = sB("angn", [64, 512], F32)
                angq = sB("angq", [64, 512], I32)

                def rope_tables(t0, n):
                    S.op("pool", lambda e: e.iota(out=angi[:, :n], pattern=[[1, n]], base=t0, channel_multiplier=0), writes=[b_tab])
                    S.op("pool", lambda e: e.tensor_copy(out=angf[:, :n], in_=angi[:, :n]), reads=[b_tab], writes=[b_tab])
                    S.op("dve", lambda e: e.tensor_scalar(out=angf[:, :n], in0=angf[:, :n], scalar1=fr[:, 1:2], scalar2=1.0 / TWO_PI, op0=ALU.mult, op1=ALU.mult), reads=[b_tab, bc], writes=[b_tab])
                    for which, dst in ((0, sinT), (1, cosT)):
                        if which == 1:
                            S.op("dve", lambda e: e.tensor_scalar(out=angf[:, :n], in0=angf[:, :n], scalar1=0.25, scalar2=None, op0=ALU.add), reads=[b_tab], writes=[b_tab])
                        S.op("dve", lambda e: e.tensor_copy(out=angq[:, :n], in_=angf[:, :n]), reads=[b_tab], writes=[b_tab])
                        S.op("dve", lambda e: e.tensor_copy(out=angn[:, :n], in_=angq[:, :n]), reads=[b_tab], writes=[b_tab])
                        S.op("dve", lambda e: e.tensor_tensor(out=angn[:, :n], in0=angf[:, :n], in1=angn[:, :n], op=ALU.subtract), reads=[b_tab], writes=[b_tab])
                        S.op("act", lambda e: e.activation(out=dst[:, :n], in_=angn[:, :n], func=AF.Sin, bias=cv[0:64, 2:3], scale=TWO_PI), reads=[b_tab, bc], writes=[b_tab])
                    S.op("dve", lambda e: e.tensor_scalar(out=sinT[0:32, :n], in0=sinT[0:32, :n], scalar1=-1.0, scalar2=None, op0=ALU.mult), reads=[b_tab], writes=[b_tab])

                rtmp = [sB(f"rtmp{i}", [64, 512], F32) for i in range(2)]
                b_rtmp = [Buf(), Buf()]

                def rope_rot(p_main, b_main, p_sw, b_sw, n, dst_ap, bdst):
                    S.op("dve", lambda e: e.tensor_tensor(out=rtmp[0][:, :n], in0=p_main, in1=cosT[:, :n], op=ALU.mult), reads=[b_main, b_tab], writes=[b_rtmp[0]])
                    S.op("dve", lambda e: e.tensor_tensor(out=rtmp[1][:, :n], in0=p_sw, in1=sinT[:, :n], op=ALU.mult), reads=[b_sw, b_tab], writes=[b_rtmp[1]])
                    S.op("dve", lambda e: e.tensor_tensor(out=dst_ap, in0=rtmp[0][:, :n], in1=rtmp[1][:, :n], op=ALU.add), reads=[b_rtmp[0], b_rtmp[1]], writes=[bdst])

                with ExitStack() as pB1:
                    s1 = lambda n, s, d: pB1.enter_context(nc.sbuf_tensor(n, s, d))
                    w1q = s1("w1q", [128, 8, 512], BF16)
                    wqn = s1("wqn", [128, 4, 2048], BF16)
                    wqr = s1("wqr", [128, 4, 1024], BF16)
                    wqrs = s1("wqrs", [128, 4, 1024], BF16)
                    b_wq = Buf()
                    for c in range(8):
                        load_w(w1q[:, c, :], D["w_in1"][c * 128:(c + 1) * 128, 0:512], 128, 512, b_wq)
                    for c in range(4):
                        load_w(wqn[:, c, :], D["wq_n"][c * 128:(c + 1) * 128, :], 128, 2048, b_wq)
                        load_w(wqr[:, c, :], D["wq_r"][c * 128:(c + 1) * 128, :], 128, 1024, b_wq)
                        load_w(wqrs[:, c, :], D["wq_rs"][c * 128:(c + 1) * 128, :], 128, 1024, b_wq)
                    cq = s1("cq", [128, 4, 512], F32)
                    b_cq = Buf()
                    qn = s1("qn", [128, 4, 512], BF16)
                    b_qn = Buf()
                    for si, (tt0, nt) in enumerate(STS):
                        t0, n = tt0 * 128, nt * 128
                        rope_tables(t0, n)
                        for c4 in range(4):
                            p_, bp_ = ps[c4 % 2], bps[c4 % 2]
                            for c in range(8):
                                S.op("pe", lambda e: e.matmul(out=p_[:, :n], lhsT=w1q[:, c, c4 * 128:(c4 + 1) * 128], rhs=uT[:, c, t0:t0 + n], start=(c == 0), stop=(c == 7)), reads=[b_wq, b_uT], writes=[bp_])
                            S.op("act", lambda e: e.copy(out=cq[:, c4, :n], in_=p_[:, :n]), reads=[bp_], writes=[b_cq])
                        feat_rmsnorm(cq, b_cq, 4, n, 0, qn, b_qn, 512)
                        for h in range(16):
                            k = cnt[0]
                            cnt[0] += 1
                            sqb, b_sqb = sqb_all[2 * (k % 2):2 * (k % 2) + 2], b_sqb_all[2 * (k % 2):2 * (k % 2) + 2]
                            p_, bp_ = ps[k % 2], bps[k % 2]
                            s_, bs_ = stg[k % 4], b_stg[k % 4]
                            for c in range(4):
                                S.op("pe", lambda e: e.matmul(out=p_[:, :n], lhsT=wqn[:, c, h * 128:(h + 1) * 128], rhs=qn[:, c, :n], start=(c == 0), stop=(c == 3)), reads=[b_wq, b_qn], writes=[bp_])
                            S.op("act", lambda e: e.copy(out=s_[:, :n], in_=p_[:, :n]), reads=[bp_], writes=[bs_])
                            S.dma("sp", D["QN_s"][h, :, t0:t0 + n], s_[:, :n], reads=[bs_], writes=[Buf()])
                            pr, bpr = ps[2 + (k % 2) * 2], bps[2 + (k % 2) * 2]
                            prs, bprs = ps[3 + (k % 2) * 2], bps[3 + (k % 2) * 2]
                            for c in range(4):
                                S.op("pe", lambda e: e.matmul(out=pr[0:64, :n], lhsT=wqr[:, c, h * 64:(h + 1) * 64], rhs=qn[:, c, :n], start=(c == 0), stop=(c == 3)), reads=[b_wq, b_qn], writes=[bpr])
                            for c in range(4):
                                S.op("pe", lambda e: e.matmul(out=prs[0:64, :n], lhsT=wqrs[:, c, h * 64:(h + 1) * 64], rhs=qn[:, c, :n], start=(c == 0), stop=(c == 3)), reads=[b_wq, b_qn], writes=[bprs])
                            flush_norm()
                            s2, bs2 = stg[(k + 2) % 4], b_stg[(k + 2) % 4]
                            rope_rot(pr[0:64, :n], bpr, prs[0:64, :n], bprs, n, s2[0:64, :n], bs2)
                            S.dma("sp", D["QR_s"][h, :, t0:t0 + n], s2[0:64, :n], reads=[bs2], writes=[Buf()])
                            S.op("act", lambda e: e.activation(out=sqb[0][:, :n], in_=s_[:, :n], func=AF.Square, bias=cv[0:128, 2:3], scale=1.0), reads=[bs_], writes=[b_sqb[0]])
                            S.op("act", lambda e: e.activation(out=sqb[1][0:64, :n], in_=s2[0:64, :n], func=AF.Square, bias=cv[0:64, 2:3], scale=1.0), reads=[bs2], writes=[b_sqb[1]])
                            pend_norm.append(([(sqb[0][:, :n], 128, b_sqb[0]), (sqb[1][0:64, :n], 64, b_sqb[1])], 0))
                    flush_norm()
                    S.run()

                with ExitStack() as pB2:
                    s2_ = lambda n, s, d: pB2.enter_context(nc.sbuf_tensor(n, s, d))
                    w1k = s2_("w1k", [128, 8, 384], BF16)
                    wkk = s2_("wkk", [128, 2, 2048], BF16)
                    wkv = s2_("wkv", [128, 2, 2048], BF16)
                    b_wk = Buf()
                    for c in range(8):
                        load_w(w1k[:, c, 0:320], D["w_in1"][c * 128:(c + 1) * 128, 512:832], 128, 320, b_wk)
                        load_w(w1k[:, c, 320:384], D["w_in1"][c * 128:(c + 1) * 128, 2880:2944], 128, 64, b_wk)
                    for c in range(2):
                        load_w(wkk[:, c, :], D["wkv_k"][c * 128:(c + 1) * 128, :], 128, 2048, b_wk)
                        load_w(wkv[:, c, :], D["wkv_v"][c * 128:(c + 1) * 128, :], 128, 2048, b_wk)
                    ckv = s2_("ckv", [128, 2, 512], F32)
                    b_ckv = Buf()
                    kvn = s2_("kvn", [128, 2, 512], BF16)
                    b_kvn = Buf()
                    vst = [s2_(f"vst{i}", [128, 16, 129], BF16) for i in range(2)]
                    b_vst = [Buf(), Buf()]
                    for i in range(2):
                        S.op("pool", lambda e: e.memset(vst[i][:, :, 128:129], 1.0), writes=[b_vst[i]])
                    vc = 0
                    for si, (tt0, nt) in enumerate(STS):
                        t0, n = tt0 * 128, nt * 128
                        rope_tables(t0, n)
                        for c2 in range(2):
                            p_, bp_ = ps[c2 % 2], bps[c2 % 2]
                            for c in range(8):
                                S.op("pe", lambda e: e.matmul(out=p_[:, :n], lhsT=w1k[:, c, c2 * 128:(c2 + 1) * 128], rhs=uT[:, c, t0:t0 + n], start=(c == 0), stop=(c == 7)), reads=[b_wk, b_uT], writes=[bp_])
                            S.op("act", lambda e: e.copy(out=ckv[:, c2, :n], in_=p_[:, :n]), reads=[bp_], writes=[b_ckv])
                        feat_rmsnorm(ckv, b_ckv, 2, n, 4, kvn, b_kvn, 256)
                        pr, bpr = ps[2], bps[2]
                        prs, bprs = ps[3], bps[3]
                        for c in range(8):
                            S.op("pe", lambda e: e.matmul(out=pr[0:64, :n], lhsT=w1k[:, c, 256:320], rhs=uT[:, c, t0:t0 + n], start=(c == 0), stop=(c == 7)), reads=[b_wk, b_uT], writes=[bpr])
                        for c in range(8):
                            S.op("pe", lambda e: e.matmul(out=prs[0:64, :n], lhsT=w1k[:, c, 320:384], rhs=uT[:, c, t0:t0 + n], start=(c == 0), stop=(c == 7)), reads=[b_wk, b_uT], writes=[bprs])
                        rope_rot(pr[0:64, :n], bpr, prs[0:64, :n], bprs, n, KR[0:64, t0:t0 + n], b_KR)
                        S.op("act", lambda e: e.activation(out=sqb[1][0:64, :n], in_=KR[0:64, t0:t0 + n], func=AF.Square, bias=cv[0:64, 2:3], scale=1.0), reads=[b_KR], writes=[b_sqb[1]])
                        norm_max([(sqb[1][0:64, :n], 64, b_sqb[1])], 2)
                        for h in range(16):
                            k = cnt[0]
                            cnt[0] += 1
                            sqb, b_sqb = sqb_all[2 * (k % 2):2 * (k % 2) + 2], b_sqb_all[2 * (k % 2):2 * (k % 2) + 2]
                            p_, bp_ = ps[k % 2], bps[k % 2]
                            s_, bs_ = stg[k % 4], b_stg[k % 4]
                            for c in range(2):
                                S.op("pe", lambda e: e.matmul(out=p_[:, :n], lhsT=wkk[:, c, h * 128:(h + 1) * 128], rhs=kvn[:, c, :n], start=(c == 0), stop=(c == 1)), reads=[b_wk, b_kvn], writes=[bp_])
                            flush_norm()
                            S.op("act", lambda e: e.copy(out=s_[:, :n], in_=p_[:, :n]), reads=[bp_], writes=[bs_])
                            S.dma("sp", D["KN_s"][h, :, t0:t0 + n], s_[:, :n], reads=[bs_], writes=[Buf()])
                            S.op("act", lambda e: e.activation(out=sqb[0][:, :n], in_=s_[:, :n], func=AF.Square, bias=cv[0:128, 2:3], scale=1.0), reads=[bs_], writes=[b_sqb[0]])
                            pend_norm.append(([(sqb[0][:, :n], 128, b_sqb[0])], 1))
                        flush_norm()
                        for j in range(nt):
                            vs, bvs = vst[vc % 2], b_vst[vc % 2]
                            vc += 1
                            for n4 in range(4):
                                p_, bp_ = ps[4 + n4 % 2], bps[4 + n4 % 2]
                                for c in range(2):
                                    S.op("pe", lambda e: e.matmul(out=p_[:], lhsT=kvn[:, c, j * 128:(j + 1) * 128], rhs=wkv[:, c, n4 * 512:(n4 + 1) * 512], start=(c == 0), stop=(c == 1)), reads=[b_wk, b_kvn], writes=[bp_])
                                if n4 % 2 == 0:
                                    S.op("act", lambda e: e.copy(out=vs[:, 4 * n4:4 * n4 + 4, 0:128], in_=p_[:].rearrange("p (a b) -> p a b", a=4)), reads=[bp_], writes=[bvs])
                                else:
                                    S.op("dve", lambda e: e.tensor_copy(out=vs[:, 4 * n4:4 * n4 + 4, 0:128], in_=p_[:].rearrange("p (a b) -> p a b", a=4)), reads=[bp_], writes=[bvs])
                            S.dma("sp", D["V1_s"][tt0 + j], vs[:], reads=[bvs], writes=[Buf()])
                    flush_norm()
                    S.run()

                with ExitStack() as pB3:
                    s3 = lambda n, s, d: pB3.enter_context(nc.sbuf_tensor(n, s, d))
                    w1g = s3("w1g", [128, 8, 2048], BF16)
                    b_wg = Buf()
                    for c in range(8):
                        load_w(w1g[:, c, :], D["w_in1"][c * 128:(c + 1) * 128, 832:2880], 128, 2048, b_wg)
                    for si, (tt0, nt) in enumerate(STS):
                        t0, n = tt0 * 128, nt * 128
                        for oc in range(16):
                            k = cnt[0]
                            cnt[0] += 1
                            p_, bp_ = ps[k % 4], bps[k % 4]
                            s_, bs_ = stg[k % 4], b_stg[k % 4]
                            for c in range(8):
                                S.op("pe", lambda e: e.matmul(out=p_[:, :n], lhsT=w1g[:, c, oc * 128:(oc + 1) * 128], rhs=uT[:, c, t0:t0 + n], start=(c == 0), stop=(c == 7)), reads=[b_wg, b_uT], writes=[bp_])
                            S.op("act", lambda e: e.activation(out=s_[:, :n], in_=p_[:, :n], func=AF.Silu, bias=cv[:, 2:3], scale=1.0), reads=[bp_, bc], writes=[bs_])
                            S.dma("sp", D["G_s"][tt0:tt0 + nt, :, oc, :].rearrange("t p k -> p t k"), s_[:, :n].rearrange("p (t k) -> p t k", t=nt), reads=[bs_], writes=[Buf()])
                    S.run()

        if getattr(C, "stop_after", None) == "l1b":
            raise StopBuild()
        S.dma("sp", KR[64:128, :], KR[0:64, :], reads=[b_KR], writes=[b_KR])
        S.op("dve", lambda e: e.tensor_tensor(out=mx[:, 3:4], in0=mx[:, 1:2], in1=mx[:, 2:3], op=ALU.add), reads=[b_mx], writes=[b_mx])
        S.op("dve", lambda e: e.tensor_tensor(out=mx[:, 3:4], in0=mx[:, 3:4], in1=mx[:, 0:1], op=ALU.mult), reads=[b_mx], writes=[b_mx])
        S.op("act", lambda e: e.activation(out=mx[:, 4:5], in_=mx[:, 3:4], func=AF.Sqrt, bias=cv[:, 2:3], scale=1.0), reads=[b_mx, bc], writes=[b_mx])
        S.op("dve", lambda e: e.tensor_scalar(out=mx[:, 4:5], in0=mx[:, 4:5], scalar1=-SCALE, scalar2=None, op0=ALU.mult), reads=[b_mx], writes=[b_mx])
        S.run()

        with ExitStack() as pC:
            sC = lambda n, s, d: pC.enter_context(nc.sbuf_tensor(n, s, d))
            QN = [sC(f"QN{i}", [128, L], BF16) for i in range(2)]
            QR = [sC(f"QR{i}", [128, L], BF16) for i in range(2)]
            KN = [sC(f"KN{i}", [128, L], BF16) for i in range(2)]
            VA = [sC(f"VA{i}", [128, NT, 129], BF16) for i in range(2)]
            b_hd = [[Buf() for _ in range(5)] for _ in range(2)]
            NPT = 6
            SB = (0, 1, 6, 7)
            PT = [sC(f"PT{i}", [128, 512], BF16) for i in range(NPT)]
            b_PT = [Buf() for _ in range(NPT)]
            ost = [sC(f"ost{i}", [128, 128], BF16) for i in range(4)]
            b_ost = [Buf() for _ in range(4)]
            rl = sC("rl", [128, 8], F32)
            b_rl = Buf()
            mUib = sC("mUib", [128, 128], BF16)
            S.op("pool", lambda e: e.tensor_copy(out=mUib[:], in_=C.mUi[:]), reads=[bc], writes=[bc])
            kblk = 0
            oc_ = 0
            for h in range(16):
                hb = h % 2
                bh = b_hd[hb]
                S.dma("sp", QN[hb][:], D["QN_s"][h], writes=[bh[0]])
                S.dma("sp", QR[hb][0:64, :], D["QR_s"][h], writes=[bh[1]])
                S.dma("sp", QR[hb][64:128, :], D["QR_s"][h], writes=[bh[4]])
                S.dma("sp", KN[hb][:], D["KN_s"][h], writes=[bh[2]])
                S.dma("sp", VA[hb][:], D["V1_s"][:, :, h, :].rearrange("t p k -> p t k"), writes=[bh[3]])
                for (tt0, nt) in STS:
                    q0 = tt0 * 128
                    pO = [ps[2 + j] for j in range(nt)]
                    bpO = [bps[2 + j] for j in range(nt)]
                    blocks = []
                    for kt in range(tt0 + nt):
                        j0 = max(0, kt - tt0)
                        blocks.append((kt, j0, (nt - j0) * 128, q0 + j0 * 128))
                    state = {}
                    sinfo = {}

                    def emit_S(i, part="all", rg=0):
                        nonlocal kblk
                        if part == "rope":
                            kt, j0, ncol, qc0 = blocks[i]
                            pS, bpS, pt_, bpt = sinfo[i]
                            rs_ = slice(rg * 64, rg * 64 + 64)
                            S.op("pe", lambda e: e.matmul(out=pS[:, :ncol], lhsT=KR[rs_, kt * 128:(kt + 1) * 128], rhs=QR[hb][rs_, qc0:qc0 + ncol], start=False, stop=True), reads=[bh[1], bh[4], b_KR], writes=[bpS])
                            S.op("act", lambda e: e.activation(out=pt_[:, :ncol], in_=pS[:, :ncol], func=AF.Exp, bias=mx[:, 4:5], scale=SCALE), reads=[bpS, b_mx], writes=[bpt])
                            if kt >= tt0:
                                S.op("dve", lambda e: e.tensor_tensor(out=pt_[:, 0:128], in0=pt_[:, 0:128], in1=mUib[:], op=ALU.mult), reads=[bpt, bc], writes=[bpt])
                            return
                        if part == "nope":
                            kt, j0, ncol, qc0 = blocks[i]
                            pS, bpS = ps[SB[kblk % 4]], bps[SB[kblk % 4]]
                            pt_, bpt = PT[kblk % NPT], b_PT[kblk % NPT]
                            kblk += 1
                            state[i] = (pt_, bpt)
                            sinfo[i] = (pS, bpS, pt_, bpt)
                            S.op("pe", lambda e: e.matmul(out=pS[:, :ncol], lhsT=KN[hb][:, kt * 128:(kt + 1) * 128], rhs=QN[hb][:, qc0:qc0 + ncol], start=True, stop=False), reads=[bh[0], bh[2]], writes=[bpS])
                            return
                        kt, j0, ncol, qc0 = blocks[i]
                        pS, bpS = ps[SB[kblk % 4]], bps[SB[kblk % 4]]
                        pt_, bpt = PT[kblk % NPT], b_PT[kblk % NPT]
                        kblk += 1
                        state[i] = (pt_, bpt)
                        S.op("pe", lambda e: e.matmul(out=pS[:, :ncol], lhsT=KN[hb][:, kt * 128:(kt + 1) * 128], rhs=QN[hb][:, qc0:qc0 + ncol], start=True, stop=False), reads=[bh[0], bh[2]], writes=[bpS])
                        S.op("pe", lambda e: e.matmul(out=pS[:, :ncol], lhsT=KR[:, kt * 128:(kt + 1) * 128], rhs=QR[hb][:, qc0:qc0 + ncol], start=False, stop=True), reads=[bh[1], b_KR], writes=[bpS])
                        S.op("act", lambda e: e.activation(out=pt_[:, :ncol], in_=pS[:, :ncol], func=AF.Exp, bias=mx[:, 4:5], scale=SCALE), reads=[bpS, b_mx], writes=[bpt])
                        if kt >= tt0:
                            S.op("dve", lambda e: e.tensor_tensor(out=pt_[:, 0:128], in0=pt_[:, 0:128], in1=mUib[:], op=ALU.mult), reads=[bpt, bc], writes=[bpt])

                    def emit_PV(i):
                        kt, j0, ncol, qc0 = blocks[i]
                        pt_, bpt = state.pop(i)
                        for j in range(j0, nt):
                            cj = (j - j0) * 128
                            S.op("pe", lambda e: e.matmul(out=pO[j][:, 0:129], lhsT=pt_[:, cj:cj + 128], rhs=VA[hb][:, kt, :], start=(kt == 0), stop=(kt == tt0 + j)), reads=[bpt, bh[3]], writes=[bpO[j]])

                    nb = len(blocks)

                    def emit_pair(i):
                        if i + 1 < nb:
                            emit_S(i, "nope")
                            emit_S(i + 1, "nope")
                            emit_S(i, "rope", 0)
                            emit_S(i + 1, "rope", 1)
                        elif i < nb:
                            emit_S(i, "nope")
                            emit_S(i, "rope", 0)
                    emit_pair(0)
                    for i in range(0, nb, 2):
                        emit_pair(i + 2)
                        emit_PV(i)
                        if i + 1 < nb:
                            emit_PV(i + 1)
                    for j in range(nt):
                        S.op("dve", lambda e: e.reciprocal(out=rl[:, j:j + 1], in_=pO[j][:, 128:129]), reads=[bpO[j]], writes=[b_rl])
                        o_, bo_ = ost[oc_ % 4], b_ost[oc_ % 4]
                        oc_ += 1
                        S.op("dve", lambda e: e.tensor_scalar(out=o_[:], in0=pO[j][:, 0:128], scalar1=rl[:, j:j + 1], scalar2=None, op0=ALU.mult), reads=[bpO[j], b_rl], writes=[bo_])
                        S.dma("sp", D["O_s"][tt0 + j, :, h * 128:(h + 1) * 128], o_[:], reads=[bo_], writes=[Buf()])
            S.run()

        if getattr(C, "stop_after", None) == "l1c":
            raise StopBuild()
        with ExitStack() as pD:
            sD = lambda n, s, d: pD.enter_context(nc.sbuf_tensor(n, s, d))
            wo = sD("wo1", [128, 16, 1024], BF16)
            gpost = sD("gpost1_sb", [128, 1024], F32)
            b_w = Buf()
            with ExitStack() as pl:
                wst = [pl.enter_context(nc.sbuf_tensor(f"wstD{i}", [128, 2048], F32)) for i in range(2)]
                b_wst = [Buf(), Buf()]
                for c in range(8):
                    S.dma("sp", wst[c % 2][:].rearrange("p (j n) -> p j n", j=2),
                          D["w_out1"][c * 256:(c + 1) * 256, :].rearrange("(j p) n -> p j n", p=128), writes=[b_wst[c % 2]])
                    S.op("pool", lambda e: e.tensor_copy(out=wo[:, 2 * c:2 * c + 2, :], in_=wst[c % 2][:].rearrange("p (j n) -> p j n", j=2)),
                         reads=[b_wst[c % 2]], writes=[b_w])
                S.dma("sp", gpost[:], D["gpost1"].partition_broadcast(128), writes=[b_w])
                S.run()
            ot = [sD(f"ot{i}", [128, 2048], BF16) for i in range(2)]
            b_ot = [Buf(), Buf()]
            Gt = [sD(f"Gt1_{i}", [128, 16, 128], BF16) for i in range(2)]
            b_G = [Buf(), Buf()]
            hin = [sD(f"hin1_{i}", [128, 1024], F32) for i in range(2)]
            b_hin = [Buf(), Buf()]
            hout = [sD(f"hout1_{i}", [128, 1024], F32) for i in range(2)]
            b_hout = [Buf(), Buf()]
            zbs = [sD(f"zb1_{i}", [128, 16, 128], BF16) for i in range(2)]
            b_zbs = [Buf(), Buf()]
            junk = sD("junkD1", [128, 512], BF16)
            b_junk = Buf()
            pst_all = sD("pst1", [128, 8], F32)
            b_pst_all = [Buf(), Buf()]
            NOUT = C.NOUT
            for tt in range(NT):
                t0 = tt * 128
                pst = pst_all[:, 4 * (tt % 2):4 * (tt % 2) + 4]
                b_pst = b_pst_all[tt % 2]
                o, bo = ot[tt % 2], b_ot[tt % 2]
                G, bG = Gt[tt % 2], b_G[tt % 2]
                S.dma("sp", o[:], D["O_s"][tt], writes=[bo])
                S.dma("sp", G[:], D["G_s"][tt], writes=[bG])
                S.dma("sp", hin[tt % 2][:], D["H1"][t0:t0 + 128, :], writes=[b_hin[tt % 2]])
                zb, b_zb = zbs[tt % 2], b_zbs[tt % 2]
                for hf in range(2):
                    pz, bpz = ps[hf + 6 * (tt % 2)], bps[hf + 6 * (tt % 2)]
                    pzb = pz[:].bitcast(BF16)
                    for j in range(8):
                        hp = hf * 8 + j
                        S.op("pe", lambda e: e.transpose(out=pzb[:, j * 128:(j + 1) * 128], in_=o[:, hp * 128:(hp + 1) * 128], identity=C.ident[:]), reads=[bo, bc], writes=[bpz])
                    h8 = slice(hf * 8, hf * 8 + 8)
                    S.op("dve", lambda e: e.tensor_tensor(out=zb[:, h8, :], in0=pzb.rearrange("p (a b) -> p a b", a=8), in1=G[:, h8, :], op=ALU.mult), reads=[bpz, bG], writes=[b_zb])
                pm = [ps[2 + 2 * (tt % 2)], ps[3 + 2 * (tt % 2)]]
                bpm = [bps[2 + 2 * (tt % 2)], bps[3 + 2 * (tt % 2)]]
                for nh in range(2):
                    for hp in range(16):
                        S.op("pe", lambda e: e.matmul(out=pm[nh][:], lhsT=zb[:, hp, :], rhs=wo[:, hp, nh * 512:(nh + 1) * 512], start=(hp == 0), stop=(hp == 15)), reads=[b_zb, b_w], writes=[bpm[nh]])
                S.op("pool", lambda e: e.memset(pst, 0.0), writes=[b_pst])
                for nh in range(2):
                    S.op("act", lambda e: e.activation(out=junk[:], in_=pm[nh][:], func=AF.Square, bias=cv[:, 2:3], scale=1.0, accum_out=pst[:, nh:nh + 1]), reads=[bpm[nh], bc, b_junk], writes=[b_pst])
                S.op("dve", lambda e: e.tensor_tensor(out=pst[:, 2:3], in0=pst[:, 0:1], in1=pst[:, 1:2], op=ALU.add), reads=[b_pst], writes=[b_pst])
                S.op("act", lambda e: e.activation(out=pst[:, 3:4], in_=pst[:, 2:3], func=AF.Sqrt, bias=cv[:, 0:1], scale=1.0 / 1024), reads=[b_pst, bc], writes=[b_pst])
                S.op("dve", lambda e: e.reciprocal(out=pst[:, 3:4], in_=pst[:, 3:4]), reads=[b_pst], writes=[b_pst])
                ho, bho = hout[tt % 2], b_hout[tt % 2]
                for nh in range(2):
                    cs_ = slice(nh * 512, (nh + 1) * 512)
                    S.op("dve", lambda e: e.scalar_tensor_tensor(out=ho[:, cs_], in0=pm[nh][:], scalar=pst[:, 3:4], in1=gpost[:, cs_], op0=ALU.mult, op1=ALU.mult), reads=[bpm[nh], b_pst, b_w], writes=[bho])
                S.op("dve", lambda e: e.tensor_tensor(out=ho[:], in0=ho[:], in1=hin[tt % 2][:], op=ALU.add), reads=[bho, b_hin[tt % 2]], writes=[bho])
                r0 = t0 - 16
                lo = max(0, -r0)
                hi = min(128, NOUT - r0)
                if hi > lo:
                    S.dma("sp", D["out"][r0 + lo:r0 + hi, :], ho[lo:hi, :], reads=[bho], writes=[Buf()])
            S.run()


N_CORES = 8
SEQ = 4096
N_META = 16
STOP_AFTER = None
NT_FULL = 33


class Ctx:
    pass


def host_l0(inp, b, NT):
    L = NT * 128
    hfull = np.concatenate([inp["meta_tokens"], inp["x"][b]], axis=0)
    h0 = np.zeros((L, 1024), np.float32)
    n = min(L, hfull.shape[0])
    h0[:n] = hfull[:n]
    mu = inp["rwkv_mu"][0]
    pp = np.zeros((128, 160), np.float32)
    pp[:, :48] = mu.reshape(6, 8, 128).transpose(2, 0, 1).reshape(128, 48)
    for j, nm in enumerate(["rwkv_w0", "rwkv_a0", "rwkv_k_k", "rwkv_k_a", "rwkv_r_k", "rwkv_ln_w", "rwkv_ln_b"]):
        pp[:, 48 + 16 * j: 48 + 16 * (j + 1)] = inp[nm][0].reshape(16, 128).T
    return {"h0": h0, "gpre0": inp["norm_pre"][0], "gpost0": inp["norm_post"][0], "pp0": pp,
            "w_in0": inp["rwkv_w_in"][0], "w2": inp["rwkv_w2"][0], "a2": inp["rwkv_a2"][0], "w_out0": inp["rwkv_w_out"][0]}


def host_l1(inp):
    w_in = inp["mla_w_in"][0]
    kr = w_in[:, 768:832]
    w_in1 = np.concatenate([w_in, kr[:, 32:64], kr[:, 0:32]], axis=1)
    pp1 = np.zeros((128, 8), np.float32)
    pp1[:, 0:4] = inp["mla_q_norm"][0].reshape(4, 128).T
    pp1[:, 4:6] = inp["mla_kv_norm"][0].reshape(2, 128).T
    wq = inp["mla_w_q_up"][0].reshape(512, 16, 192)
    wq_n = np.ascontiguousarray(wq[:, :, 0:128]).reshape(512, 2048)
    wq_r = np.ascontiguousarray(wq[:, :, 128:192]).reshape(512, 1024)
    wq_rs = np.ascontiguousarray(np.concatenate([wq[:, :, 160:192], wq[:, :, 128:160]], axis=2)).reshape(512, 1024)
    wkv = inp["mla_w_kv_up"][0].reshape(256, 16, 256)
    wkv_k = np.ascontiguousarray(wkv[:, :, 0:128]).reshape(256, 2048)
    wkv_v = np.ascontiguousarray(wkv[:, :, 128:256]).reshape(256, 2048)
    return {"w_in1": np.ascontiguousarray(w_in1), "pp1": pp1, "wq_n": wq_n, "wq_r": wq_r, "wq_rs": wq_rs,
            "wkv_k": wkv_k, "wkv_v": wkv_v, "w_out1": inp["mla_w_out"][0],
            "gpre1": inp["norm_pre"][1], "gpost1": inp["norm_post"][1]}


def build(NT, NOUT):
    nc = bass.Bass("TRN2", target_bir_lowering=False)
    L = NT * 128
    D = {}

    def din(name, shape, dt=F32):
        D[name] = nc.dram_tensor(name, shape, dt, kind="ExternalInput").ap()

    def dsc(name, shape, dt=F32):
        D[name] = nc.dram_tensor(name, shape, dt, kind="Internal").ap()
    din("h0", [L, 1024]); din("gpre0", [1024]); din("gpost0", [1024]); din("pp0", [128, 160])
    din("w_in0", [1024, 8320]); din("w2", [64, 2048]); din("a2", [64, 2048]); din("w_out0", [2048, 1024])
    din("w_in1", [1024, 2944]); din("pp1", [128, 8]); din("wq_n", [512, 2048]); din("wq_r", [512, 1024]); din("wq_rs", [512, 1024])
    din("wkv_k", [256, 2048]); din("wkv_v", [256, 2048]); din("w_out1", [2048, 1024]); din("gpre1", [1024]); din("gpost1", [1024])
    for nm in ("R_s", "K_s", "V_s"):
        dsc(nm, [NT, 128, 16, 128])
    dsc("G_s", [NT, 128, 16, 128], BF16); dsc("Z_s", [NT, 128, 16, 128], BF16)
    dsc("H1", [L, 1024])
    dsc("QN_s", [16, 128, L], BF16); dsc("QR_s", [16, 64, L], BF16); dsc("KN_s", [16, 128, L], BF16)
    dsc("V1_s", [NT, 128, 16, 129], BF16); dsc("O_s", [NT, 128, 2048], BF16)
    D["out"] = nc.dram_tensor("out", [NOUT, 1024], F32, kind="ExternalOutput").ap()
    with ExitStack() as st:
        C = Ctx()
        C.nc = nc; C.stack = st; C.S = Sched(nc, st); C.NT = NT; C.dram = D; C.dbg_barrier = 0; C.NOUT = NOUT
        C.stop_after = STOP_AFTER
        consts(C)
        C.S.run()
        try:
            layer0(C)
            with ExitStack() as mid:
                C.uT1 = mid.enter_context(nc.sbuf_tensor("uT1", [128, 8, L], BF16))
                C.b_uT1 = Buf("uT1")
                C.gpre1 = mid.enter_context(nc.sbuf_tensor("gpre1_x", [128, 1024], F32))
                C.b_gpre1 = Buf("gpre1")
                C.S.dma("sp", C.gpre1[:], D["gpre1"].partition_broadcast(128), writes=[C.b_gpre1])
                phase_x(C)
                layer1(C)
        except StopBuild:
            C.S.ops = {e: [] for e in C.S.ENGS}
        C.S.run()
    return nc


def kernel(**inputs):
    inp = {k: np.asarray(v) for k, v in inputs.items()}
    B = inp["x"].shape[0]
    nc = build(NT_FULL, SEQ)
    shared = host_l1(inp)
    in_maps = []
    for b in range(B):
        m = host_l0(inp, b, NT_FULL)
        m.update(shared)
        in_maps.append({k: np.ascontiguousarray(v, dtype=np.float32) for k, v in m.items()})
    res = run_bass_kernel_spmd(nc, in_maps, core_ids=list(range(B)))
    out = np.stack([np.asarray(res.results[b]["out"]) for b in range(B)], axis=0)
    return out.astype(np.float32)
```

```python
from contextlib import ExitStack
import math
import contextlib
import numpy as np
import concourse.bass as bass
import concourse.mybir as mybir
from concourse.bass_utils import run_bass_kernel_spmd

F32 = mybir.dt.float32
BF16 = mybir.dt.bfloat16
I32 = mybir.dt.int32
ALU = mybir.AluOpType
AF = mybir.ActivationFunctionType
AX = mybir.AxisListType


class Buf:
    __slots__ = ("name", "w", "r")

    def __init__(self, name=""):
        self.name = name
        self.w = None
        self.r = {}


class _Rec:
    def __init__(self):
        self.call = None

    def __getattr__(self, name):
        def f(*a, **k):
            self.call = (name, a, k)
            return self
        return f


class Sched:
    ENGS = ("pe", "act", "dve", "pool", "sp")
    EPOCH = 24000
    NSLOT = {"sp": 24, "pool": 12, "act": 8}
    STRICT = ("act", "dve", "pool")

    def __init__(self, nc, stack):
        self.nc = nc
        self.stack = stack
        self.ops = {e: [] for e in self.ENGS}
        self.cnt = {e: 0 for e in self.ENGS}
        self.sems = {}
        self.dcnt = {q: 0 for q in self.NSLOT}
        self.seen = {e: {} for e in self.ENGS}
        self.nwait = 0

    def sem(self, key):
        s = self.sems.get(key)
        if s is None:
            s = self.stack.enter_context(self.nc.semaphore("s_" + "_".join(str(k) for k in key)))
            self.sems[key] = s
        return s

    def _deps(self, eng, reads, writes, strict=False):
        toks = {}

        def add(tok, same_ok):
            if tok is None:
                return
            key, val = tok
            if same_ok and not strict and eng not in self.STRICT and key[0] == "e" and key[1] == eng:
                return
            if toks.get(key, -1) < val:
                toks[key] = val
        for b in reads:
            add(b.w, False)
        for b in writes:
            add(b.w, True)
            for k, v in b.r.items():
                add((k, v), True)
        out = []
        seen = self.seen[eng]
        for key, val in toks.items():
            if seen.get(key, -1) >= val:
                continue
            seen[key] = val
            out.append((key, val))
        return out

    def _mark(self, tok, reads, writes):
        key, val = tok
        for b in reads:
            if b.r.get(key, -1) < val:
                b.r[key] = val
        for b in writes:
            b.w = tok
            b.r = {}

    def op(self, eng, fn, reads=(), writes=()):
        waits = self._deps(eng, reads, writes)
        self.cnt[eng] += 1
        c = self.cnt[eng]
        key = ("e", eng, c // self.EPOCH)
        val = c % self.EPOCH
        if val == 0:
            self.cnt[eng] += 1
            c = self.cnt[eng]
            val = c % self.EPOCH
        self.sem(key)
        rec = _Rec()
        fn(rec)
        name, a, k = rec.call
        self.ops[eng].append((waits, (lambda e, name=name, a=a, k=k: getattr(e, name)(*a, **k)), key, 1))
        self._mark((key, val), reads, writes)
        self.nwait += len(waits)

    def dma(self, q, out, in_, reads=(), writes=(), **kw):
        n = self.NSLOT[q]
        idx = self.dcnt[q]
        self.dcnt[q] += 1
        slot = idx % n
        val = 16 * (idx // n + 1)
        key = ("d", q, slot)
        self.sem(key)
        waits = self._deps(q, reads, writes, strict=True)
        if val > 16:
            seen = self.seen[q]
            if seen.get(key, -1) < val - 16:
                seen[key] = val - 16
                waits.append((key, val - 16))
        self.ops[q].append((waits, (lambda e, out=out, in_=in_, kw=kw: e.dma_start(out=out, in_=in_, **kw)), key, 16))
        self._mark((key, val), reads, writes)
        self.nwait += len(waits)

    def _finals(self):
        waits = []
        for q, n in self.NSLOT.items():
            tot = self.dcnt[q]
            for slot in range(min(n, tot)):
                uses = (tot - slot + n - 1) // n
                waits.append((("d", q, slot), 16 * uses))
        for e in ("pe", "act", "dve", "pool"):
            c = self.cnt[e]
            if c > 0:
                waits.append((("e", e, c // self.EPOCH), c % self.EPOCH))
        return waits

    def run(self, barrier=True):
        nc = self.nc
        sched = self
        finals = self._finals() if barrier else []
        plan = {}
        for name in self.ENGS:
            seen = self.seen[name]
            w = []
            for k, v in finals:
                if seen.get(k, -1) < v:
                    seen[k] = v
                    w.append((k, v))
            plan[name] = w
        ops = self.ops
        self.ops = {e: [] for e in self.ENGS}

        def replay(name, eng):
            for waits, fn, key, inc in ops[name]:
                for k, v in waits:
                    eng.wait_ge(sched.sems[k], v)
                ins = fn(eng)
                ins.then_inc(sched.sems[key], inc)
            for k, v in plan[name]:
                eng.wait_ge(sched.sems[k], v)

        with nc.Block() as block:
            @block.tensor
            def _(e):
                replay("pe", e)

            @block.scalar
            def _(e):
                replay("act", e)

            @block.vector
            def _(e):
                replay("dve", e)

            @block.gpsimd
            def _(e):
                replay("pool", e)

            @block.sync
            def _(e):
                replay("sp", e)


class StopBuild(Exception):
    pass


C0 = float(np.exp(-0.5))
GN_EPS = 64e-5
RMS_EPS = 1e-6


def supertiles(NT):
    out = []
    t = 0
    while t < NT:
        n = min(4, NT - t)
        out.append((t, n))
        t += n
    return out


def consts(C):
    nc, S = C.nc, C.S
    sb = lambda n, s, d: C.stack.enter_context(nc.sbuf_tensor(n, s, d))
    C.b_const = Buf("const")
    bc = C.b_const
    C.identf = sb("identf", [128, 128], F32)
    C.ident = sb("ident", [128, 128], BF16)
    C.blkf = sb("blkf", [128, 128], F32)
    C.blk = sb("blk", [128, 128], BF16)
    C.mU = sb("mU", [128, 128], F32)
    C.mUi = sb("mUi", [128, 128], F32)
    C.mL = sb("mL", [128, 128], F32)
    C.mask4 = sb("mask4", [128, 4, 128], F32)
    C.cvals = sb("cvals", [128, 8], F32)
    C.ones_s = sb("ones_s", [128, 4, 128], F32)
    for t, pat_op, base in ((C.mU, ALU.is_gt, 0), (C.mUi, ALU.is_ge, 0), (C.mL, ALU.is_gt, 0)):
        pass
    S.op("pool", lambda e: e.memset(C.identf[:], 1.0), writes=[bc])
    S.op("pool", lambda e: e.affine_select(out=C.identf[:], in_=C.identf[:], pattern=[[-1, 128]],
                                           compare_op=ALU.is_equal, fill=0.0, base=0, channel_multiplier=1),
         reads=[bc], writes=[bc])
    S.op("pool", lambda e: e.tensor_copy(out=C.ident[:], in_=C.identf[:]), reads=[bc], writes=[bc])
    S.op("pool", lambda e: e.memset(C.mU[:], 1.0), writes=[bc])
    S.op("pool", lambda e: e.affine_select(out=C.mU[:], in_=C.mU[:], pattern=[[1, 128]],
                                           compare_op=ALU.is_gt, fill=0.0, base=0, channel_multiplier=-1),
         reads=[bc], writes=[bc])
    S.op("pool", lambda e: e.memset(C.mUi[:], 1.0), writes=[bc])
    S.op("pool", lambda e: e.affine_select(out=C.mUi[:], in_=C.mUi[:], pattern=[[1, 128]],
                                           compare_op=ALU.is_ge, fill=0.0, base=0, channel_multiplier=-1),
         reads=[bc], writes=[bc])
    S.op("pool", lambda e: e.memset(C.mL[:], 1.0), writes=[bc])
    S.op("pool", lambda e: e.affine_select(out=C.mL[:], in_=C.mL[:], pattern=[[-1, 128]],
                                           compare_op=ALU.is_gt, fill=0.0, base=0, channel_multiplier=1),
         reads=[bc], writes=[bc])
    for k in range(4):
        src = C.mU if k % 2 == 0 else C.mUi
        S.op("pool", lambda e, k=k, src=src: e.tensor_copy(out=C.mask4[:, k, :], in_=src[:]), reads=[bc], writes=[bc])
    S.op("pool", lambda e: e.memset(C.blkf[:], 0.0), writes=[bc])
    S.op("pool", lambda e: e.memset(C.blkf[0:64, 0:64], 1.0), writes=[bc])
    S.op("pool", lambda e: e.memset(C.blkf[64:128, 64:128], 1.0), writes=[bc])
    S.op("pool", lambda e: e.tensor_copy(out=C.blk[:], in_=C.blkf[:]), reads=[bc], writes=[bc])
    S.op("pool", lambda e: e.memset(C.cvals[:, 0:1], RMS_EPS), writes=[bc])
    S.op("pool", lambda e: e.memset(C.cvals[:, 1:2], GN_EPS), writes=[bc])
    S.op("pool", lambda e: e.memset(C.cvals[:, 2:3], 0.0), writes=[bc])
    S.op("pool", lambda e: e.memset(C.cvals[:, 3:4], 1.0), writes=[bc])
    S.op("pool", lambda e: e.memset(C.ones_s[:], 1.0), writes=[bc])
    S.op("pool", lambda e: e.memset(C.ones_s[:, :, 0:1], 0.0), writes=[bc])
    C.psum = [C.stack.enter_context(nc.psum_tensor(f"ps{i}", [128, 512], F32)) for i in range(8)]
    C.b_ps = [Buf(f"ps{i}") for i in range(8)]


def rmsnorm_stats(C, src_ap, ncols, ss_ap, rs_ap, junk_ap, reads, b_stat):
    S = C.S
    S.op("act", lambda e: e.activation(out=junk_ap, in_=src_ap, func=AF.Square, bias=C.cvals[:, 2:3], scale=1.0, accum_out=ss_ap),
         reads=reads + [C.b_const], writes=[b_stat])
    S.op("act", lambda e: e.activation(out=rs_ap, in_=ss_ap, func=AF.Sqrt, bias=C.cvals[:, 0:1], scale=1.0 / ncols),
         reads=[b_stat, C.b_const], writes=[b_stat])
    S.op("dve", lambda e: e.reciprocal(out=rs_ap, in_=rs_ap), reads=[b_stat], writes=[b_stat])


def layer0(C):
    nc, S = C.nc, C.S
    NT = C.NT
    L = NT * 128
    STS = supertiles(NT)
    ps, bps = C.psum, C.b_ps
    D = C.dram
    with ExitStack() as l0:
        sb0 = lambda n, s, d: l0.enter_context(nc.sbuf_tensor(n, s, d))
        pp = sb0("pp0_sb", [128, 160], F32)
        b_pp = Buf("pp")
        tw = sb0("tw", [64, L], BF16)
        al = sb0("al", [64, L], BF16)
        b_tw, b_al = Buf("tw"), Buf("al")
        S.dma("sp", pp[:], D["pp0"], writes=[b_pp])

        with ExitStack() as pa:
            sa = lambda n, s, d: pa.enter_context(nc.sbuf_tensor(n, s, d))
            uT = sa("uT", [128, 8, 1 + L], BF16)
            b_uT = Buf("uT")
            gpre = sa("gpre", [128, 1024], F32)
            b_g = Buf("gpre")
            S.dma("sp", gpre[:], D["gpre0"].partition_broadcast(128), writes=[b_g])
            S.op("pool", lambda e: e.memset(uT[:, :, 0:1], 0.0), writes=[b_uT])
            with ExitStack() as pA:
                sA = lambda n, s, d: pA.enter_context(nc.sbuf_tensor(n, s, d))
                xin = [sA(f"xin{i}", [128, 1024], F32) for i in range(2)]
                b_xin = [Buf() for _ in range(2)]
                ub = [sA(f"ub{i}", [128, 1024], BF16) for i in range(2)]
                b_ub = [Buf() for _ in range(2)]
                junk = sA("junkA", [128, 1024], BF16)
                b_junk = Buf()
                st_ss = sA("ssA", [128, NT], F32)
                st_rs = sA("rsA", [128, NT], F32)
                b_st = [Buf() for _ in range(NT)]
                for tt in range(NT):
                    x, bx = xin[tt % 2], b_xin[tt % 2]
                    u, bu = ub[tt % 2], b_ub[tt % 2]
                    S.dma("sp", x[:], D["h0"][tt * 128:(tt + 1) * 128, :], writes=[bx])
                    S.op("pool", lambda e, tt=tt: e.memset(st_ss[:, tt:tt + 1], 0.0), writes=[b_st[tt]])
                    rmsnorm_stats(C, x[:], 1024, st_ss[:, tt:tt + 1], st_rs[:, tt:tt + 1], junk[:], [bx, b_junk], b_st[tt])
                    S.op("dve", lambda e, x=x, u=u, tt=tt: e.scalar_tensor_tensor(
                        out=u[:], in0=x[:], scalar=st_rs[:, tt:tt + 1], in1=gpre[:], op0=ALU.mult, op1=ALU.mult),
                        reads=[bx, b_st[tt], b_g], writes=[bu])
                    pb = ps[tt % 2][:].bitcast(BF16)
                    for c in range(8):
                        S.op("pe", lambda e, c=c, u=u, pb=pb: e.transpose(out=pb[:, c * 128:(c + 1) * 128], in_=u[:, c * 128:(c + 1) * 128], identity=C.ident[:]),
                             reads=[bu, C.b_const], writes=[bps[tt % 2]])
                    eng = "act" if tt % 2 == 0 else "dve"
                    dst = uT[:, :, 1 + tt * 128: 1 + (tt + 1) * 128]
                    src = pb.rearrange("p (c k) -> p c k", c=8)
                    if eng == "act":
                        S.op("act", lambda e, dst=dst, src=src: e.copy(out=dst, in_=src), reads=[bps[tt % 2]], writes=[b_uT])
                    else:
                        S.op("dve", lambda e, dst=dst, src=src: e.tensor_copy(out=dst, in_=src), reads=[bps[tt % 2]], writes=[b_uT])
                S.run()

            with ExitStack() as pBC:
                sB = lambda n, s, d: pBC.enter_context(nc.sbuf_tensor(n, s, d))
                dxt = sB("dxt", [128, 8, 512], F32)
                tmpl = sB("tmpl", [128, 8, 512], F32)
                b_dx, b_tmpl = Buf(), Buf()
                xg = [sB(f"xg{i}", [128, 8, 512], BF16) for i in range(2)]
                b_xg = [Buf() for _ in range(2)]
                lerp_cnt = [0]

                def lerp(g, t0, n):
                    i = lerp_cnt[0] % 2
                    lerp_cnt[0] += 1
                    cur = uT[:, :, 1 + t0: 1 + t0 + n]
                    prev = uT[:, :, t0: t0 + n]
                    mu_bc = pp[:, g * 8:(g + 1) * 8].unsqueeze(2).to_broadcast([128, 8, n])
                    S.op("dve", lambda e: e.tensor_tensor(out=dxt[:, :, :n], in0=prev, in1=cur, op=ALU.subtract),
                         reads=[b_uT], writes=[b_dx])
                    S.op("pool", lambda e: e.tensor_tensor(out=tmpl[:, :, :n], in0=dxt[:, :, :n], in1=mu_bc, op=ALU.mult),
                         reads=[b_dx, b_pp], writes=[b_tmpl])
                    S.op("dve", lambda e: e.tensor_tensor(out=xg[i][:, :, :n], in0=tmpl[:, :, :n], in1=cur, op=ALU.add),
                         reads=[b_tmpl, b_uT], writes=[b_xg[i]])
                    return xg[i], b_xg[i]

                with ExitStack() as pB:
                    sBb = lambda n, s, d: pB.enter_context(nc.sbuf_tensor(n, s, d))
                    wlf = sBb("wlf", [128, 8, 128], F32)
                    wl = sBb("wl", [128, 8, 128], BF16)
                    b_wl = Buf()
                    S.dma("sp", wlf[:], D["w_in0"][:, 8192:8320].rearrange("(c p) n -> p c n", p=128), writes=[b_wl])
                    S.op("pool", lambda e: e.tensor_copy(out=wl[:], in_=wlf[:]), reads=[b_wl], writes=[b_wl])
                    for si, (tt0, nt) in enumerate(STS):
                        t0, n = tt0 * 128, nt * 128
                        xw, bxw = lerp(4, t0, n)
                        xa, bxa = lerp(5, t0, n)
                        pw, pa_ = ps[2 + (si % 2) * 2], ps[3 + (si % 2) * 2]
                        bpw, bpa = bps[2 + (si % 2) * 2], bps[3 + (si % 2) * 2]
                        for c in range(8):
                            S.op("pe", lambda e, c=c, xw=xw, pw=pw: e.matmul(out=pw[0:64, :n], lhsT=wl[:, c, 0:64], rhs=xw[:, c, :n], start=(c == 0), stop=(c == 7)),
                                 reads=[b_wl, bxw], writes=[bpw])
                        for c in range(8):
                            S.op("pe", lambda e, c=c, xa=xa, pa_=pa_: e.matmul(out=pa_[0:64, :n], lhsT=wl[:, c, 64:128], rhs=xa[:, c, :n], start=(c == 0), stop=(c == 7)),
                                 reads=[b_wl, bxa], writes=[bpa])
                        S.op("act", lambda e, pw=pw, t0=t0, n=n: e.activation(out=tw[:, t0:t0 + n], in_=pw[0:64, :n], func=AF.Tanh, bias=C.cvals[0:64, 2:3], scale=1.0),
                             reads=[bpw, C.b_const], writes=[b_tw])
                        S.op("dve", lambda e, pa_=pa_, t0=t0, n=n: e.tensor_copy(out=al[:, t0:t0 + n], in_=pa_[0:64, :n]),
                             reads=[bpa], writes=[b_al])
                    S.run()

                with ExitStack() as pC:
                    sC = lambda n, s, d: pC.enter_context(nc.sbuf_tensor(n, s, d))
                    wst = [sC(f"wst{i}", [128, 2048], F32) for i in range(2)]
                    b_wst = [Buf() for _ in range(2)]
                    wg = sC("wg", [128, 8, 2048], BF16)
                    b_wg = Buf()
                    stg = [sC(f"stg{i}", [128, 512], F32) for i in range(4)]
                    b_stg = [Buf() for _ in range(4)]
                    k = 0
                    for g in range(4):
                        for c in range(8):
                            S.dma("sp", wst[c % 2][:], D["w_in0"][c * 128:(c + 1) * 128, g * 2048:(g + 1) * 2048], writes=[b_wst[c % 2]])
                            if c % 2 == 0:
                                S.op("pool", lambda e, c=c: e.tensor_copy(out=wg[:, c, :], in_=wst[c % 2][:]), reads=[b_wst[c % 2]], writes=[b_wg])
                            else:
                                S.op("act", lambda e, c=c: e.copy(out=wg[:, c, :], in_=wst[c % 2][:]), reads=[b_wst[c % 2]], writes=[b_wg])
                        scr = D[("R_s", "K_s", "V_s", "G_s")[g]]
                        for si, (tt0, nt) in enumerate(STS):
                            t0, n = tt0 * 128, nt * 128
                            x_, bx_ = lerp(g, t0, n)
                            for oc in range(16):
                                pi = k % 4
                                p_, bp_ = ps[4 + pi], bps[4 + pi]
                                s_, bs_ = stg[pi], b_stg[pi]
                                k += 1
                                for c in range(8):
                                    S.op("pe", lambda e, c=c, oc=oc, x_=x_, p_=p_: e.matmul(out=p_[:, :n], lhsT=wg[:, c, oc * 128:(oc + 1) * 128], rhs=x_[:, c, :n], start=(c == 0), stop=(c == 7)),
                                         reads=[b_wg, bx_], writes=[bp_])
                                if g == 3:
                                    sv = s_[:].bitcast(BF16)[:, :n]
                                    S.op("act", lambda e, sv=sv, p_=p_: e.activation(out=sv, in_=p_[:, :n], func=AF.Silu, bias=C.cvals[:, 2:3], scale=1.0),
                                         reads=[bp_, C.b_const], writes=[bs_])
                                else:
                                    sv = s_[:, :n]
                                    if oc % 2 == 0:
                                        S.op("act", lambda e, sv=sv, p_=p_: e.copy(out=sv, in_=p_[:, :n]), reads=[bp_], writes=[bs_])
                                    else:
                                        S.op("dve", lambda e, sv=sv, p_=p_: e.tensor_copy(out=sv, in_=p_[:, :n]), reads=[bp_], writes=[bs_])
                                S.dma("sp", scr[tt0:tt0 + nt, :, oc, :].rearrange("t p k -> p t k"),
                                      sv.rearrange("p (t k) -> p t k", t=nt), reads=[bs_], writes=[Buf()])
                    S.run()
        C.l0_keep = (pp, b_pp, tw, al, b_tw, b_al)
        if getattr(C, "stop_after", None) == "l0c":
            raise StopBuild()
        layer0_wkv(C)
        if getattr(C, "stop_after", None) == "l0":
            raise StopBuild()


def layer0_wkv(C):
    nc, S = C.nc, C.S
    NT = C.NT
    L = NT * 128
    ps, bps = C.psum, C.b_ps
    D = C.dram
    pp, b_pp, tw, al, b_tw, b_al = C.l0_keep
    PW0, PA0, PKK, PKA, PRK, PLW, PLB = [48 + 16 * j for j in range(7)]
    with ExitStack() as pd:
        sb = lambda n, s, d: pd.enter_context(nc.sbuf_tensor(n, s, d))
        w2b = sb("w2b", [64, 2048], BF16)
        a2b = sb("a2b", [64, 2048], BF16)
        b_w = Buf("wres")
        with ExitStack() as pl:
            wst = [pl.enter_context(nc.sbuf_tensor(f"wst2_{i}", [128, 2048], F32)) for i in range(2)]
            b_wst = [Buf(), Buf()]
            S.dma("sp", wst[0][0:64, :], D["w2"], writes=[b_wst[0]])
            S.op("pool", lambda e: e.tensor_copy(out=w2b[:], in_=wst[0][0:64, :]), reads=[b_wst[0]], writes=[b_w])
            S.dma("sp", wst[1][0:64, :], D["a2"], writes=[b_wst[1]])
            S.op("pool", lambda e: e.tensor_copy(out=a2b[:], in_=wst[1][0:64, :]), reads=[b_wst[1]], writes=[b_w])
            S.run()
        ST = sb("ST", [128, 16, 64], F32)
        STb = sb("STb", [128, 16, 64], BF16)
        b_ST = [Buf(f"ST{g}") for g in range(4)]
        S.op("pool", lambda e: e.memset(ST[:], 0.0), writes=b_ST)
        S.op("pool", lambda e: e.memset(STb[:], 0.0), writes=b_ST)
        ARt = sb("ARt", [128, 16, 2, 128], BF16)
        BKt = sb("BKt", [128, 16, 2, 128], BF16)
        BhT = sb("BhT", [128, 2048], BF16)
        KhT = sb("KhT", [128, 2048], BF16)
        VT = sb("VT", [128, 2048], BF16)
        bonT = [sb(f"bonT{i}", [128, 16, 128], BF16) for i in range(2)]
        wc = sb("wc", [128, 16], F32)
        b_op = [Buf(f"op{q}") for q in range(4)]
        Rq = [sb(f"Rq{i}", [128, 4, 128], F32) for i in range(2)]
        Kq = [sb(f"Kq{i}", [128, 4, 128], F32) for i in range(2)]
        Vq = [sb(f"Vq{i}", [128, 4, 128], F32) for i in range(2)]
        b_in = [[Buf(), Buf(), Buf()], [Buf(), Buf(), Buf()]]
        NTMP = 12
        tmp = [sb(f"tmpD{i}", [128, 4, 128], F32) for i in range(NTMP)]
        b_tmp = [Buf() for _ in range(NTMP)]
        tmpb = [sb(f"tmpDb{i}", [128, 4, 128], BF16) for i in range(4)]
        b_tmpb = [Buf() for _ in range(4)]
        M3 = [sb(f"M3_{i}", [128, 16, 3, 128], BF16) for i in range(2)]
        Tm = [sb(f"Tm_{i}", [128, 16, 128], BF16) for i in range(2)]
        b_M3 = [[Buf() for _ in range(4)] for _ in range(2)]
        Ab = [sb(f"A_{i}", [128, 16, 128], BF16) for i in range(2)]
        ATb = [sb(f"AT_{i}", [128, 16, 128], BF16) for i in range(2)]
        Tb = [sb(f"T_{i}", [128, 16, 128], BF16) for i in range(2)]
        b_A = [[Buf() for _ in range(4)] for _ in range(2)]
        b_AT = [[Buf() for _ in range(4)] for _ in range(2)]
        b_T = [[Buf() for _ in range(4)] for _ in range(2)]
        XTb = [sb(f"XTb{i}", [128, 512], BF16) for i in range(2)]
        UTb = [sb(f"UTb{i}", [128, 512], BF16) for i in range(2)]
        b_X, b_U = [Buf(), Buf()], [Buf(), Buf()]
        yf = sb("yf", [128, 8, 64], F32)
        ysq = sb("ysq", [128, 8, 64], F32)
        b_y, b_ysq = Buf(), Buf()
        gst = sb("gst", [128, 6, 32], F32)
        b_gst = Buf()
        yn = sb("yn", [128, 2048], BF16)
        b_yn = [Buf() for _ in range(4)]
        Gt = sb("Gt0", [128, 16, 128], BF16)
        b_G = Buf()
        zf = sb("zf", [128, 8, 128], F32)
        b_zf = Buf()
        zb = [sb(f"zbw{i}", [128, 16, 128], BF16) for i in range(2)]
        b_zb = [Buf(), Buf()]

        def bc4(col0, q):
            return pp[:, col0 + 4 * q: col0 + 4 * q + 4].unsqueeze(2).to_broadcast([128, 4, 128])

        tctr = [0]

        def T_(n=1):
            i = tctr[0] % NTMP
            tctr[0] += 1
            return tmp[i], b_tmp[i]

        rot = [0]

        def bank():
            i = rot[0] % 6
            rot[0] += 1
            return ps[i], bps[i]

        def D_quarter(tt, q):
            t0 = tt * 128
            ib = (tt * 4 + q) % 2
            R, K, V, bi = Rq[ib], Kq[ib], Vq[ib], b_in[ib]
            S.dma("sp", R[:], D["R_s"][tt, :, 4 * q:4 * q + 4, :], writes=[bi[0]])
            S.dma("sp", K[:], D["K_s"][tt, :, 4 * q:4 * q + 4, :], writes=[bi[1]])
            S.dma("sp", V[:], D["V_s"][tt, :, 4 * q:4 * q + 4, :], writes=[bi[2]])
            bo = b_op[q]
            pw, bpw = bank()
            pa_, bpa = bank()
            for j in range(4):
                hp = 4 * q + j
                S.op("pe", lambda e, j=j, hp=hp: e.matmul(out=pw[:, j * 128:(j + 1) * 128], lhsT=w2b[:, hp * 128:(hp + 1) * 128], rhs=tw[:, t0:t0 + 128], start=True, stop=True),
                     reads=[b_w, b_tw], writes=[bpw])
            for j in range(4):
                hp = 4 * q + j
                S.op("pe", lambda e, j=j, hp=hp: e.matmul(out=pa_[:, j * 128:(j + 1) * 128], lhsT=a2b[:, hp * 128:(hp + 1) * 128], rhs=al[:, t0:t0 + 128], start=True, stop=True),
                     reads=[b_w, b_al], writes=[bpa])
            sw, bsw = T_()
            sa_, bsa = T_()
            for j in range(4):
                hp = 4 * q + j
                S.op("act", lambda e, j=j, hp=hp, sw=sw: e.activation(out=sw[:, j, :], in_=pw[:, j * 128:(j + 1) * 128], func=AF.Sigmoid, bias=pp[:, PW0 + hp:PW0 + hp + 1], scale=1.0),
                     reads=[bpw, b_pp], writes=[bsw])
                S.op("act", lambda e, j=j, hp=hp, sa_=sa_: e.activation(out=sa_[:, j, :], in_=pa_[:, j * 128:(j + 1) * 128], func=AF.Sigmoid, bias=pp[:, PA0 + hp:PA0 + hp + 1], scale=1.0),
                     reads=[bpa, b_pp], writes=[bsa])
            yield
            cs, bcs = T_()
            S.op("dve", lambda e, cs=cs, sw=sw: e.tensor_tensor_scan(out=cs[:].rearrange("p a b -> p (a b)"), data0=C.ones_s[:].rearrange("p a b -> p (a b)"), data1=sw[:].rearrange("p a b -> p (a b)"), initial=0.0, op0=ALU.mult, op1=ALU.add),
                 reads=[bsw, C.b_const], writes=[bcs])
            cp_, bcp = T_()
            S.op("pool", lambda e, cp_=cp_, cs=cs, sw=sw: e.tensor_tensor(out=cp_[:], in0=cs[:], in1=sw[:], op=ALU.subtract), reads=[bcs, bsw], writes=[bcp])
            ce, bce = T_()
            S.op("pool", lambda e, ce=ce, cs=cs: e.tensor_tensor(out=ce[:], in0=cs[:, :, 127:128].to_broadcast([128, 4, 128]), in1=cs[:], op=ALU.subtract), reads=[bcs], writes=[bce])
            epos, bepos = T_()
            eneg, beneg = T_()
            S.op("act", lambda e, epos=epos, cs=cs: e.activation(out=epos[:], in_=cs[:], func=AF.Exp, bias=C.cvals[:, 2:3], scale=-C0), reads=[bcs, C.b_const], writes=[bepos])
            S.op("act", lambda e, eneg=eneg, cs=cs: e.activation(out=eneg[:], in_=cs[:], func=AF.Exp, bias=C.cvals[:, 2:3], scale=C0), reads=[bcs, C.b_const], writes=[beneg])
            S.op("act", lambda e, cp_=cp_: e.activation(out=cp_[:], in_=cp_[:], func=AF.Exp, bias=C.cvals[:, 2:3], scale=-C0), reads=[bcp, C.b_const], writes=[bcp])
            S.op("act", lambda e, ce=ce: e.activation(out=ce[:], in_=ce[:], func=AF.Exp, bias=C.cvals[:, 2:3], scale=-C0), reads=[bce, C.b_const], writes=[bce])
            S.op("dve", lambda e, epos=epos, q=q: e.tensor_copy(out=wc[:, 4 * q:4 * q + 4], in_=epos[:, :, 127]), reads=[bepos], writes=[bo])
            yield
            kkn, bkkn = T_()
            S.op("pool", lambda e, kkn=kkn, K=K, q=q: e.tensor_tensor(out=kkn[:], in0=K[:], in1=bc4(PKK, q), op=ALU.mult), reads=[*bi, b_pp], writes=[bkkn])
            sq, bsq = tmpb[0], b_tmpb[0]
            S.op("dve", lambda e, kkn=kkn: e.tensor_tensor(out=sq[:], in0=kkn[:], in1=kkn[:], op=ALU.mult), reads=[bkkn], writes=[bsq])
            yield
            pn, bpn = bank()
            S.op("pe", lambda e: e.matmul(out=pn[:], lhsT=C.blk[:], rhs=sq[:].rearrange("p a b -> p (a b)"), start=True, stop=True), reads=[bsq, C.b_const], writes=[bpn])
            rn, brn = T_()
            S.op("act", lambda e, rn=rn: e.activation(out=rn[:].rearrange("p a b -> p (a b)"), in_=pn[:], func=AF.Sqrt, bias=C.cvals[:, 2:3], scale=1.0), reads=[bpn, C.b_const], writes=[brn])
            S.op("dve", lambda e, rn=rn: e.tensor_scalar(out=rn[:], in0=rn[:], scalar1=1e-12, scalar2=None, op0=ALU.max), reads=[brn], writes=[brn])
            S.op("dve", lambda e, rn=rn: e.reciprocal(out=rn[:], in_=rn[:]), reads=[brn], writes=[brn])
            S.op("pool", lambda e, kkn=kkn, rn=rn: e.tensor_tensor(out=kkn[:], in0=kkn[:], in1=rn[:], op=ALU.mult), reads=[bkkn, brn], writes=[bkkn])
            kk, bkk = kkn, bkkn
            yield
            bb, bbb = rn, brn
            S.op("dve", lambda e, bb=bb, kk=kk, sa_=sa_: e.tensor_tensor(out=bb[:], in0=kk[:], in1=sa_[:], op=ALU.mult), reads=[bkk, bsa, brn], writes=[bbb])
            t1, bt1 = T_()
            S.op("dve", lambda e, t1=t1, sa_=sa_, q=q: e.scalar_tensor_tensor(out=t1[:], in0=sa_[:], scalar=-1.0, in1=bc4(PKA, q), op0=ALU.add, op1=ALU.mult), reads=[bsa, b_pp], writes=[bt1])
            kp, bkp = t1, bt1
            S.op("dve", lambda e, t1=t1, K=K: e.scalar_tensor_tensor(out=t1[:], in0=t1[:], scalar=1.0, in1=K[:], op0=ALU.add, op1=ALU.mult), reads=[bt1, *bi], writes=[bt1])
            hs = slice(4 * q, 4 * q + 4)
            yield
            S.op("dve", lambda e, kk=kk, cp_=cp_, hs=hs: e.scalar_tensor_tensor(out=ARt[:, hs, 0, :], in0=kk[:], scalar=-1.0, in1=cp_[:], op0=ALU.mult, op1=ALU.mult), reads=[bkk, bcp], writes=[bo])
            S.op("pool", lambda e, R=R, epos=epos, hs=hs: e.tensor_tensor(out=ARt[:, hs, 1, :], in0=R[:], in1=epos[:], op=ALU.mult), reads=[*bi, bepos], writes=[bo])
            S.op("dve", lambda e, bb=bb, eneg=eneg, hs=hs: e.tensor_tensor(out=BKt[:, hs, 0, :], in0=bb[:], in1=eneg[:], op=ALU.mult), reads=[bbb, beneg], writes=[bo])
            S.op("pool", lambda e, kp=kp, eneg=eneg, hs=hs: e.tensor_tensor(out=BKt[:, hs, 1, :], in0=kp[:], in1=eneg[:], op=ALU.mult), reads=[bkp, beneg], writes=[bo])
            yield
            bh, bbh = tmpb[1], b_tmpb[1]
            kh, bkh = tmpb[2], b_tmpb[2]
            vb, bvb = tmpb[3], b_tmpb[3]
            S.op("dve", lambda e, bb=bb, ce=ce: e.tensor_tensor(out=bh[:], in0=bb[:], in1=ce[:], op=ALU.mult), reads=[bbb, bce], writes=[bbh])
            S.op("pool", lambda e, kp=kp, ce=ce: e.tensor_tensor(out=kh[:], in0=kp[:], in1=ce[:], op=ALU.mult), reads=[bkp, bce], writes=[bkh])
            S.op("act", lambda e, V=V: e.copy(out=vb[:], in_=V[:]), reads=[*bi], writes=[bvb])
            yield
            ptr, bptr = bank()
            ptb = ptr[:].bitcast(BF16)
            for j in range(4):
                S.op("pe", lambda e, j=j: e.transpose(out=ptb[:, j * 128:(j + 1) * 128], in_=bh[:, j, :], identity=C.ident[:]), reads=[bbh, C.b_const], writes=[bptr])
            for j in range(4):
                S.op("pe", lambda e, j=j: e.transpose(out=ptb[:, 512 + j * 128:512 + (j + 1) * 128], in_=kh[:, j, :], identity=C.ident[:]), reads=[bkh, C.b_const], writes=[bptr])
            S.op("act", lambda e, q=q: e.copy(out=BhT[:, q * 512:(q + 1) * 512], in_=ptb[:, 0:512]), reads=[bptr], writes=[bo, bptr])
            S.op("dve", lambda e, q=q: e.tensor_copy(out=KhT[:, q * 512:(q + 1) * 512], in_=ptb[:, 512:1024]), reads=[bptr], writes=[bo])
            ptr2, bptr2 = bank()
            ptb2 = ptr2[:].bitcast(BF16)
            for j in range(4):
                S.op("pe", lambda e, j=j: e.transpose(out=ptb2[:, j * 128:(j + 1) * 128], in_=vb[:, j, :], identity=C.ident[:]), reads=[bvb, C.b_const], writes=[bptr2])
            S.op("act", lambda e, q=q: e.copy(out=VT[:, q * 512:(q + 1) * 512], in_=ptb2[:, 0:512]), reads=[bptr2], writes=[bo])
            yield
            rk, brk = T_()
            S.op("pool", lambda e, rk=rk, R=R, kp=kp: e.tensor_tensor(out=rk[:], in0=R[:], in1=kp[:], op=ALU.mult), reads=[*bi, bkp], writes=[brk])
            rkb, brkb = tmpb[0], b_tmpb[0]
            S.op("dve", lambda e, rk=rk, q=q: e.tensor_tensor(out=rkb[:], in0=rk[:], in1=bc4(PRK, q), op=ALU.mult), reads=[brk, b_pp], writes=[brkb])
            yield
            pbn, bpbn = bank()
            S.op("pe", lambda e: e.matmul(out=pbn[:], lhsT=C.blk[:], rhs=rkb[:].rearrange("p a b -> p (a b)"), start=True, stop=True), reads=[brkb, C.b_const], writes=[bpbn])
            S.op("dve", lambda e, V=V, hs=hs: e.tensor_tensor(out=bonT[tt % 2][:, hs, :], in0=pbn[:].rearrange("p (a b) -> p a b", a=4), in1=V[:], op=ALU.mult), reads=[bpbn, *bi], writes=[bo])


        def G_pass(tt, p):
            gi = p
            for qd in range(4):
                bo = b_op[2 * p + qd // 2]
                for hh in range(4):
                    l16 = qd * 4 + hh
                    h = 16 * p + l16
                    hp, par = h // 2, h % 2
                    pr = slice(par * 64, par * 64 + 64)
                    pg, bpg = bank()
                    arhs = ARt[pr, hp, :, :].rearrange("p a b -> p (a b)")
                    S.op("pe", lambda e: e.matmul(out=pg[:, 0:256], lhsT=BKt[pr, hp, 0, :], rhs=arhs, start=True, stop=True), reads=[bo], writes=[bpg])
                    S.op("pe", lambda e: e.matmul(out=pg[:, 256:512], lhsT=BKt[pr, hp, 1, :], rhs=arhs, start=True, stop=True), reads=[bo], writes=[bpg])
                    S.op("dve", lambda e: e.tensor_tensor(out=Ab[0][:, l16, :], in0=pg[:, 0:128], in1=C.mU[:], op=ALU.mult), reads=[bpg, C.b_const], writes=[b_A[0][qd]])
                    S.op("dve", lambda e: e.tensor_tensor(out=M3[gi][:, l16, :, :], in0=pg[:, 128:512].rearrange("p (a b) -> p a b", a=3), in1=C.mask4[:, 1:4, :], op=ALU.mult), reads=[bpg, C.b_const], writes=[b_M3[gi][qd]])
                q4 = slice(qd * 4, qd * 4 + 4)
                if qd % 2 == 1:
                    qp = qd // 2
                    ATv = ATb[0][:].rearrange("p (h two) t -> p h two t", two=2)
                    for par in range(2):
                        pt_, bpt_ = bank()
                        pr = slice(par * 64, par * 64 + 64)
                        for k4 in range(4):
                            h = 16 * p + qp * 8 + 2 * k4 + par
                            hp = h // 2
                            S.op("pe", lambda e: e.matmul(out=pt_[:, k4 * 128:(k4 + 1) * 128], lhsT=ARt[pr, hp, 0, :], rhs=BKt[pr, hp, 0, :], start=True, stop=True), reads=[b_op[2 * p + qp]], writes=[bpt_])
                        S.op("dve", lambda e: e.tensor_tensor(out=ATv[:, qp * 4:(qp + 1) * 4, par, :], in0=pt_[:].rearrange("p (a b) -> p a b", a=4), in1=C.mL[:].unsqueeze(1).to_broadcast([128, 4, 128]), op=ALU.mult), reads=[bpt_, C.b_const], writes=[b_AT[0][qd - 1], b_AT[0][qd]])
                S.op("pool", lambda e: e.tensor_tensor(out=Tb[0][:, q4, :], in0=Ab[0][:, q4, :], in1=C.ident[:].unsqueeze(1).to_broadcast([128, 4, 128]), op=ALU.add), reads=[b_A[0][qd], C.b_const], writes=[b_T[0][qd]])
                yield

        def N_level(p, lv):
            gi = p
            i0, i1 = (lv - 1) % 2, lv % 2
            last = (lv == 6)
            pend = []
            for qd in range(4):
                q4 = slice(qd * 4, qd * 4 + 4)
                pAT, bpAT = bank()
                for hh in range(4):
                    l16 = qd * 4 + hh
                    S.op("pe", lambda e: e.matmul(out=pAT[:, hh * 128:(hh + 1) * 128], lhsT=Ab[i0][:, l16, :], rhs=ATb[i0][:, l16, :], start=True, stop=True), reads=[b_A[i0][qd], b_AT[i0][qd]], writes=[bpAT])
                S.op("act", lambda e: e.copy(out=ATb[i1][:, q4, :].rearrange("p a b -> p (a b)"), in_=pAT[:]), reads=[bpAT], writes=[b_AT[i1][qd]])
                if not last:
                    pA_, bpA = bank()
                    for hh in range(4):
                        l16 = qd * 4 + hh
                        S.op("pe", lambda e: e.matmul(out=pA_[:, hh * 128:(hh + 1) * 128], lhsT=ATb[i0][:, l16, :], rhs=Ab[i0][:, l16, :], start=True, stop=True), reads=[b_A[i0][qd], b_AT[i0][qd]], writes=[bpA])
                    S.op("act", lambda e: e.copy(out=Ab[i1][:, q4, :].rearrange("p a b -> p (a b)"), in_=pA_[:]), reads=[bpA], writes=[b_A[i1][qd]])
            for qd in range(4):
                q4 = slice(qd * 4, qd * 4 + 4)
                pT, bpT = bank()
                for hh in range(4):
                    l16 = qd * 4 + hh
                    S.op("pe", lambda e: e.matmul(out=pT[:, hh * 128:(hh + 1) * 128], lhsT=ATb[i1][:, l16, :], rhs=Tb[i0][:, l16, :], start=True, stop=True), reads=[b_AT[i1][qd], b_T[i0][qd]], writes=[bpT])
                if last:
                    S.op("dve", lambda e: e.tensor_tensor(out=Tm[gi][:, q4, :], in0=pT[:].rearrange("p (a b) -> p a b", a=4), in1=Tb[i0][:, q4, :], op=ALU.add), reads=[bpT, b_T[i0][qd]], writes=[b_M3[gi][qd]])
                else:
                    S.op("dve", lambda e: e.tensor_tensor(out=Tb[i1][:, q4, :], in0=pT[:].rearrange("p (a b) -> p a b", a=4), in1=Tb[i0][:, q4, :], op=ALU.add), reads=[bpT, b_T[i0][qd]], writes=[b_T[i1][qd]])

        def state_stages(tt, p):
            gi = p
            t0 = tt * 128
            stages = []

            def stage_X():
                for g2 in range(2):
                    grp = 2 * p + g2
                    bo = b_op[grp]
                    bm = [b_M3[gi][2 * g2], b_M3[gi][2 * g2 + 1]]
                    pX, bpX = ps[6 + g2], bps[6 + g2]
                    for l8 in range(8):
                        h = grp * 8 + l8
                        l16 = g2 * 8 + l8
                        hp, par = h // 2, h % 2
                        pr = slice(par * 64, par * 64 + 64)
                        S.op("pe", lambda e: e.matmul(out=pX[:, l8 * 64:(l8 + 1) * 64], lhsT=ARt[pr, hp, 0, :], rhs=STb[pr, hp, :], start=True, stop=False), reads=[bo, b_ST[grp]], writes=[bpX])
                        S.op("pe", lambda e: e.matmul(out=pX[:, l8 * 64:(l8 + 1) * 64], lhsT=M3[gi][:, l16, 1, :], rhs=VT[:, h * 64:(h + 1) * 64], start=False, stop=True), reads=bm + [bo], writes=[bpX])
                    S.op("act", lambda e: e.copy(out=XTb[g2][:], in_=pX[:]), reads=[bpX], writes=[b_X[g2]])

            def stage_U():
                for g2 in range(2):
                    bm = [b_M3[gi][2 * g2], b_M3[gi][2 * g2 + 1]]
                    pU, bpU = ps[6 + g2], bps[6 + g2]
                    for l8 in range(8):
                        l16 = g2 * 8 + l8
                        S.op("pe", lambda e: e.matmul(out=pU[:, l8 * 64:(l8 + 1) * 64], lhsT=Tm[gi][:, l16, :], rhs=XTb[g2][:, l8 * 64:(l8 + 1) * 64], start=True, stop=True), reads=bm + [b_X[g2]], writes=[bpU])
                    S.op("act", lambda e: e.copy(out=UTb[g2][:], in_=pU[:]), reads=[bpU], writes=[b_U[g2]])

            def stage_S():
                for g2 in range(2):
                    grp = 2 * p + g2
                    bo = b_op[grp]
                    pS, bpS = ps[6 + g2], bps[6 + g2]
                    for l8 in range(8):
                        h = grp * 8 + l8
                        hp = h // 2
                        o = pS[:, l8 * 64:(l8 + 1) * 64]
                        S.op("pe", lambda e: e.matmul(out=o, lhsT=BhT[:, hp * 128:(hp + 1) * 128], rhs=UTb[g2][:, l8 * 64:(l8 + 1) * 64], start=True, stop=False), reads=[bo, b_U[g2]], writes=[bpS])
                        S.op("pe", lambda e: e.matmul(out=o, lhsT=KhT[:, hp * 128:(hp + 1) * 128], rhs=VT[:, h * 64:(h + 1) * 64], start=False, stop=True), reads=[bo], writes=[bpS])
                    hps = slice(grp * 4, grp * 4 + 4)
                    S.op("pool", lambda e: e.tensor_tensor(out=ST[:, hps, :], in0=ST[:, hps, :], in1=wc[:, hps].unsqueeze(2).to_broadcast([128, 4, 64]), op=ALU.mult), reads=[b_ST[grp], bo], writes=[b_ST[grp]])
                    for par in range(2):
                        pr = slice(par * 64, par * 64 + 64)
                        srcp = pS[pr, :].rearrange("p (a two b) -> p a two b", a=4, two=2)[:, :, par, :]
                        S.op("dve", lambda e: e.tensor_tensor(out=ST[pr, hps, :], in0=ST[pr, hps, :], in1=srcp, op=ALU.add), reads=[b_ST[grp], bpS], writes=[b_ST[grp]])
                    S.op("pool", lambda e: e.tensor_copy(out=STb[:, hps, :], in_=ST[:, hps, :]), reads=[b_ST[grp]], writes=[b_ST[grp]])

            def stage_Y():
                for g2 in range(2):
                    grp = 2 * p + g2
                    bo = b_op[grp]
                    bm = [b_M3[gi][2 * g2], b_M3[gi][2 * g2 + 1]]
                    pY, bpY = ps[6 + g2], bps[6 + g2]
                    for l8 in range(8):
                        h = grp * 8 + l8
                        l16 = g2 * 8 + l8
                        hp, par = h // 2, h % 2
                        pr = slice(par * 64, par * 64 + 64)
                        o = pY[:, l8 * 64:(l8 + 1) * 64]
                        S.op("pe", lambda e: e.matmul(out=o, lhsT=ARt[pr, hp, 1, :], rhs=STb[pr, hp, :], start=True, stop=False), reads=[bo, b_ST[grp]], writes=[bpY])
                        S.op("pe", lambda e: e.matmul(out=o, lhsT=M3[gi][:, l16, 0, :], rhs=UTb[g2][:, l8 * 64:(l8 + 1) * 64], start=False, stop=False), reads=bm + [b_U[g2]], writes=[bpY])
                        S.op("pe", lambda e: e.matmul(out=o, lhsT=M3[gi][:, l16, 2, :], rhs=VT[:, h * 64:(h + 1) * 64], start=False, stop=True), reads=bm + [bo], writes=[bpY])
                    S.op("act", lambda e: e.copy(out=yf[:].rearrange("p a b -> p (a b)"), in_=pY[:]), reads=[bpY], writes=[b_y])
                    S.op("pool", lambda e: e.tensor_tensor(out=ysq[:], in0=yf[:], in1=yf[:], op=ALU.mult), reads=[b_y], writes=[b_ysq])
                    g8 = slice(grp * 8, grp * 8 + 8)
                    S.op("dve", lambda e: e.tensor_reduce(out=gst[:, 0, g8], in_=yf[:], axis=AX.X, op=ALU.add), reads=[b_y], writes=[b_gst])
                    S.op("dve", lambda e: e.tensor_reduce(out=gst[:, 1, g8], in_=ysq[:], axis=AX.X, op=ALU.add), reads=[b_ysq], writes=[b_gst])
                    S.op("dve", lambda e: e.tensor_scalar(out=gst[:, 2, g8], in0=gst[:, 0, g8], scalar1=1.0 / 64, scalar2=None, op0=ALU.mult), reads=[b_gst], writes=[b_gst])
                    S.op("dve", lambda e: e.tensor_tensor(out=gst[:, 3, g8], in0=gst[:, 2, g8], in1=gst[:, 2, g8], op=ALU.mult), reads=[b_gst], writes=[b_gst])
                    S.op("dve", lambda e: e.scalar_tensor_tensor(out=gst[:, 4, g8], in0=gst[:, 1, g8], scalar=1.0 / 64, in1=gst[:, 3, g8], op0=ALU.mult, op1=ALU.subtract), reads=[b_gst], writes=[b_gst])
                    S.op("act", lambda e: e.activation(out=gst[:, 5, g8], in_=gst[:, 4, g8], func=AF.Sqrt, bias=C.cvals[:, 1:2], scale=1.0), reads=[b_gst, C.b_const], writes=[b_gst])
                    S.op("dve", lambda e: e.reciprocal(out=gst[:, 5, g8], in_=gst[:, 5, g8]), reads=[b_gst], writes=[b_gst])
                    S.op("dve", lambda e: e.tensor_tensor(out=yf[:], in0=yf[:], in1=gst[:, 2, g8].unsqueeze(2).to_broadcast([128, 8, 64]), op=ALU.subtract), reads=[b_y, b_gst], writes=[b_y])
                    S.op("pool", lambda e: e.tensor_tensor(out=yn[:, grp * 512:(grp + 1) * 512].rearrange("p (a b) -> p a b", a=8), in0=yf[:], in1=gst[:, 5, g8].unsqueeze(2).to_broadcast([128, 8, 64]), op=ALU.mult), reads=[b_y, b_gst], writes=[b_yn[grp]])

            return [stage_X, stage_U, stage_Y, stage_S]

        def F2a(tt):
            S.dma("sp", Gt[:], D["G_s"][tt], writes=[b_G])
            z_, bz_ = zb[tt % 2], b_zb[tt % 2]
            for hf in range(2):
                pz, bpz = bank()
                pzb = pz[:].bitcast(BF16)
                for j in range(8):
                    hp = hf * 8 + j
                    S.op("pe", lambda e: e.transpose(out=pzb[:, j * 128:(j + 1) * 128], in_=yn[:, hp * 128:(hp + 1) * 128], identity=C.ident[:]), reads=[b_yn[hp // 4], C.b_const], writes=[bpz])
                h8 = slice(hf * 8, hf * 8 + 8)
                lw_bc = pp[:, PLW + hf * 8:PLW + hf * 8 + 8].unsqueeze(2).to_broadcast([128, 8, 128])
                lb_bc = pp[:, PLB + hf * 8:PLB + hf * 8 + 8].unsqueeze(2).to_broadcast([128, 8, 128])
                S.op("dve", lambda e: e.tensor_tensor(out=zf[:], in0=pzb.rearrange("p (a b) -> p a b", a=8), in1=lw_bc, op=ALU.mult), reads=[bpz, b_pp], writes=[b_zf])
                S.op("pool", lambda e: e.tensor_tensor(out=zf[:], in0=zf[:], in1=lb_bc, op=ALU.add), reads=[b_zf, b_pp], writes=[b_zf])
                S.op("pool", lambda e: e.tensor_tensor(out=zf[:], in0=zf[:], in1=bonT[tt % 2][:, h8, :], op=ALU.add), reads=[b_zf] + b_op, writes=[b_zf])
                S.op("pool", lambda e: e.tensor_tensor(out=z_[:, h8, :], in0=zf[:], in1=Gt[:, h8, :], op=ALU.mult), reads=[b_zf, b_G], writes=[bz_])
            S.dma("sp", D["Z_s"][tt], z_[:], reads=[bz_], writes=[Buf()])

        passes = [(tt, p) for tt in range(NT) for p in range(2)]
        for g in (D_quarter(0, 0), D_quarter(0, 1)):
            for _ in g:
                pass
        for _ in G_pass(0, 0):
            pass
        for i, (tt, p) in enumerate(passes):
            gens = []
            if i + 1 < len(passes):
                tn, pn_ = passes[i + 1]
                gens = [D_quarter(tn, 2 * pn_), D_quarter(tn, 2 * pn_ + 1)]
            plan = {1: (0, 4), 2: (0, 3), 3: (0, 3), 4: (1, 4), 5: (1, 3), 6: (1, 3)}
            for lv in range(1, 7):
                N_level(p, lv)
                if gens:
                    gi_, nch = plan[lv]
                    for _ in range(nch):
                        next(gens[gi_], None)
            for g in gens:
                for _ in g:
                    pass
            gnext = G_pass(*passes[i + 1]) if i + 1 < len(passes) else iter(())
            for st_fn in state_stages(tt, p):
                st_fn()
                next(gnext, None)
            for _ in gnext:
                pass
            if p == 1:
                F2a(tt)
        S.run()


def phase_x(C):
    nc, S = C.nc, C.S
    NT = C.NT
    ps, bps = C.psum, C.b_ps
    D = C.dram
    with ExitStack() as px:
        sb = lambda n, s, d: px.enter_context(nc.sbuf_tensor(n, s, d))
        wo = sb("wo", [128, 16, 1024], BF16)
        gpost = sb("gpost", [128, 1024], F32)
        b_w = Buf("wres2")
        with ExitStack() as pl:
            wst = [pl.enter_context(nc.sbuf_tensor(f"wst3_{i}", [128, 2048], F32)) for i in range(2)]
            b_wst = [Buf(), Buf()]
            for c in range(8):
                S.dma("sp", wst[c % 2][:].rearrange("p (j n) -> p j n", j=2),
                      D["w_out0"][c * 256:(c + 1) * 256, :].rearrange("(j p) n -> p j n", p=128), writes=[b_wst[c % 2]])
                S.op("pool", lambda e: e.tensor_copy(out=wo[:, 2 * c:2 * c + 2, :], in_=wst[c % 2][:].rearrange("p (j n) -> p j n", j=2)),
                     reads=[b_wst[c % 2]], writes=[b_w])
            S.dma("sp", gpost[:], D["gpost0"].partition_broadcast(128), writes=[b_w])
            S.run()
        zin = [sb(f"zin{i}", [128, 16, 128], BF16) for i in range(2)]
        b_zin = [Buf(), Buf()]
        hin = [sb(f"hinX{i}", [128, 1024], F32) for i in range(2)]
        b_hin = [Buf(), Buf()]
        hout = [sb(f"houtX{i}", [128, 1024], F32) for i in range(2)]
        b_hout = [Buf(), Buf()]
        junk = sb("junkF", [128, 512], BF16)
        b_junk = Buf()
        pst = sb("pst", [128, 2, 4], F32)
        b_pst = [Buf(), Buf()]
        if getattr(C, "uT1", None) is not None:
            ub1 = [sb(f"ubX{i}", [128, 1024], BF16) for i in range(2)]
            b_ub1 = [Buf(), Buf()]
            junk1 = sb("junkX1", [128, 1024], BF16)
            b_junk1 = Buf()
            st1 = sb("ssX1", [128, NT], F32)
            rs1 = sb("rsX1", [128, NT], F32)
            b_st1 = [Buf() for _ in range(NT)]
        for tt in range(NT):
            t0 = tt * 128
            i2 = tt % 2
            S.dma("pool", zin[i2][:], D["Z_s"][tt], writes=[b_zin[i2]])
            S.dma("pool", hin[i2][:], D["h0"][t0:t0 + 128, :], writes=[b_hin[i2]])
            pm = [ps[2 * i2], ps[2 * i2 + 1]]
            bpm = [bps[2 * i2], bps[2 * i2 + 1]]
            for nh in range(2):
                for hp in range(16):
                    S.op("pe", lambda e: e.matmul(out=pm[nh][:], lhsT=zin[i2][:, hp, :], rhs=wo[:, hp, nh * 512:(nh + 1) * 512], start=(hp == 0), stop=(hp == 15)), reads=[b_zin[i2], b_w], writes=[bpm[nh]])
            st_ = pst[:, i2, :]
            S.op("pool", lambda e: e.memset(st_, 0.0), writes=[b_pst[i2]])
            for nh in range(2):
                S.op("act", lambda e: e.activation(out=junk[:], in_=pm[nh][:], func=AF.Square, bias=C.cvals[:, 2:3], scale=1.0, accum_out=st_[:, nh:nh + 1]), reads=[bpm[nh], C.b_const, b_junk], writes=[b_pst[i2]])
            S.op("dve", lambda e: e.tensor_tensor(out=st_[:, 2:3], in0=st_[:, 0:1], in1=st_[:, 1:2], op=ALU.add), reads=[b_pst[i2]], writes=[b_pst[i2]])
            S.op("act", lambda e: e.activation(out=st_[:, 3:4], in_=st_[:, 2:3], func=AF.Sqrt, bias=C.cvals[:, 0:1], scale=1.0 / 1024), reads=[b_pst[i2], C.b_const], writes=[b_pst[i2]])
            S.op("dve", lambda e: e.reciprocal(out=st_[:, 3:4], in_=st_[:, 3:4]), reads=[b_pst[i2]], writes=[b_pst[i2]])
            ho, bho = hout[i2], b_hout[i2]
            for nh in range(2):
                cs_ = slice(nh * 512, (nh + 1) * 512)
                S.op("dve", lambda e: e.scalar_tensor_tensor(out=ho[:, cs_], in0=pm[nh][:], scalar=st_[:, 3:4], in1=gpost[:, cs_], op0=ALU.mult, op1=ALU.mult), reads=[bpm[nh], b_pst[i2], b_w], writes=[bho])
            S.op("dve", lambda e: e.tensor_tensor(out=ho[:], in0=ho[:], in1=hin[i2][:], op=ALU.add), reads=[bho, b_hin[i2]], writes=[bho])
            S.dma("sp", D["H1"][t0:t0 + 128, :], ho[:], reads=[bho], writes=[Buf()])
            if getattr(C, "uT1", None) is not None:
                uT1, b_uT1, gpre1 = C.uT1, C.b_uT1, C.gpre1
                S.op("pool", lambda e: e.memset(st1[:, tt:tt + 1], 0.0), writes=[b_st1[tt]])
                rmsnorm_stats(C, ho[:], 1024, st1[:, tt:tt + 1], rs1[:, tt:tt + 1], junk1[:], [bho, b_junk1], b_st1[tt])
                u1, bu1 = ub1[i2], b_ub1[i2]
                S.op("dve", lambda e: e.scalar_tensor_tensor(out=u1[:], in0=ho[:], scalar=rs1[:, tt:tt + 1], in1=gpre1[:], op0=ALU.mult, op1=ALU.mult), reads=[bho, b_st1[tt], C.b_gpre1], writes=[bu1])
                pb = ps[4 + i2][:].bitcast(BF16)
                for c in range(8):
                    S.op("pe", lambda e: e.transpose(out=pb[:, c * 128:(c + 1) * 128], in_=u1[:, c * 128:(c + 1) * 128], identity=C.ident[:]), reads=[bu1, C.b_const], writes=[bps[4 + i2]])
                S.op("act", lambda e: e.copy(out=uT1[:, :, t0:t0 + 128], in_=pb.rearrange("p (c k) -> p c k", c=8)), reads=[bps[4 + i2]], writes=[b_uT1])
        S.run()


SCALE = 192.0 ** -0.5
TWO_PI = 2.0 * math.pi


def layer1(C):
    nc, S = C.nc, C.S
    NT = C.NT
    L = NT * 128
    STS = supertiles(NT)
    ps, bps = C.psum, C.b_ps
    D = C.dram
    cv = C.cvals
    bc = C.b_const
    with ExitStack() as l1:
        sb1 = lambda n, s, d: l1.enter_context(nc.sbuf_tensor(n, s, d))
        pp = sb1("pp1_sb", [128, 8], F32)
        b_pp = Buf("pp1")
        S.dma("sp", pp[:], D["pp1"], writes=[b_pp])
        KR = sb1("KR", [128, L], BF16)
        b_KR = Buf("KR")
        mx = sb1("mx", [128, 8], F32)
        b_mx = Buf("mx")
        S.op("pool", lambda e: e.memset(mx[:], 0.0), writes=[b_mx])
        onesb = sb1("onesb", [128, 128], BF16)
        S.op("pool", lambda e: e.memset(onesb[:], 1.0), writes=[bc])
        fr = sb1("fr", [64, 2], F32)
        fri = sb1("fri", [64, 2], I32)
        S.op("pool", lambda e: e.iota(out=fri[:, 0:1], pattern=[[0, 1]], base=0, channel_multiplier=1), writes=[bc])
        S.op("dve", lambda e: e.tensor_single_scalar(out=fri[:, 1:2], in_=fri[:, 0:1], scalar=31, op=ALU.bitwise_and), reads=[bc], writes=[bc])
        S.op("pool", lambda e: e.tensor_copy(out=fr[:, 0:1], in_=fri[:, 1:2]), reads=[bc], writes=[bc])
        S.op("act", lambda e: e.activation(out=fr[:, 1:2], in_=fr[:, 0:1], func=AF.Exp, bias=cv[0:64, 2:3], scale=-math.log(10000.0) / 32.0), reads=[bc], writes=[bc])
        S.run()

        with ExitStack() as pa:
            sa = lambda n, s, d: pa.enter_context(nc.sbuf_tensor(n, s, d))
            fusedA = getattr(C, "uT1", None) is not None
            if fusedA:
                uT, b_uT = C.uT1, C.b_uT1
            else:
                uT = sa("uT1", [128, 8, L], BF16)
                b_uT = Buf("uT1")
            with ExitStack() as pA:
                if fusedA:
                    NTA = 0
                else:
                    NTA = NT
                sA = lambda n, s, d: pA.enter_context(nc.sbuf_tensor(n, s, d))
                gpre = sA("gpre1_sb", [128, 1024], F32)
                b_g = Buf()
                S.dma("sp", gpre[:], D["gpre1"].partition_broadcast(128), writes=[b_g])
                xin = [sA(f"xin1_{i}", [128, 1024], F32) for i in range(2)]
                b_xin = [Buf() for _ in range(2)]
                ub = [sA(f"ub1_{i}", [128, 1024], BF16) for i in range(2)]
                b_ub = [Buf() for _ in range(2)]
                junk = sA("junkA1", [128, 1024], BF16)
                b_junk = Buf()
                st_ss = sA("ssA1", [128, NT], F32)
                st_rs = sA("rsA1", [128, NT], F32)
                b_st = [Buf() for _ in range(NT)]
                for tt in range(NTA):
                    x, bx = xin[tt % 2], b_xin[tt % 2]
                    u, bu = ub[tt % 2], b_ub[tt % 2]
                    S.dma("sp", x[:], D["H1"][tt * 128:(tt + 1) * 128, :], writes=[bx])
                    S.op("pool", lambda e: e.memset(st_ss[:, tt:tt + 1], 0.0), writes=[b_st[tt]])
                    rmsnorm_stats(C, x[:], 1024, st_ss[:, tt:tt + 1], st_rs[:, tt:tt + 1], junk[:], [bx, b_junk], b_st[tt])
                    S.op("dve", lambda e: e.scalar_tensor_tensor(out=u[:], in0=x[:], scalar=st_rs[:, tt:tt + 1], in1=gpre[:], op0=ALU.mult, op1=ALU.mult),
                         reads=[bx, b_st[tt], b_g], writes=[bu])
                    pb = ps[tt % 2][:].bitcast(BF16)
                    for c in range(8):
                        S.op("pe", lambda e: e.transpose(out=pb[:, c * 128:(c + 1) * 128], in_=u[:, c * 128:(c + 1) * 128], identity=C.ident[:]),
                             reads=[bu, bc], writes=[bps[tt % 2]])
                    dst = uT[:, :, tt * 128:(tt + 1) * 128]
                    src = pb.rearrange("p (c k) -> p c k", c=8)
                    if tt % 2 == 0:
                        S.op("act", lambda e: e.copy(out=dst, in_=src), reads=[bps[tt % 2]], writes=[b_uT])
                    else:
                        S.op("dve", lambda e: e.tensor_copy(out=dst, in_=src), reads=[bps[tt % 2]], writes=[b_uT])
                S.run()

            with ExitStack() as pB:
                sB = lambda n, s, d: pB.enter_context(nc.sbuf_tensor(n, s, d))
                wst = [sB(f"wstB{i}", [128, 2048], F32) for i in range(2)]
                b_wst = [Buf(), Buf()]
                wcnt = [0]

                def load_w(dst_ap, src_ap, rows, cols, bdst):
                    i = wcnt[0] % 2
                    wcnt[0] += 1
                    S.dma("sp", wst[i][0:rows, 0:cols], src_ap, writes=[b_wst[i]])
                    if i == 0:
                        S.op("pool", lambda e: e.tensor_copy(out=dst_ap, in_=wst[i][0:rows, 0:cols]), reads=[b_wst[i]], writes=[bdst])
                    else:
                        S.op("act", lambda e: e.copy(out=dst_ap, in_=wst[i][0:rows, 0:cols]), reads=[b_wst[i]], writes=[bdst])

                stg = [sB(f"stgB{i}", [128, 512], BF16) for i in range(4)]
                b_stg = [Buf() for _ in range(4)]
                sqb_all = [sB(f"sqB{i}", [128, 512], BF16) for i in range(4)]
                b_sqb_all = [Buf() for _ in range(4)]
                sqb, b_sqb = sqb_all[0:2], b_sqb_all[0:2]
                nmc = [0]
                red = sB("redB", [128, 4], F32)
                b_red = Buf()
                cnt = [0]

                pend_norm = []

                def flush_norm():
                    while pend_norm:
                        a_, c_ = pend_norm.pop(0)
                        norm_max(a_, c_)

                def norm_max(sq_list, col):
                    nmc[0] += 1
                    pn, bpn = ps[6 + nmc[0] % 2], bps[6 + nmc[0] % 2]
                    n = sq_list[0][0].shape[-1]
                    for i, (ap, K, b) in enumerate(sq_list):
                        S.op("pe", lambda e: e.matmul(out=pn[:, :n], lhsT=onesb[0:K, :], rhs=ap, start=(i == 0), stop=(i == len(sq_list) - 1)), reads=[b, bc], writes=[bpn])
                    S.op("dve", lambda e: e.tensor_reduce(out=red[:, 0:1], in_=pn[:, :n], axis=AX.X, op=ALU.max), reads=[bpn], writes=[b_red])
                    S.op("dve", lambda e: e.tensor_tensor(out=mx[:, col:col + 1], in0=mx[:, col:col + 1], in1=red[:, 0:1], op=ALU.max), reads=[b_red, b_mx], writes=[b_mx])

                def feat_rmsnorm(src, bsrc, nch, n, gcol0, dst, bdst, nfeat):
                    pn, bpn = ps[6], bps[6]
                    for c in range(nch):
                        sq, bsq = sqb[c % 2], b_sqb[c % 2]
                        S.op("act", lambda e: e.activation(out=sq[:, :n], in_=src[:, c, :n], func=AF.Square, bias=cv[:, 2:3], scale=1.0), reads=[bsrc, bc], writes=[bsq])
                        S.op("pe", lambda e: e.matmul(out=pn[:, :n], lhsT=onesb[:], rhs=sq[:, :n], start=(c == 0), stop=(c == nch - 1)), reads=[bsq, bc], writes=[bpn])
                    rs, brs = rstd, b_rstd
                    S.op("act", lambda e: e.activation(out=rs[:, :n], in_=pn[:, :n], func=AF.Sqrt, bias=cv[:, 0:1], scale=1.0 / nfeat), reads=[bpn, bc], writes=[brs])
                    S.op("dve", lambda e: e.reciprocal(out=rs[:, :n], in_=rs[:, :n]), reads=[brs], writes=[brs])
                    for c in range(nch):
                        S.op("dve", lambda e: e.scalar_tensor_tensor(out=dst[:, c, :n], in0=src[:, c, :n], scalar=pp[:, gcol0 + c:gcol0 + c + 1], in1=rs[:, :n], op0=ALU.mult, op1=ALU.mult),
                             reads=[bsrc, brs, b_pp], writes=[bdst])

                rstd = sB("rstdB", [128, 512], F32)
                b_rstd = Buf()
                cosT = sB("cosT", [64, 512], F32)
                sinT = sB("sinT", [64, 512], F32)
                b_tab = Buf()
                angi = sB("angi", [64, 512], I32)
                angf = sB("angf", [64, 512], F32)
                angn = sB("angn", [64, 512], F32)
                angq = sB("angq", [64, 512], I32)

                def rope_tables(t0, n):
                    S.op("pool", lambda e: e.iota(out=angi[:, :n], pattern=[[1, n]], base=t0, channel_multiplier=0), writes=[b_tab])
                    S.op("pool", lambda e: e.tensor_copy(out=angf[:, :n], in_=angi[:, :n]), reads=[b_tab], writes=[b_tab])
                    S.op("dve", lambda e: e.tensor_scalar(out=angf[:, :n], in0=angf[:, :n], scalar1=fr[:, 1:2], scalar2=1.0 / TWO_PI, op0=ALU.mult, op1=ALU.mult), reads=[b_tab, bc], writes=[b_tab])
                    for which, dst in ((0, sinT), (1, cosT)):
                        if which == 1:
                            S.op("dve", lambda e: e.tensor_scalar(out=angf[:, :n], in0=angf[:, :n], scalar1=0.25, scalar2=None, op0=ALU.add), reads=[b_tab], writes=[b_tab])
                        S.op("dve", lambda e: e.tensor_copy(out=angq[:, :n], in_=angf[:, :n]), reads=[b_tab], writes=[b_tab])
                        S.op("dve", lambda e: e.tensor_copy(out=angn[:, :n], in_=angq[:, :n]), reads=[b_tab], writes=[b_tab])
                        S.op("dve", lambda e: e.tensor_tensor(out=angn[:, :n], in0=angf[:, :n], in1=angn[:, :n], op=ALU.subtract), reads=[b_tab], writes=[b_tab])
                        S.op("act", lambda e: e.activation(out=dst[:, :n], in_=angn[:, :n], func=AF.Sin, bias=cv[0:64, 2:3], scale=TWO_PI), reads=[b_tab, bc], writes=[b_tab])
                    S.op("dve", lambda e: e.tensor_scalar(out=sinT[0:32, :n], in0=sinT[0:32, :n], scalar1=-1.0, scalar2=None, op0=ALU.mult), reads=[b_tab], writes=[b_tab])

                rtmp = [sB(f"rtmp{i}", [64, 512], F32) for i in range(2)]
                b_rtmp = [Buf(), Buf()]

                def rope_rot(p_main, b_main, p_sw, b_sw, n, dst_ap, bdst):
                    S.op("dve", lambda e: e.tensor_tensor(out=rtmp[0][:, :n], in0=p_main, in1=cosT[:, :n], op=ALU.mult), reads=[b_main, b_tab], writes=[b_rtmp[0]])
                    S.op("dve", lambda e: e.tensor_tensor(out=rtmp[1][:, :n], in0=p_sw, in1=sinT[:, :n], op=ALU.mult), reads=[b_sw, b_tab], writes=[b_rtmp[1]])
                    S.op("dve", lambda e: e.tensor_tensor(out=dst_ap, in0=rtmp[0][:, :n], in1=rtmp[1][:, :n], op=ALU.add), reads=[b_rtmp[0], b_rtmp[1]], writes=[bdst])

                with ExitStack() as pB1:
                    s1 = lambda n, s, d: pB1.enter_context(nc.sbuf_tensor(n, s, d))
                    w1q = s1("w1q", [128, 8, 512], BF16)
                    wqn = s1("wqn", [128, 4, 2048], BF16)
                    wqr = s1("wqr", [128, 4, 1024], BF16)
                    wqrs = s1("wqrs", [128, 4, 1024], BF16)
                    b_wq = Buf()
                    for c in range(8):
                        load_w(w1q[:, c, :], D["w_in1"][c * 128:(c + 1) * 128, 0:512], 128, 512, b_wq)
                    for c in range(4):
                        load_w(wqn[:, c, :], D["wq_n"][c * 128:(c + 1) * 128, :], 128, 2048, b_wq)
                        load_w(wqr[:, c, :], D["wq_r"][c * 128:(c + 1) * 128, :], 128, 1024, b_wq)
                        load_w(wqrs[:, c, :], D["wq_rs"][c * 128:(c + 1) * 128, :], 128, 1024, b_wq)
                    cq = s1("cq", [128, 4, 512], F32)
                    b_cq = Buf()
                    qn = s1("qn", [128, 4, 512], BF16)
                    b_qn = Buf()
                    for si, (tt0, nt) in enumerate(STS):
                        t0, n = tt0 * 128, nt * 128
                        rope_tables(t0, n)
                        for c4 in range(4):
                            p_, bp_ = ps[c4 % 2], bps[c4 % 2]
                            for c in range(8):
                                S.op("pe", lambda e: e.matmul(out=p_[:, :n], lhsT=w1q[:, c, c4 * 128:(c4 + 1) * 128], rhs=uT[:, c, t0:t0 + n], start=(c == 0), stop=(c == 7)), reads=[b_wq, b_uT], writes=[bp_])
                            S.op("act", lambda e: e.copy(out=cq[:, c4, :n], in_=p_[:, :n]), reads=[bp_], writes=[b_cq])
                        feat_rmsnorm(cq, b_cq, 4, n, 0, qn, b_qn, 512)
                        for h in range(16):
                            k = cnt[0]
                            cnt[0] += 1
                            sqb, b_sqb = sqb_all[2 * (k % 2):2 * (k % 2) + 2], b_sqb_all[2 * (k % 2):2 * (k % 2) + 2]
                            p_, bp_ = ps[k % 2], bps[k % 2]
                            s_, bs_ = stg[k % 4], b_stg[k % 4]
                            for c in range(4):
                                S.op("pe", lambda e: e.matmul(out=p_[:, :n], lhsT=wqn[:, c, h * 128:(h + 1) * 128], rhs=qn[:, c, :n], start=(c == 0), stop=(c == 3)), reads=[b_wq, b_qn], writes=[bp_])
                            S.op("act", lambda e: e.copy(out=s_[:, :n], in_=p_[:, :n]), reads=[bp_], writes=[bs_])
                            S.dma("sp", D["QN_s"][h, :, t0:t0 + n], s_[:, :n], reads=[bs_], writes=[Buf()])
                            pr, bpr = ps[2 + (k % 2) * 2], bps[2 + (k % 2) * 2]
                            prs, bprs = ps[3 + (k % 2) * 2], bps[3 + (k % 2) * 2]
                            for c in range(4):
                                S.op("pe", lambda e: e.matmul(out=pr[0:64, :n], lhsT=wqr[:, c, h * 64:(h + 1) * 64], rhs=qn[:, c, :n], start=(c == 0), stop=(c == 3)), reads=[b_wq, b_qn], writes=[bpr])
                            for c in range(4):
                                S.op("pe", lambda e: e.matmul(out=prs[0:64, :n], lhsT=wqrs[:, c, h * 64:(h + 1) * 64], rhs=qn[:, c, :n], start=(c == 0), stop=(c == 3)), reads=[b_wq, b_qn], writes=[bprs])
                            flush_norm()
                            s2, bs2 = stg[(k + 2) % 4], b_stg[(k + 2) % 4]
                            rope_rot(pr[0:64, :n], bpr, prs[0:64, :n], bprs, n, s2[0:64, :n], bs2)
                            S.dma("sp", D["QR_s"][h, :, t0:t0 + n], s2[0:64, :n], reads=[bs2], writes=[Buf()])
                            S.op("act", lambda e: e.activation(out=sqb[0][:, :n], in_=s_[:, :n], func=AF.Square, bias=cv[0:128, 2:3], scale=1.0), reads=[bs_], writes=[b_sqb[0]])
                            S.op("act", lambda e: e.activation(out=sqb[1][0:64, :n], in_=s2[0:64, :n], func=AF.Square, bias=cv[0:64, 2:3], scale=1.0), reads=[bs2], writes=[b_sqb[1]])
                            pend_norm.append(([(sqb[0][:, :n], 128, b_sqb[0]), (sqb[1][0:64, :n], 64, b_sqb[1])], 0))
                    flush_norm()
                    S.run()

                with ExitStack() as pB2:
                    s2_ = lambda n, s, d: pB2.enter_context(nc.sbuf_tensor(n, s, d))
                    w1k = s2_("w1k", [128, 8, 384], BF16)
                    wkk = s2_("wkk", [128, 2, 2048], BF16)
                    wkv = s2_("wkv", [128, 2, 2048], BF16)
                    b_wk = Buf()
                    for c in range(8):
                        load_w(w1k[:, c, 0:320], D["w_in1"][c * 128:(c + 1) * 128, 512:832], 128, 320, b_wk)
                        load_w(w1k[:, c, 320:384], D["w_in1"][c * 128:(c + 1) * 128, 2880:2944], 128, 64, b_wk)
                    for c in range(2):
                        load_w(wkk[:, c, :], D["wkv_k"][c * 128:(c + 1) * 128, :], 128, 2048, b_wk)
                        load_w(wkv[:, c, :], D["wkv_v"][c * 128:(c + 1) * 128, :], 128, 2048, b_wk)
                    ckv = s2_("ckv", [128, 2, 512], F32)
                    b_ckv = Buf()
                    kvn = s2_("kvn", [128, 2, 512], BF16)
                    b_kvn = Buf()
                    vst = [s2_(f"vst{i}", [128, 16, 129], BF16) for i in range(2)]
                    b_vst = [Buf(), Buf()]
                    for i in range(2):
                        S.op("pool", lambda e: e.memset(vst[i][:, :, 128:129], 1.0), writes=[b_vst[i]])
                    vc = 0
                    for si, (tt0, nt) in enumerate(STS):
                        t0, n = tt0 * 128, nt * 128
                        rope_tables(t0, n)
                        for c2 in range(2):
                            p_, bp_ = ps[c2 % 2], bps[c2 % 2]
                            for c in range(8):
                                S.op("pe", lambda e: e.matmul(out=p_[:, :n], lhsT=w1k[:, c, c2 * 128:(c2 + 1) * 128], rhs=uT[:, c, t0:t0 + n], start=(c == 0), stop=(c == 7)), reads=[b_wk, b_uT], writes=[bp_])
                            S.op("act", lambda e: e.copy(out=ckv[:, c2, :n], in_=p_[:, :n]), reads=[bp_], writes=[b_ckv])
                        feat_rmsnorm(ckv, b_ckv, 2, n, 4, kvn, b_kvn, 256)
                        pr, bpr = ps[2], bps[2]
                        prs, bprs = ps[3], bps[3]
                        for c in range(8):
                            S.op("pe", lambda e: e.matmul(out=pr[0:64, :n], lhsT=w1k[:, c, 256:320], rhs=uT[:, c, t0:t0 + n], start=(c == 0), stop=(c == 7)), reads=[b_wk, b_uT], writes=[bpr])
                        for c in range(8):
                            S.op("pe", lambda e: e.matmul(out=prs[0:64, :n], lhsT=w1k[:, c, 320:384], rhs=uT[:, c, t0:t0 + n], start=(c == 0), stop=(c == 7)), reads=[b_wk, b_uT], writes=[bprs])
                        rope_rot(pr[0:64, :n], bpr, prs[0:64, :n], bprs, n, KR[0:64, t0:t0 + n], b_KR)
                        S.op("act", lambda e: e.activation(out=sqb[1][0:64, :n], in_=KR[0:64, t0:t0 + n], func=AF.Square, bias=cv[0:64, 2:3], scale=1.0), reads=[b_KR], writes=[b_sqb[1]])
                        norm_max([(sqb[1][0:64, :n], 64, b_sqb[1])], 2)
                        for h in range(16):
                            k = cnt[0]
                            cnt[0] += 1
                            sqb, b_sqb = sqb_all[2 * (k % 2):2 * (k % 2) + 2], b_sqb_all[2 * (k % 2):2 * (k % 2) + 2]
                            p_, bp_ = ps[k % 2], bps[k % 2]
                            s_, bs_ = stg[k % 4], b_stg[k % 4]
                            for c in range(2):
                                S.op("pe", lambda e: e.matmul(out=p_[:, :n], lhsT=wkk[:, c, h * 128:(h + 1) * 128], rhs=kvn[:, c, :n], start=(c == 0), stop=(c == 1)), reads=[b_wk, b_kvn], writes=[bp_])
                            flush_norm()
                            S.op("act", lambda e: e.copy(out=s_[:, :n], in_=p_[:, :n]), reads=[bp_], writes=[bs_])
                            S.dma("sp", D["KN_s"][h, :, t0:t0 + n], s_[:, :n], reads=[bs_], writes=[Buf()])
                            S.op("act", lambda e: e.activation(out=sqb[0][:, :n], in_=s_[:, :n], func=AF.Square, bias=cv[0:128, 2:3], scale=1.0), reads=[bs_], writes=[b_sqb[0]])
                            pend_norm.append(([(sqb[0][:, :n], 128, b_sqb[0])], 1))
                        flush_norm()
                        for j in range(nt):
                            vs, bvs = vst[vc % 2], b_vst[vc % 2]
                            vc += 1
                            for n4 in range(4):
                                p_, bp_ = ps[4 + n4 % 2], bps[4 + n4 % 2]
                                for c in range(2):
                                    S.op("pe", lambda e: e.matmul(out=p_[:], lhsT=kvn[:, c, j * 128:(j + 1) * 128], rhs=wkv[:, c, n4 * 512:(n4 + 1) * 512], start=(c == 0), stop=(c == 1)), reads=[b_wk, b_kvn], writes=[bp_])
                                if n4 % 2 == 0:
                                    S.op("act", lambda e: e.copy(out=vs[:, 4 * n4:4 * n4 + 4, 0:128], in_=p_[:].rearrange("p (a b) -> p a b", a=4)), reads=[bp_], writes=[bvs])
                                else:
                                    S.op("dve", lambda e: e.tensor_copy(out=vs[:, 4 * n4:4 * n4 + 4, 0:128], in_=p_[:].rearrange("p (a b) -> p a b", a=4)), reads=[bp_], writes=[bvs])
                            S.dma("sp", D["V1_s"][tt0 + j], vs[:], reads=[bvs], writes=[Buf()])
                    flush_norm()
                    S.run()

                with ExitStack() as pB3:
                    s3 = lambda n, s, d: pB3.enter_context(nc.sbuf_tensor(n, s, d))
                    w1g = s3("w1g", [128, 8, 2048], BF16)
                    b_wg = Buf()
                    for c in range(8):
                        load_w(w1g[:, c, :], D["w_in1"][c * 128:(c + 1) * 128, 832:2880], 128, 2048, b_wg)
                    for si, (tt0, nt) in enumerate(STS):
                        t0, n = tt0 * 128, nt * 128
                        for oc in range(16):
                            k = cnt[0]
                            cnt[0] += 1
                            p_, bp_ = ps[k % 4], bps[k % 4]
                            s_, bs_ = stg[k % 4], b_stg[k % 4]
                            for c in range(8):
                                S.op("pe", lambda e: e.matmul(out=p_[:, :n], lhsT=w1g[:, c, oc * 128:(oc + 1) * 128], rhs=uT[:, c, t0:t0 + n], start=(c == 0), stop=(c == 7)), reads=[b_wg, b_uT], writes=[bp_])
                            S.op("act", lambda e: e.activation(out=s_[:, :n], in_=p_[:, :n], func=AF.Silu, bias=cv[:, 2:3], scale=1.0), reads=[bp_, bc], writes=[bs_])
                            S.dma("sp", D["G_s"][tt0:tt0 + nt, :, oc, :].rearrange("t p k -> p t k"), s_[:, :n].rearrange("p (t k) -> p t k", t=nt), reads=[bs_], writes=[Buf()])
                    S.run()

        if getattr(C, "stop_after", None) == "l1b":
            raise StopBuild()
        S.dma("sp", KR[64:128, :], KR[0:64, :], reads=[b_KR], writes=[b_KR])
        S.op("dve", lambda e: e.tensor_tensor(out=mx[:, 3:4], in0=mx[:, 1:2], in1=mx[:, 2:3], op=ALU.add), reads=[b_mx], writes=[b_mx])
        S.op("dve", lambda e: e.tensor_tensor(out=mx[:, 3:4], in0=mx[:, 3:4], in1=mx[:, 0:1], op=ALU.mult), reads=[b_mx], writes=[b_mx])
        S.op("act", lambda e: e.activation(out=mx[:, 4:5], in_=mx[:, 3:4], func=AF.Sqrt, bias=cv[:, 2:3], scale=1.0), reads=[b_mx, bc], writes=[b_mx])
        S.op("dve", lambda e: e.tensor_scalar(out=mx[:, 4:5], in0=mx[:, 4:5], scalar1=-SCALE, scalar2=None, op0=ALU.mult), reads=[b_mx], writes=[b_mx])
        S.run()

        with ExitStack() as pC:
            sC = lambda n, s, d: pC.enter_context(nc.sbuf_tensor(n, s, d))
            QN = [sC(f"QN{i}", [128, L], BF16) for i in range(2)]
            QR = [sC(f"QR{i}", [128, L], BF16) for i in range(2)]
            KN = [sC(f"KN{i}", [128, L], BF16) for i in range(2)]
            VA = [sC(f"VA{i}", [128, NT, 129], BF16) for i in range(2)]
            b_hd = [[Buf() for _ in range(5)] for _ in range(2)]
            NPT = 6
            SB = (0, 1, 6, 7)
            PT = [sC(f"PT{i}", [128, 512], BF16) for i in range(NPT)]
            b_PT = [Buf() for _ in range(NPT)]
            ost = [sC(f"ost{i}", [128, 128], BF16) for i in range(4)]
            b_ost = [Buf() for _ in range(4)]
            rl = sC("rl", [128, 8], F32)
            b_rl = Buf()
            mUib = sC("mUib", [128, 128], BF16)
            S.op("pool", lambda e: e.tensor_copy(out=mUib[:], in_=C.mUi[:]), reads=[bc], writes=[bc])
            kblk = 0
            oc_ = 0
            for h in range(16):
                hb = h % 2
                bh = b_hd[hb]
                S.dma("pool", QN[hb][:], D["QN_s"][h], writes=[bh[0]])
                S.dma("pool", QR[hb][0:64, :], D["QR_s"][h], writes=[bh[1]])
                S.dma("pool", QR[hb][64:128, :], D["QR_s"][h], writes=[bh[4]])
                S.dma("pool", KN[hb][:], D["KN_s"][h], writes=[bh[2]])
                S.dma("pool", VA[hb][:], D["V1_s"][:, :, h, :].rearrange("t p k -> p t k"), writes=[bh[3]])
                for (tt0, nt) in STS:
                    q0 = tt0 * 128
                    pO = [ps[2 + j] for j in range(nt)]
                    bpO = [bps[2 + j] for j in range(nt)]
                    blocks = []
                    for kt in range(tt0 + nt):
                        j0 = max(0, kt - tt0)
                        blocks.append((kt, j0, (nt - j0) * 128, q0 + j0 * 128))
                    state = {}
                    sinfo = {}

                    def emit_S(i, part="all", rg=0):
                        nonlocal kblk
                        if part == "rope":
                            kt, j0, ncol, qc0 = blocks[i]
                            pS, bpS, pt_, bpt = sinfo[i]
                            rs_ = slice(rg * 64, rg * 64 + 64)
                            S.op("pe", lambda e: e.matmul(out=pS[:, :ncol], lhsT=KR[rs_, kt * 128:(kt + 1) * 128], rhs=QR[hb][rs_, qc0:qc0 + ncol], start=False, stop=True), reads=[bh[1], bh[4], b_KR], writes=[bpS])
                            S.op("act", lambda e: e.activation(out=pt_[:, :ncol], in_=pS[:, :ncol], func=AF.Exp, bias=mx[:, 4:5], scale=SCALE), reads=[bpS, b_mx], writes=[bpt])
                            if kt >= tt0:
                                S.op("dve", lambda e: e.tensor_tensor(out=pt_[:, 0:128], in0=pt_[:, 0:128], in1=mUib[:], op=ALU.mult), reads=[bpt, bc], writes=[bpt])
                            return
                        if part == "nope":
                            kt, j0, ncol, qc0 = blocks[i]
                            pS, bpS = ps[SB[kblk % 4]], bps[SB[kblk % 4]]
                            pt_, bpt = PT[kblk % NPT], b_PT[kblk % NPT]
                            kblk += 1
                            state[i] = (pt_, bpt)
                            sinfo[i] = (pS, bpS, pt_, bpt)
                            S.op("pe", lambda e: e.matmul(out=pS[:, :ncol], lhsT=KN[hb][:, kt * 128:(kt + 1) * 128], rhs=QN[hb][:, qc0:qc0 + ncol], start=True, stop=False), reads=[bh[0], bh[2]], writes=[bpS])
                            return
                        kt, j0, ncol, qc0 = blocks[i]
                        pS, bpS = ps[SB[kblk % 4]], bps[SB[kblk % 4]]
                        pt_, bpt = PT[kblk % NPT], b_PT[kblk % NPT]
                        kblk += 1
                        state[i] = (pt_, bpt)
                        S.op("pe", lambda e: e.matmul(out=pS[:, :ncol], lhsT=KN[hb][:, kt * 128:(kt + 1) * 128], rhs=QN[hb][:, qc0:qc0 + ncol], start=True, stop=False), reads=[bh[0], bh[2]], writes=[bpS])
                        S.op("pe", lambda e: e.matmul(out=pS[:, :ncol], lhsT=KR[:, kt * 128:(kt + 1) * 128], rhs=QR[hb][:, qc0:qc0 + ncol], start=False, stop=True), reads=[bh[1], b_KR], writes=[bpS])
                        S.op("act", lambda e: e.activation(out=pt_[:, :ncol], in_=pS[:, :ncol], func=AF.Exp, bias=mx[:, 4:5], scale=SCALE), reads=[bpS, b_mx], writes=[bpt])
                        if kt >= tt0:
                            S.op("dve", lambda e: e.tensor_tensor(out=pt_[:, 0:128], in0=pt_[:, 0:128], in1=mUib[:], op=ALU.mult), reads=[bpt, bc], writes=[bpt])

                    def emit_PV(i):
                        kt, j0, ncol, qc0 = blocks[i]
                        pt_, bpt = state.pop(i)
                        for j in range(j0, nt):
                            cj = (j - j0) * 128
                            S.op("pe", lambda e: e.matmul(out=pO[j][:, 0:129], lhsT=pt_[:, cj:cj + 128], rhs=VA[hb][:, kt, :], start=(kt == 0), stop=(kt == tt0 + j)), reads=[bpt, bh[3]], writes=[bpO[j]])

                    nb = len(blocks)

                    def emit_pair(i):
                        if i + 1 < nb:
                            emit_S(i, "nope")
                            emit_S(i + 1, "nope")
                            emit_S(i, "rope", 0)
                            emit_S(i + 1, "rope", 1)
                        elif i < nb:
                            emit_S(i, "nope")
                            emit_S(i, "rope", 0)
                    emit_pair(0)
                    for i in range(0, nb, 2):
                        emit_pair(i + 2)
                        emit_PV(i)
                        if i + 1 < nb:
                            emit_PV(i + 1)
                    for j in range(nt):
                        S.op("dve", lambda e: e.reciprocal(out=rl[:, j:j + 1], in_=pO[j][:, 128:129]), reads=[bpO[j]], writes=[b_rl])
                        o_, bo_ = ost[oc_ % 4], b_ost[oc_ % 4]
                        oc_ += 1
                        S.op("dve", lambda e: e.tensor_scalar(out=o_[:], in0=pO[j][:, 0:128], scalar1=rl[:, j:j + 1], scalar2=None, op0=ALU.mult), reads=[bpO[j], b_rl], writes=[bo_])
                        S.dma("sp", D["O_s"][tt0 + j, :, h * 128:(h + 1) * 128], o_[:], reads=[bo_], writes=[Buf()])
            S.run()

        if getattr(C, "stop_after", None) == "l1c":
            raise StopBuild()
        with ExitStack() as pD:
            sD = lambda n, s, d: pD.enter_context(nc.sbuf_tensor(n, s, d))
            wo = sD("wo1", [128, 16, 1024], BF16)
            gpost = sD("gpost1_sb", [128, 1024], F32)
            b_w = Buf()
            with ExitStack() as pl:
                wst = [pl.enter_context(nc.sbuf_tensor(f"wstD{i}", [128, 2048], F32)) for i in range(2)]
                b_wst = [Buf(), Buf()]
                for c in range(8):
                    S.dma("sp", wst[c % 2][:].rearrange("p (j n) -> p j n", j=2),
                          D["w_out1"][c * 256:(c + 1) * 256, :].rearrange("(j p) n -> p j n", p=128), writes=[b_wst[c % 2]])
                    S.op("pool", lambda e: e.tensor_copy(out=wo[:, 2 * c:2 * c + 2, :], in_=wst[c % 2][:].rearrange("p (j n) -> p j n", j=2)),
                         reads=[b_wst[c % 2]], writes=[b_w])
                S.dma("sp", gpost[:], D["gpost1"].partition_broadcast(128), writes=[b_w])
                S.run()
            ot = [sD(f"ot{i}", [128, 2048], BF16) for i in range(2)]
            b_ot = [Buf(), Buf()]
            Gt = [sD(f"Gt1_{i}", [128, 16, 128], BF16) for i in range(2)]
            b_G = [Buf(), Buf()]
            hin = [sD(f"hin1_{i}", [128, 1024], F32) for i in range(2)]
            b_hin = [Buf(), Buf()]
            hout = [sD(f"hout1_{i}", [128, 1024], F32) for i in range(2)]
            b_hout = [Buf(), Buf()]
            zbs = [sD(f"zb1_{i}", [128, 16, 128], BF16) for i in range(2)]
            b_zbs = [Buf(), Buf()]
            junk = sD("junkD1", [128, 512], BF16)
            b_junk = Buf()
            pst_all = sD("pst1", [128, 8], F32)
            b_pst_all = [Buf(), Buf()]
            NOUT = C.NOUT
            for tt in range(NT):
                t0 = tt * 128
                pst = pst_all[:, 4 * (tt % 2):4 * (tt % 2) + 4]
                b_pst = b_pst_all[tt % 2]
                o, bo = ot[tt % 2], b_ot[tt % 2]
                G, bG = Gt[tt % 2], b_G[tt % 2]
                S.dma("pool", o[:], D["O_s"][tt], writes=[bo])
                S.dma("pool", G[:], D["G_s"][tt], writes=[bG])
                S.dma("pool", hin[tt % 2][:], D["H1"][t0:t0 + 128, :], writes=[b_hin[tt % 2]])
                zb, b_zb = zbs[tt % 2], b_zbs[tt % 2]
                for hf in range(2):
                    pz, bpz = ps[hf + 6 * (tt % 2)], bps[hf + 6 * (tt % 2)]
                    pzb = pz[:].bitcast(BF16)
                    for j in range(8):
                        hp = hf * 8 + j
                        S.op("pe", lambda e: e.transpose(out=pzb[:, j * 128:(j + 1) * 128], in_=o[:, hp * 128:(hp + 1) * 128], identity=C.ident[:]), reads=[bo, bc], writes=[bpz])
                    h8 = slice(hf * 8, hf * 8 + 8)
                    S.op("dve", lambda e: e.tensor_tensor(out=zb[:, h8, :], in0=pzb.rearrange("p (a b) -> p a b", a=8), in1=G[:, h8, :], op=ALU.mult), reads=[bpz, bG], writes=[b_zb])
                pm = [ps[2 + 2 * (tt % 2)], ps[3 + 2 * (tt % 2)]]
                bpm = [bps[2 + 2 * (tt % 2)], bps[3 + 2 * (tt % 2)]]
                for nh in range(2):
                    for hp in range(16):
                        S.op("pe", lambda e: e.matmul(out=pm[nh][:], lhsT=zb[:, hp, :], rhs=wo[:, hp, nh * 512:(nh + 1) * 512], start=(hp == 0), stop=(hp == 15)), reads=[b_zb, b_w], writes=[bpm[nh]])
                S.op("pool", lambda e: e.memset(pst, 0.0), writes=[b_pst])
                for nh in range(2):
                    S.op("act", lambda e: e.activation(out=junk[:], in_=pm[nh][:], func=AF.Square, bias=cv[:, 2:3], scale=1.0, accum_out=pst[:, nh:nh + 1]), reads=[bpm[nh], bc, b_junk], writes=[b_pst])
                S.op("dve", lambda e: e.tensor_tensor(out=pst[:, 2:3], in0=pst[:, 0:1], in1=pst[:, 1:2], op=ALU.add), reads=[b_pst], writes=[b_pst])
                S.op("act", lambda e: e.activation(out=pst[:, 3:4], in_=pst[:, 2:3], func=AF.Sqrt, bias=cv[:, 0:1], scale=1.0 / 1024), reads=[b_pst, bc], writes=[b_pst])
                S.op("dve", lambda e: e.reciprocal(out=pst[:, 3:4], in_=pst[:, 3:4]), reads=[b_pst], writes=[b_pst])
                ho, bho = hout[tt % 2], b_hout[tt % 2]
                for nh in range(2):
                    cs_ = slice(nh * 512, (nh + 1) * 512)
                    S.op("dve", lambda e: e.scalar_tensor_tensor(out=ho[:, cs_], in0=pm[nh][:], scalar=pst[:, 3:4], in1=gpost[:, cs_], op0=ALU.mult, op1=ALU.mult), reads=[bpm[nh], b_pst, b_w], writes=[bho])
                S.op("dve", lambda e: e.tensor_tensor(out=ho[:], in0=ho[:], in1=hin[tt % 2][:], op=ALU.add), reads=[bho, b_hin[tt % 2]], writes=[bho])
                r0 = t0 - 16
                lo = max(0, -r0)
                hi = min(128, NOUT - r0)
                if hi > lo:
                    S.dma("sp", D["out"][r0 + lo:r0 + hi, :], ho[lo:hi, :], reads=[bho], writes=[Buf()])
            S.run()


N_CORES = 8
SEQ = 4096
N_META = 16
STOP_AFTER = None
NT_FULL = 33


class Ctx:
    pass


def host_l0(inp, b, NT):
    L = NT * 128
    hfull = np.concatenate([inp["meta_tokens"], inp["x"][b]], axis=0)
    h0 = np.zeros((L, 1024), np.float32)
    n = min(L, hfull.shape[0])
    h0[:n] = hfull[:n]
    mu = inp["rwkv_mu"][0]
    pp = np.zeros((128, 160), np.float32)
    pp[:, :48] = mu.reshape(6, 8, 128).transpose(2, 0, 1).reshape(128, 48)
    for j, nm in enumerate(["rwkv_w0", "rwkv_a0", "rwkv_k_k", "rwkv_k_a", "rwkv_r_k", "rwkv_ln_w", "rwkv_ln_b"]):
        pp[:, 48 + 16 * j: 48 + 16 * (j + 1)] = inp[nm][0].reshape(16, 128).T
    return {"h0": h0, "gpre0": inp["norm_pre"][0], "gpost0": inp["norm_post"][0], "pp0": pp,
            "w_in0": inp["rwkv_w_in"][0], "w2": inp["rwkv_w2"][0], "a2": inp["rwkv_a2"][0], "w_out0": inp["rwkv_w_out"][0]}


def host_l1(inp):
    w_in = inp["mla_w_in"][0]
    kr = w_in[:, 768:832]
    w_in1 = np.concatenate([w_in, kr[:, 32:64], kr[:, 0:32]], axis=1)
    pp1 = np.zeros((128, 8), np.float32)
    pp1[:, 0:4] = inp["mla_q_norm"][0].reshape(4, 128).T
    pp1[:, 4:6] = inp["mla_kv_norm"][0].reshape(2, 128).T
    wq = inp["mla_w_q_up"][0].reshape(512, 16, 192)
    wq_n = np.ascontiguousarray(wq[:, :, 0:128]).reshape(512, 2048)
    wq_r = np.ascontiguousarray(wq[:, :, 128:192]).reshape(512, 1024)
    wq_rs = np.ascontiguousarray(np.concatenate([wq[:, :, 160:192], wq[:, :, 128:160]], axis=2)).reshape(512, 1024)
    wkv = inp["mla_w_kv_up"][0].reshape(256, 16, 256)
    wkv_k = np.ascontiguousarray(wkv[:, :, 0:128]).reshape(256, 2048)
    wkv_v = np.ascontiguousarray(wkv[:, :, 128:256]).reshape(256, 2048)
    return {"w_in1": np.ascontiguousarray(w_in1), "pp1": pp1, "wq_n": wq_n, "wq_r": wq_r, "wq_rs": wq_rs,
            "wkv_k": wkv_k, "wkv_v": wkv_v, "w_out1": inp["mla_w_out"][0],
            "gpre1": inp["norm_pre"][1], "gpost1": inp["norm_post"][1]}


def build(NT, NOUT):
    nc = bass.Bass("TRN2", target_bir_lowering=False)
    L = NT * 128
    D = {}

    def din(name, shape, dt=F32):
        D[name] = nc.dram_tensor(name, shape, dt, kind="ExternalInput").ap()

    def dsc(name, shape, dt=F32):
        D[name] = nc.dram_tensor(name, shape, dt, kind="Internal").ap()
    din("h0", [L, 1024]); din("gpre0", [1024]); din("gpost0", [1024]); din("pp0", [128, 160])
    din("w_in0", [1024, 8320]); din("w2", [64, 2048]); din("a2", [64, 2048]); din("w_out0", [2048, 1024])
    din("w_in1", [1024, 2944]); din("pp1", [128, 8]); din("wq_n", [512, 2048]); din("wq_r", [512, 1024]); din("wq_rs", [512, 1024])
    din("wkv_k", [256, 2048]); din("wkv_v", [256, 2048]); din("w_out1", [2048, 1024]); din("gpre1", [1024]); din("gpost1", [1024])
    for nm in ("R_s", "K_s", "V_s"):
        dsc(nm, [NT, 128, 16, 128])
    dsc("G_s", [NT, 128, 16, 128], BF16); dsc("Z_s", [NT, 128, 16, 128], BF16)
    dsc("H1", [L, 1024])
    dsc("QN_s", [16, 128, L], BF16); dsc("QR_s", [16, 64, L], BF16); dsc("KN_s", [16, 128, L], BF16)
    dsc("V1_s", [NT, 128, 16, 129], BF16); dsc("O_s", [NT, 128, 2048], BF16)
    D["out"] = nc.dram_tensor("out", [NOUT, 1024], F32, kind="ExternalOutput").ap()
    with ExitStack() as st:
        C = Ctx()
        C.nc = nc; C.stack = st; C.S = Sched(nc, st); C.NT = NT; C.dram = D; C.dbg_barrier = 0; C.NOUT = NOUT
        C.stop_after = STOP_AFTER
        consts(C)
        C.S.run()
        try:
            layer0(C)
            with ExitStack() as mid:
                C.uT1 = mid.enter_context(nc.sbuf_tensor("uT1", [128, 8, L], BF16))
                C.b_uT1 = Buf("uT1")
                C.gpre1 = mid.enter_context(nc.sbuf_tensor("gpre1_x", [128, 1024], F32))
                C.b_gpre1 = Buf("gpre1")
                C.S.dma("sp", C.gpre1[:], D["gpre1"].partition_broadcast(128), writes=[C.b_gpre1])
                phase_x(C)
                layer1(C)
        except StopBuild:
            C.S.ops = {e: [] for e in C.S.ENGS}
        C.S.run()
    return nc


def kernel(**inputs):
    inp = {k: np.asarray(v) for k, v in inputs.items()}
    B = inp["x"].shape[0]
    nc = build(NT_FULL, SEQ)
    shared = host_l1(inp)
    in_maps = []
    for b in range(B):
        m = host_l0(inp, b, NT_FULL)
        m.update(shared)
        in_maps.append({k: np.ascontiguousarray(v, dtype=np.float32) for k, v in m.items()})
    res = run_bass_kernel_spmd(nc, in_maps, core_ids=list(range(B)))
    out = np.stack([np.asarray(res.results[b]["out"]) for b in range(B)], axis=0)
    return out.astype(np.float32)
```

```python
from contextlib import ExitStack
import math
import contextlib
import numpy as np
import concourse.bass as bass
import concourse.mybir as mybir
from concourse.bass_utils import run_bass_kernel_spmd

F32 = mybir.dt.float32
BF16 = mybir.dt.bfloat16
I32 = mybir.dt.int32
ALU = mybir.AluOpType
AF = mybir.ActivationFunctionType
AX = mybir.AxisListType


class Buf:
    __slots__ = ("name", "w", "r")

    def __init__(self, name=""):
        self.name = name
        self.w = None
        self.r = {}


class _Rec:
    def __init__(self):
        self.call = None

    def __getattr__(self, name):
        def f(*a, **k):
            self.call = (name, a, k)
            return self
        return f


class Sched:
    ENGS = ("pe", "act", "dve", "pool", "sp")
    EPOCH = 24000
    NSLOT = {"sp": 24, "pool": 12, "act": 8}
    STRICT = ("act", "dve", "pool")

    def __init__(self, nc, stack):
        self.nc = nc
        self.stack = stack
        self.ops = {e: [] for e in self.ENGS}
        self.cnt = {e: 0 for e in self.ENGS}
        self.sems = {}
        self.dcnt = {q: 0 for q in self.NSLOT}
        self.seen = {e: {} for e in self.ENGS}
        self.nwait = 0

    def sem(self, key):
        s = self.sems.get(key)
        if s is None:
            s = self.stack.enter_context(self.nc.semaphore("s_" + "_".join(str(k) for k in key)))
            self.sems[key] = s
        return s

    def _deps(self, eng, reads, writes, strict=False):
        toks = {}

        def add(tok, same_ok):
            if tok is None:
                return
            key, val = tok
            if same_ok and not strict and eng not in self.STRICT and key[0] == "e" and key[1] == eng:
                return
            if toks.get(key, -1) < val:
                toks[key] = val
        for b in reads:
            add(b.w, False)
        for b in writes:
            add(b.w, True)
            for k, v in b.r.items():
                add((k, v), True)
        out = []
        seen = self.seen[eng]
        for key, val in toks.items():
            if seen.get(key, -1) >= val:
                continue
            seen[key] = val
            out.append((key, val))
        return out

    def _mark(self, tok, reads, writes):
        key, val = tok
        for b in reads:
            if b.r.get(key, -1) < val:
                b.r[key] = val
        for b in writes:
            b.w = tok
            b.r = {}

    def op(self, eng, fn, reads=(), writes=()):
        waits = self._deps(eng, reads, writes)
        self.cnt[eng] += 1
        c = self.cnt[eng]
        key = ("e", eng, c // self.EPOCH)
        val = c % self.EPOCH
        if val == 0:
            self.cnt[eng] += 1
            c = self.cnt[eng]
            val = c % self.EPOCH
        self.sem(key)
        rec = _Rec()
        fn(rec)
        name, a, k = rec.call
        self.ops[eng].append((waits, (lambda e, name=name, a=a, k=k: getattr(e, name)(*a, **k)), key, 1))
        self._mark((key, val), reads, writes)
        self.nwait += len(waits)

    def dma(self, q, out, in_, reads=(), writes=(), **kw):
        n = self.NSLOT[q]
        idx = self.dcnt[q]
        self.dcnt[q] += 1
        slot = idx % n
        val = 16 * (idx // n + 1)
        key = ("d", q, slot)
        self.sem(key)
        waits = self._deps(q, reads, writes, strict=True)
        if val > 16:
            seen = self.seen[q]
            if seen.get(key, -1) < val - 16:
                seen[key] = val - 16
                waits.append((key, val - 16))
        self.ops[q].append((waits, (lambda e, out=out, in_=in_, kw=kw: e.dma_start(out=out, in_=in_, **kw)), key, 16))
        self._mark((key, val), reads, writes)
        self.nwait += len(waits)

    def _finals(self):
        waits = []
        for q, n in self.NSLOT.items():
            tot = self.dcnt[q]
            for slot in range(min(n, tot)):
                uses = (tot - slot + n - 1) // n
                waits.append((("d", q, slot), 16 * uses))
        for e in ("pe", "act", "dve", "pool"):
            c = self.cnt[e]
            if c > 0:
                waits.append((("e", e, c // self.EPOCH), c % self.EPOCH))
        return waits

    def run(self, barrier=True):
        nc = self.nc
        sched = self
        finals = self._finals() if barrier else []
        plan = {}
        for name in self.ENGS:
            seen = self.seen[name]
            w = []
            for k, v in finals:
                if seen.get(k, -1) < v:
                    seen[k] = v
                    w.append((k, v))
            plan[name] = w
        ops = self.ops
        self.ops = {e: [] for e in self.ENGS}

        def replay(name, eng):
            for waits, fn, key, inc in ops[name]:
                for k, v in waits:
                    eng.wait_ge(sched.sems[k], v)
                ins = fn(eng)
                ins.then_inc(sched.sems[key], inc)
            for k, v in plan[name]:
                eng.wait_ge(sched.sems[k], v)

        with nc.Block() as block:
            @block.tensor
            def _(e):
                replay("pe", e)

            @block.scalar
            def _(e):
                replay("act", e)

            @block.vector
            def _(e):
                replay("dve", e)

            @block.gpsimd
            def _(e):
                replay("pool", e)

            @block.sync
            def _(e):
                replay("sp", e)


class StopBuild(Exception):
    pass


C0 = float(np.exp(-0.5))
GN_EPS = 64e-5
RMS_EPS = 1e-6


def supertiles(NT):
    out = []
    t = 0
    while t < NT:
        n = min(4, NT - t)
        out.append((t, n))
        t += n
    return out


def consts(C):
    nc, S = C.nc, C.S
    sb = lambda n, s, d: C.stack.enter_context(nc.sbuf_tensor(n, s, d))
    C.b_const = Buf("const")
    bc = C.b_const
    C.identf = sb("identf", [128, 128], F32)
    C.ident = sb("ident", [128, 128], BF16)
    C.blkf = sb("blkf", [128, 128], F32)
    C.blk = sb("blk", [128, 128], BF16)
    C.mU = sb("mU", [128, 128], F32)
    C.mUi = sb("mUi", [128, 128], F32)
    C.mL = sb("mL", [128, 128], F32)
    C.mask4 = sb("mask4", [128, 4, 128], F32)
    C.cvals = sb("cvals", [128, 8], F32)
    C.ones_s = sb("ones_s", [128, 4, 128], F32)
    for t, pat_op, base in ((C.mU, ALU.is_gt, 0), (C.mUi, ALU.is_ge, 0), (C.mL, ALU.is_gt, 0)):
        pass
    S.op("pool", lambda e: e.memset(C.identf[:], 1.0), writes=[bc])
    S.op("pool", lambda e: e.affine_select(out=C.identf[:], in_=C.identf[:], pattern=[[-1, 128]],
                                           compare_op=ALU.is_equal, fill=0.0, base=0, channel_multiplier=1),
         reads=[bc], writes=[bc])
    S.op("pool", lambda e: e.tensor_copy(out=C.ident[:], in_=C.identf[:]), reads=[bc], writes=[bc])
    S.op("pool", lambda e: e.memset(C.mU[:], 1.0), writes=[bc])
    S.op("pool", lambda e: e.affine_select(out=C.mU[:], in_=C.mU[:], pattern=[[1, 128]],
                                           compare_op=ALU.is_gt, fill=0.0, base=0, channel_multiplier=-1),
         reads=[bc], writes=[bc])
    S.op("pool", lambda e: e.memset(C.mUi[:], 1.0), writes=[bc])
    S.op("pool", lambda e: e.affine_select(out=C.mUi[:], in_=C.mUi[:], pattern=[[1, 128]],
                                           compare_op=ALU.is_ge, fill=0.0, base=0, channel_multiplier=-1),
         reads=[bc], writes=[bc])
    S.op("pool", lambda e: e.memset(C.mL[:], 1.0), writes=[bc])
    S.op("pool", lambda e: e.affine_select(out=C.mL[:], in_=C.mL[:], pattern=[[-1, 128]],
                                           compare_op=ALU.is_gt, fill=0.0, base=0, channel_multiplier=1),
         reads=[bc], writes=[bc])
    for k in range(4):
        src = C.mU if k % 2 == 0 else C.mUi
        S.op("pool", lambda e, k=k, src=src: e.tensor_copy(out=C.mask4[:, k, :], in_=src[:]), reads=[bc], writes=[bc])
    S.op("pool", lambda e: e.memset(C.blkf[:], 0.0), writes=[bc])
    S.op("pool", lambda e: e.memset(C.blkf[0:64, 0:64], 1.0), writes=[bc])
    S.op("pool", lambda e: e.memset(C.blkf[64:128, 64:128], 1.0), writes=[bc])
    S.op("pool", lambda e: e.tensor_copy(out=C.blk[:], in_=C.blkf[:]), reads=[bc], writes=[bc])
    S.op("pool", lambda e: e.memset(C.cvals[:, 0:1], RMS_EPS), writes=[bc])
    S.op("pool", lambda e: e.memset(C.cvals[:, 1:2], GN_EPS), writes=[bc])
    S.op("pool", lambda e: e.memset(C.cvals[:, 2:3], 0.0), writes=[bc])
    S.op("pool", lambda e: e.memset(C.cvals[:, 3:4], 1.0), writes=[bc])
    S.op("pool", lambda e: e.memset(C.ones_s[:], 1.0), writes=[bc])
    S.op("pool", lambda e: e.memset(C.ones_s[:, :, 0:1], 0.0), writes=[bc])
    C.psum = [C.stack.enter_context(nc.psum_tensor(f"ps{i}", [128, 512], F32)) for i in range(8)]
    C.b_ps = [Buf(f"ps{i}") for i in range(8)]


def rmsnorm_stats(C, src_ap, ncols, ss_ap, rs_ap, junk_ap, reads, b_stat):
    S = C.S
    S.op("act", lambda e: e.activation(out=junk_ap, in_=src_ap, func=AF.Square, bias=C.cvals[:, 2:3], scale=1.0, accum_out=ss_ap),
         reads=reads + [C.b_const], writes=[b_stat])
    S.op("act", lambda e: e.activation(out=rs_ap, in_=ss_ap, func=AF.Sqrt, bias=C.cvals[:, 0:1], scale=1.0 / ncols),
         reads=[b_stat, C.b_const], writes=[b_stat])
    S.op("dve", lambda e: e.reciprocal(out=rs_ap, in_=rs_ap), reads=[b_stat], writes=[b_stat])


def layer0(C):
    nc, S = C.nc, C.S
    NT = C.NT
    L = NT * 128
    STS = supertiles(NT)
    ps, bps = C.psum, C.b_ps
    D = C.dram
    with ExitStack() as l0:
        sb0 = lambda n, s, d: l0.enter_context(nc.sbuf_tensor(n, s, d))
        pp = sb0("pp0_sb", [128, 160], F32)
        b_pp = Buf("pp")
        tw = sb0("tw", [64, L], BF16)
        al = sb0("al", [64, L], BF16)
        b_tw, b_al = Buf("tw"), Buf("al")
        S.dma("sp", pp[:], D["pp0"], writes=[b_pp])

        with ExitStack() as pa:
            sa = lambda n, s, d: pa.enter_context(nc.sbuf_tensor(n, s, d))
            uT = sa("uT", [128, 8, 1 + L], BF16)
            b_uT = Buf("uT")
            gpre = sa("gpre", [128, 1024], F32)
            b_g = Buf("gpre")
            S.dma("sp", gpre[:], D["gpre0"].partition_broadcast(128), writes=[b_g])
            S.op("pool", lambda e: e.memset(uT[:, :, 0:1], 0.0), writes=[b_uT])
            with ExitStack() as pA:
                sA = lambda n, s, d: pA.enter_context(nc.sbuf_tensor(n, s, d))
                xin = [sA(f"xin{i}", [128, 1024], F32) for i in range(2)]
                b_xin = [Buf() for _ in range(2)]
                ub = [sA(f"ub{i}", [128, 1024], BF16) for i in range(2)]
                b_ub = [Buf() for _ in range(2)]
                junk = sA("junkA", [128, 1024], BF16)
                b_junk = Buf()
                st_ss = sA("ssA", [128, NT], F32)
                st_rs = sA("rsA", [128, NT], F32)
                b_st = [Buf() for _ in range(NT)]
                for tt in range(NT):
                    x, bx = xin[tt % 2], b_xin[tt % 2]
                    u, bu = ub[tt % 2], b_ub[tt % 2]
                    S.dma("sp", x[:], D["h0"][tt * 128:(tt + 1) * 128, :], writes=[bx])
                    S.op("pool", lambda e, tt=tt: e.memset(st_ss[:, tt:tt + 1], 0.0), writes=[b_st[tt]])
                    rmsnorm_stats(C, x[:], 1024, st_ss[:, tt:tt + 1], st_rs[:, tt:tt + 1], junk[:], [bx, b_junk], b_st[tt])
                    S.op("dve", lambda e, x=x, u=u, tt=tt: e.scalar_tensor_tensor(
                        out=u[:], in0=x[:], scalar=st_rs[:, tt:tt + 1], in1=gpre[:], op0=ALU.mult, op1=ALU.mult),
                        reads=[bx, b_st[tt], b_g], writes=[bu])
                    pb = ps[tt % 2][:].bitcast(BF16)
                    for c in range(8):
                        S.op("pe", lambda e, c=c, u=u, pb=pb: e.transpose(out=pb[:, c * 128:(c + 1) * 128], in_=u[:, c * 128:(c + 1) * 128], identity=C.ident[:]),
                             reads=[bu, C.b_const], writes=[bps[tt % 2]])
                    eng = "act" if tt % 2 == 0 else "dve"
                    dst = uT[:, :, 1 + tt * 128: 1 + (tt + 1) * 128]
                    src = pb.rearrange("p (c k) -> p c k", c=8)
                    if eng == "act":
                        S.op("act", lambda e, dst=dst, src=src: e.copy(out=dst, in_=src), reads=[bps[tt % 2]], writes=[b_uT])
                    else:
                        S.op("dve", lambda e, dst=dst, src=src: e.tensor_copy(out=dst, in_=src), reads=[bps[tt % 2]], writes=[b_uT])
                S.run()

            with ExitStack() as pBC:
                sB = lambda n, s, d: pBC.enter_context(nc.sbuf_tensor(n, s, d))
                dxt = sB("dxt", [128, 8, 512], F32)
                tmpl = sB("tmpl", [128, 8, 512], F32)
                b_dx, b_tmpl = Buf(), Buf()
                xg = [sB(f"xg{i}", [128, 8, 512], BF16) for i in range(2)]
                b_xg = [Buf() for _ in range(2)]
                lerp_cnt = [0]

                def lerp(g, t0, n):
                    i = lerp_cnt[0] % 2
                    lerp_cnt[0] += 1
                    cur = uT[:, :, 1 + t0: 1 + t0 + n]
                    prev = uT[:, :, t0: t0 + n]
                    mu_bc = pp[:, g * 8:(g + 1) * 8].unsqueeze(2).to_broadcast([128, 8, n])
                    S.op("dve", lambda e: e.tensor_tensor(out=dxt[:, :, :n], in0=prev, in1=cur, op=ALU.subtract),
                         reads=[b_uT], writes=[b_dx])
                    S.op("pool", lambda e: e.tensor_tensor(out=tmpl[:, :, :n], in0=dxt[:, :, :n], in1=mu_bc, op=ALU.mult),
                         reads=[b_dx, b_pp], writes=[b_tmpl])
                    S.op("dve", lambda e: e.tensor_tensor(out=xg[i][:, :, :n], in0=tmpl[:, :, :n], in1=cur, op=ALU.add),
                         reads=[b_tmpl, b_uT], writes=[b_xg[i]])
                    return xg[i], b_xg[i]

                with ExitStack() as pB:
                    sBb = lambda n, s, d: pB.enter_context(nc.sbuf_tensor(n, s, d))
                    wlf = sBb("wlf", [128, 8, 128], F32)
                    wl = sBb("wl", [128, 8, 128], BF16)
                    b_wl = Buf()
                    S.dma("sp", wlf[:], D["w_in0"][:, 8192:8320].rearrange("(c p) n -> p c n", p=128), writes=[b_wl])
                    S.op("pool", lambda e: e.tensor_copy(out=wl[:], in_=wlf[:]), reads=[b_wl], writes=[b_wl])
                    for si, (tt0, nt) in enumerate(STS):
                        t0, n = tt0 * 128, nt * 128
                        xw, bxw = lerp(4, t0, n)
                        xa, bxa = lerp(5, t0, n)
                        pw, pa_ = ps[2 + (si % 2) * 2], ps[3 + (si % 2) * 2]
                        bpw, bpa = bps[2 + (si % 2) * 2], bps[3 + (si % 2) * 2]
                        for c in range(8):
                            S.op("pe", lambda e, c=c, xw=xw, pw=pw: e.matmul(out=pw[0:64, :n], lhsT=wl[:, c, 0:64], rhs=xw[:, c, :n], start=(c == 0), stop=(c == 7)),
                                 reads=[b_wl, bxw], writes=[bpw])
                        for c in range(8):
                            S.op("pe", lambda e, c=c, xa=xa, pa_=pa_: e.matmul(out=pa_[0:64, :n], lhsT=wl[:, c, 64:128], rhs=xa[:, c, :n], start=(c == 0), stop=(c == 7)),
                                 reads=[b_wl, bxa], writes=[bpa])
                        S.op("act", lambda e, pw=pw, t0=t0, n=n: e.activation(out=tw[:, t0:t0 + n], in_=pw[0:64, :n], func=AF.Tanh, bias=C.cvals[0:64, 2:3], scale=1.0),
                             reads=[bpw, C.b_const], writes=[b_tw])
                        S.op("dve", lambda e, pa_=pa_, t0=t0, n=n: e.tensor_copy(out=al[:, t0:t0 + n], in_=pa_[0:64, :n]),
                             reads=[bpa], writes=[b_al])
                    S.run()

                with ExitStack() as pC:
                    sC = lambda n, s, d: pC.enter_context(nc.sbuf_tensor(n, s, d))
                    wst = [sC(f"wst{i}", [128, 2048], F32) for i in range(2)]
                    b_wst = [Buf() for _ in range(2)]
                    wg = sC("wg", [128, 8, 2048], BF16)
                    b_wg = Buf()
                    stg = [sC(f"stg{i}", [128, 512], F32) for i in range(4)]
                    b_stg = [Buf() for _ in range(4)]
                    k = 0
                    for g in range(4):
                        for c in range(8):
                            S.dma("sp", wst[c % 2][:], D["w_in0"][c * 128:(c + 1) * 128, g * 2048:(g + 1) * 2048], writes=[b_wst[c % 2]])
                            if c % 2 == 0:
                                S.op("pool", lambda e, c=c: e.tensor_copy(out=wg[:, c, :], in_=wst[c % 2][:]), reads=[b_wst[c % 2]], writes=[b_wg])
                            else:
                                S.op("act", lambda e, c=c: e.copy(out=wg[:, c, :], in_=wst[c % 2][:]), reads=[b_wst[c % 2]], writes=[b_wg])
                        scr = D[("R_s", "K_s", "V_s", "G_s")[g]]
                        for si, (tt0, nt) in enumerate(STS):
                            t0, n = tt0 * 128, nt * 128
                            x_, bx_ = lerp(g, t0, n)
                            for oc in range(16):
                                pi = k % 4
                                p_, bp_ = ps[4 + pi], bps[4 + pi]
                                s_, bs_ = stg[pi], b_stg[pi]
                                k += 1
                                for c in range(8):
                                    S.op("pe", lambda e, c=c, oc=oc, x_=x_, p_=p_: e.matmul(out=p_[:, :n], lhsT=wg[:, c, oc * 128:(oc + 1) * 128], rhs=x_[:, c, :n], start=(c == 0), stop=(c == 7)),
                                         reads=[b_wg, bx_], writes=[bp_])
                                if g == 3:
                                    sv = s_[:].bitcast(BF16)[:, :n]
                                    S.op("act", lambda e, sv=sv, p_=p_: e.activation(out=sv, in_=p_[:, :n], func=AF.Silu, bias=C.cvals[:, 2:3], scale=1.0),
                                         reads=[bp_, C.b_const], writes=[bs_])
                                else:
                                    sv = s_[:, :n]
                                    if oc % 2 == 0:
                                        S.op("act", lambda e, sv=sv, p_=p_: e.copy(out=sv, in_=p_[:, :n]), reads=[bp_], writes=[bs_])
                                    else:
                                        S.op("dve", lambda e, sv=sv, p_=p_: e.tensor_copy(out=sv, in_=p_[:, :n]), reads=[bp_], writes=[bs_])
                                S.dma("sp", scr[tt0:tt0 + nt, :, oc, :].rearrange("t p k -> p t k"),
                                      sv.rearrange("p (t k) -> p t k", t=nt), reads=[bs_], writes=[Buf()])
                    S.run()
        C.l0_keep = (pp, b_pp, tw, al, b_tw, b_al)
        if getattr(C, "stop_after", None) == "l0c":
            raise StopBuild()
        layer0_wkv(C)
        if getattr(C, "stop_after", None) == "l0":
            raise StopBuild()


def layer0_wkv(C):
    nc, S = C.nc, C.S
    NT = C.NT
    L = NT * 128
    ps, bps = C.psum, C.b_ps
    D = C.dram
    pp, b_pp, tw, al, b_tw, b_al = C.l0_keep
    PW0, PA0, PKK, PKA, PRK, PLW, PLB = [48 + 16 * j for j in range(7)]
    with ExitStack() as pd:
        sb = lambda n, s, d: pd.enter_context(nc.sbuf_tensor(n, s, d))
        w2b = sb("w2b", [64, 2048], BF16)
        a2b = sb("a2b", [64, 2048], BF16)
        b_w = Buf("wres")
        with ExitStack() as pl:
            wst = [pl.enter_context(nc.sbuf_tensor(f"wst2_{i}", [128, 2048], F32)) for i in range(2)]
            b_wst = [Buf(), Buf()]
            S.dma("sp", wst[0][0:64, :], D["w2"], writes=[b_wst[0]])
            S.op("pool", lambda e: e.tensor_copy(out=w2b[:], in_=wst[0][0:64, :]), reads=[b_wst[0]], writes=[b_w])
            S.dma("sp", wst[1][0:64, :], D["a2"], writes=[b_wst[1]])
            S.op("pool", lambda e: e.tensor_copy(out=a2b[:], in_=wst[1][0:64, :]), reads=[b_wst[1]], writes=[b_w])
            S.run()
        ST = sb("ST", [128, 16, 64], F32)
        STb = sb("STb", [128, 16, 64], BF16)
        b_ST = [Buf(f"ST{g}") for g in range(4)]
        S.op("pool", lambda e: e.memset(ST[:], 0.0), writes=b_ST)
        S.op("pool", lambda e: e.memset(STb[:], 0.0), writes=b_ST)
        ARt = sb("ARt", [128, 16, 2, 128], BF16)
        BKt = sb("BKt", [128, 16, 2, 128], BF16)
        BhT = sb("BhT", [128, 2048], BF16)
        KhT = sb("KhT", [128, 2048], BF16)
        VT = sb("VT", [128, 2048], BF16)
        bonT = [sb(f"bonT{i}", [128, 16, 128], BF16) for i in range(2)]
        wc = sb("wc", [128, 16], F32)
        b_op = [Buf(f"op{q}") for q in range(4)]
        Rq = [sb(f"Rq{i}", [128, 4, 128], F32) for i in range(2)]
        Kq = [sb(f"Kq{i}", [128, 4, 128], F32) for i in range(2)]
        Vq = [sb(f"Vq{i}", [128, 4, 128], F32) for i in range(2)]
        b_in = [[Buf(), Buf(), Buf()], [Buf(), Buf(), Buf()]]
        NTMP = 12
        tmp = [sb(f"tmpD{i}", [128, 4, 128], F32) for i in range(NTMP)]
        b_tmp = [Buf() for _ in range(NTMP)]
        tmpb = [sb(f"tmpDb{i}", [128, 4, 128], BF16) for i in range(4)]
        b_tmpb = [Buf() for _ in range(4)]
        M3 = [sb(f"M3_{i}", [128, 16, 3, 128], BF16) for i in range(2)]
        Tm = [sb(f"Tm_{i}", [128, 16, 128], BF16) for i in range(2)]
        b_M3 = [[Buf() for _ in range(4)] for _ in range(2)]
        Ab = [sb(f"A_{i}", [128, 16, 128], BF16) for i in range(2)]
        ATb = [sb(f"AT_{i}", [128, 16, 128], BF16) for i in range(2)]
        Tb = [sb(f"T_{i}", [128, 16, 128], BF16) for i in range(2)]
        b_A = [[Buf() for _ in range(4)] for _ in range(2)]
        b_AT = [[Buf() for _ in range(4)] for _ in range(2)]
        b_T = [[Buf() for _ in range(4)] for _ in range(2)]
        XTb = [sb(f"XTb{i}", [128, 512], BF16) for i in range(2)]
        UTb = [sb(f"UTb{i}", [128, 512], BF16) for i in range(2)]
        b_X, b_U = [Buf(), Buf()], [Buf(), Buf()]
        yf = sb("yf", [128, 8, 64], F32)
        ysq = sb("ysq", [128, 8, 64], F32)
        b_y, b_ysq = Buf(), Buf()
        gst = sb("gst", [128, 6, 32], F32)
        b_gst = Buf()
        yn = sb("yn", [128, 2048], BF16)
        b_yn = [Buf() for _ in range(4)]
        Gt = sb("Gt0", [128, 16, 128], BF16)
        b_G = Buf()
        zf = sb("zf", [128, 8, 128], F32)
        b_zf = Buf()
        zb = [sb(f"zbw{i}", [128, 16, 128], BF16) for i in range(2)]
        b_zb = [Buf(), Buf()]

        def bc4(col0, q):
            return pp[:, col0 + 4 * q: col0 + 4 * q + 4].unsqueeze(2).to_broadcast([128, 4, 128])

        tctr = [0]

        def T_(n=1):
            i = tctr[0] % NTMP
            tctr[0] += 1
            return tmp[i], b_tmp[i]

        rot = [0]

        def bank():
            i = rot[0] % 6
            rot[0] += 1
            return ps[i], bps[i]

        def D_quarter(tt, q):
            t0 = tt * 128
            ib = (tt * 4 + q) % 2
            R, K, V, bi = Rq[ib], Kq[ib], Vq[ib], b_in[ib]
            S.dma("sp", R[:], D["R_s"][tt, :, 4 * q:4 * q + 4, :], writes=[bi[0]])
            S.dma("sp", K[:], D["K_s"][tt, :, 4 * q:4 * q + 4, :], writes=[bi[1]])
            S.dma("sp", V[:], D["V_s"][tt, :, 4 * q:4 * q + 4, :], writes=[bi[2]])
            bo = b_op[q]
            pw, bpw = bank()
            pa_, bpa = bank()
            for j in range(4):
                hp = 4 * q + j
                S.op("pe", lambda e, j=j, hp=hp: e.matmul(out=pw[:, j * 128:(j + 1) * 128], lhsT=w2b[:, hp * 128:(hp + 1) * 128], rhs=tw[:, t0:t0 + 128], start=True, stop=True),
                     reads=[b_w, b_tw], writes=[bpw])
            for j in range(4):
                hp = 4 * q + j
                S.op("pe", lambda e, j=j, hp=hp: e.matmul(out=pa_[:, j * 128:(j + 1) * 128], lhsT=a2b[:, hp * 128:(hp + 1) * 128], rhs=al[:, t0:t0 + 128], start=True, stop=True),
                     reads=[b_w, b_al], writes=[bpa])
            sw, bsw = T_()
            sa_, bsa = T_()
            for j in range(4):
                hp = 4 * q + j
                S.op("act", lambda e, j=j, hp=hp, sw=sw: e.activation(out=sw[:, j, :], in_=pw[:, j * 128:(j + 1) * 128], func=AF.Sigmoid, bias=pp[:, PW0 + hp:PW0 + hp + 1], scale=1.0),
                     reads=[bpw, b_pp], writes=[bsw])
                S.op("act", lambda e, j=j, hp=hp, sa_=sa_: e.activation(out=sa_[:, j, :], in_=pa_[:, j * 128:(j + 1) * 128], func=AF.Sigmoid, bias=pp[:, PA0 + hp:PA0 + hp + 1], scale=1.0),
                     reads=[bpa, b_pp], writes=[bsa])
            yield
            cs, bcs = T_()
            S.op("dve", lambda e, cs=cs, sw=sw: e.tensor_tensor_scan(out=cs[:].rearrange("p a b -> p (a b)"), data0=C.ones_s[:].rearrange("p a b -> p (a b)"), data1=sw[:].rearrange("p a b -> p (a b)"), initial=0.0, op0=ALU.mult, op1=ALU.add),
                 reads=[bsw, C.b_const], writes=[bcs])
            cp_, bcp = T_()
            S.op("pool", lambda e, cp_=cp_, cs=cs, sw=sw: e.tensor_tensor(out=cp_[:], in0=cs[:], in1=sw[:], op=ALU.subtract), reads=[bcs, bsw], writes=[bcp])
            ce, bce = T_()
            S.op("pool", lambda e, ce=ce, cs=cs: e.tensor_tensor(out=ce[:], in0=cs[:, :, 127:128].to_broadcast([128, 4, 128]), in1=cs[:], op=ALU.subtract), reads=[bcs], writes=[bce])
            epos, bepos = T_()
            eneg, beneg = T_()
            S.op("act", lambda e, epos=epos, cs=cs: e.activation(out=epos[:], in_=cs[:], func=AF.Exp, bias=C.cvals[:, 2:3], scale=-C0), reads=[bcs, C.b_const], writes=[bepos])
            S.op("act", lambda e, eneg=eneg, cs=cs: e.activation(out=eneg[:], in_=cs[:], func=AF.Exp, bias=C.cvals[:, 2:3], scale=C0), reads=[bcs, C.b_const], writes=[beneg])
            S.op("act", lambda e, cp_=cp_: e.activation(out=cp_[:], in_=cp_[:], func=AF.Exp, bias=C.cvals[:, 2:3], scale=-C0), reads=[bcp, C.b_const], writes=[bcp])
            S.op("act", lambda e, ce=ce: e.activation(out=ce[:], in_=ce[:], func=AF.Exp, bias=C.cvals[:, 2:3], scale=-C0), reads=[bce, C.b_const], writes=[bce])
            S.op("dve", lambda e, epos=epos, q=q: e.tensor_copy(out=wc[:, 4 * q:4 * q + 4], in_=epos[:, :, 127]), reads=[bepos], writes=[bo])
            yield
            kkn, bkkn = T_()
            S.op("pool", lambda e, kkn=kkn, K=K, q=q: e.tensor_tensor(out=kkn[:], in0=K[:], in1=bc4(PKK, q), op=ALU.mult), reads=[*bi, b_pp], writes=[bkkn])
            sq, bsq = tmpb[0], b_tmpb[0]
            S.op("dve", lambda e, kkn=kkn: e.tensor_tensor(out=sq[:], in0=kkn[:], in1=kkn[:], op=ALU.mult), reads=[bkkn], writes=[bsq])
            yield
            pn, bpn = bank()
            S.op("pe", lambda e: e.matmul(out=pn[:], lhsT=C.blk[:], rhs=sq[:].rearrange("p a b -> p (a b)"), start=True, stop=True), reads=[bsq, C.b_const], writes=[bpn])
            rn, brn = T_()
            S.op("act", lambda e, rn=rn: e.activation(out=rn[:].rearrange("p a b -> p (a b)"), in_=pn[:], func=AF.Sqrt, bias=C.cvals[:, 2:3], scale=1.0), reads=[bpn, C.b_const], writes=[brn])
            S.op("dve", lambda e, rn=rn: e.tensor_scalar(out=rn[:], in0=rn[:], scalar1=1e-12, scalar2=None, op0=ALU.max), reads=[brn], writes=[brn])
            S.op("dve", lambda e, rn=rn: e.reciprocal(out=rn[:], in_=rn[:]), reads=[brn], writes=[brn])
            S.op("pool", lambda e, kkn=kkn, rn=rn: e.tensor_tensor(out=kkn[:], in0=kkn[:], in1=rn[:], op=ALU.mult), reads=[bkkn, brn], writes=[bkkn])
            kk, bkk = kkn, bkkn
            yield
            bb, bbb = rn, brn
            S.op("dve", lambda e, bb=bb, kk=kk, sa_=sa_: e.tensor_tensor(out=bb[:], in0=kk[:], in1=sa_[:], op=ALU.mult), reads=[bkk, bsa, brn], writes=[bbb])
            t1, bt1 = T_()
            S.op("dve", lambda e, t1=t1, sa_=sa_, q=q: e.scalar_tensor_tensor(out=t1[:], in0=sa_[:], scalar=-1.0, in1=bc4(PKA, q), op0=ALU.add, op1=ALU.mult), reads=[bsa, b_pp], writes=[bt1])
            kp, bkp = t1, bt1
            S.op("dve", lambda e, t1=t1, K=K: e.scalar_tensor_tensor(out=t1[:], in0=t1[:], scalar=1.0, in1=K[:], op0=ALU.add, op1=ALU.mult), reads=[bt1, *bi], writes=[bt1])
            hs = slice(4 * q, 4 * q + 4)
            yield
            S.op("dve", lambda e, kk=kk, cp_=cp_, hs=hs: e.scalar_tensor_tensor(out=ARt[:, hs, 0, :], in0=kk[:], scalar=-1.0, in1=cp_[:], op0=ALU.mult, op1=ALU.mult), reads=[bkk, bcp], writes=[bo])
            S.op("pool", lambda e, R=R, epos=epos, hs=hs: e.tensor_tensor(out=ARt[:, hs, 1, :], in0=R[:], in1=epos[:], op=ALU.mult), reads=[*bi, bepos], writes=[bo])
            S.op("dve", lambda e, bb=bb, eneg=eneg, hs=hs: e.tensor_tensor(out=BKt[:, hs, 0, :], in0=bb[:], in1=eneg[:], op=ALU.mult), reads=[bbb, beneg], writes=[bo])
            S.op("pool", lambda e, kp=kp, eneg=eneg, hs=hs: e.tensor_tensor(out=BKt[:, hs, 1, :], in0=kp[:], in1=eneg[:], op=ALU.mult), reads=[bkp, beneg], writes=[bo])
            yield
            bh, bbh = tmpb[1], b_tmpb[1]
            kh, bkh = tmpb[2], b_tmpb[2]
            vb, bvb = tmpb[3], b_tmpb[3]
            S.op("dve", lambda e, bb=bb, ce=ce: e.tensor_tensor(out=bh[:], in0=bb[:], in1=ce[:], op=ALU.mult), reads=[bbb, bce], writes=[bbh])
            S.op("pool", lambda e, kp=kp, ce=ce: e.tensor_tensor(out=kh[:], in0=kp[:], in1=ce[:], op=ALU.mult), reads=[bkp, bce], writes=[bkh])
            S.op("act", lambda e, V=V: e.copy(out=vb[:], in_=V[:]), reads=[*bi], writes=[bvb])
            yield
            ptr, bptr = bank()
            ptb = ptr[:].bitcast(BF16)
            for j in range(4):
                S.op("pe", lambda e, j=j: e.transpose(out=ptb[:, j * 128:(j + 1) * 128], in_=bh[:, j, :], identity=C.ident[:]), reads=[bbh, C.b_const], writes=[bptr])
            for j in range(4):
                S.op("pe", lambda e, j=j: e.transpose(out=ptb[:, 512 + j * 128:512 + (j + 1) * 128], in_=kh[:, j, :], identity=C.ident[:]), reads=[bkh, C.b_const], writes=[bptr])
            S.op("act", lambda e, q=q: e.copy(out=BhT[:, q * 512:(q + 1) * 512], in_=ptb[:, 0:512]), reads=[bptr], writes=[bo, bptr])
            S.op("dve", lambda e, q=q: e.tensor_copy(out=KhT[:, q * 512:(q + 1) * 512], in_=ptb[:, 512:1024]), reads=[bptr], writes=[bo])
            ptr2, bptr2 = bank()
            ptb2 = ptr2[:].bitcast(BF16)
            for j in range(4):
                S.op("pe", lambda e, j=j: e.transpose(out=ptb2[:, j * 128:(j + 1) * 128], in_=vb[:, j, :], identity=C.ident[:]), reads=[bvb, C.b_const], writes=[bptr2])
            S.op("act", lambda e, q=q: e.copy(out=VT[:, q * 512:(q + 1) * 512], in_=ptb2[:, 0:512]), reads=[bptr2], writes=[bo])
            yield
            rk, brk = T_()
            S.op("pool", lambda e, rk=rk, R=R, kp=kp: e.tensor_tensor(out=rk[:], in0=R[:], in1=kp[:], op=ALU.mult), reads=[*bi, bkp], writes=[brk])
            rkb, brkb = tmpb[0], b_tmpb[0]
            S.op("dve", lambda e, rk=rk, q=q: e.tensor_tensor(out=rkb[:], in0=rk[:], in1=bc4(PRK, q), op=ALU.mult), reads=[brk, b_pp], writes=[brkb])
            yield
            pbn, bpbn = bank()
            S.op("pe", lambda e: e.matmul(out=pbn[:], lhsT=C.blk[:], rhs=rkb[:].rearrange("p a b -> p (a b)"), start=True, stop=True), reads=[brkb, C.b_const], writes=[bpbn])
            S.op("dve", lambda e, V=V, hs=hs: e.tensor_tensor(out=bonT[tt % 2][:, hs, :], in0=pbn[:].rearrange("p (a b) -> p a b", a=4), in1=V[:], op=ALU.mult), reads=[bpbn, *bi], writes=[bo])


        def G_pass(tt, p):
            gi = p
            for qd in range(4):
                bo = b_op[2 * p + qd // 2]
                for hh in range(4):
                    l16 = qd * 4 + hh
                    h = 16 * p + l16
                    hp, par = h // 2, h % 2
                    pr = slice(par * 64, par * 64 + 64)
                    pg, bpg = bank()
                    arhs = ARt[pr, hp, :, :].rearrange("p a b -> p (a b)")
                    S.op("pe", lambda e: e.matmul(out=pg[:, 0:256], lhsT=BKt[pr, hp, 0, :], rhs=arhs, start=True, stop=True), reads=[bo], writes=[bpg])
                    S.op("pe", lambda e: e.matmul(out=pg[:, 256:512], lhsT=BKt[pr, hp, 1, :], rhs=arhs, start=True, stop=True), reads=[bo], writes=[bpg])
                    S.op("dve", lambda e: e.tensor_tensor(out=Ab[0][:, l16, :], in0=pg[:, 0:128], in1=C.mU[:], op=ALU.mult), reads=[bpg, C.b_const], writes=[b_A[0][qd]])
                    S.op("dve", lambda e: e.tensor_tensor(out=M3[gi][:, l16, :, :], in0=pg[:, 128:512].rearrange("p (a b) -> p a b", a=3), in1=C.mask4[:, 1:4, :], op=ALU.mult), reads=[bpg, C.b_const], writes=[b_M3[gi][qd]])
                q4 = slice(qd * 4, qd * 4 + 4)
                if qd % 2 == 1:
                    qp = qd // 2
                    ATv = ATb[0][:].rearrange("p (h two) t -> p h two t", two=2)
                    for par in range(2):
                        pt_, bpt_ = bank()
                        pr = slice(par * 64, par * 64 + 64)
                        for k4 in range(4):
                            h = 16 * p + qp * 8 + 2 * k4 + par
                            hp = h // 2
                            S.op("pe", lambda e: e.matmul(out=pt_[:, k4 * 128:(k4 + 1) * 128], lhsT=ARt[pr, hp, 0, :], rhs=BKt[pr, hp, 0, :], start=True, stop=True), reads=[b_op[2 * p + qp]], writes=[bpt_])
                        S.op("dve", lambda e: e.tensor_tensor(out=ATv[:, qp * 4:(qp + 1) * 4, par, :], in0=pt_[:].rearrange("p (a b) -> p a b", a=4), in1=C.mL[:].unsqueeze(1).to_broadcast([128, 4, 128]), op=ALU.mult), reads=[bpt_, C.b_const], writes=[b_AT[0][qd - 1], b_AT[0][qd]])
                S.op("pool", lambda e: e.tensor_tensor(out=Tb[0][:, q4, :], in0=Ab[0][:, q4, :], in1=C.ident[:].unsqueeze(1).to_broadcast([128, 4, 128]), op=ALU.add), reads=[b_A[0][qd], C.b_const], writes=[b_T[0][qd]])
                yield

        def N_level(p, lv):
            gi = p
            i0, i1 = (lv - 1) % 2, lv % 2
            last = (lv == 6)
            pend = []
            for qd in range(4):
                q4 = slice(qd * 4, qd * 4 + 4)
                pAT, bpAT = bank()
                for hh in range(4):
                    l16 = qd * 4 + hh
                    S.op("pe", lambda e: e.matmul(out=pAT[:, hh * 128:(hh + 1) * 128], lhsT=Ab[i0][:, l16, :], rhs=ATb[i0][:, l16, :], start=True, stop=True), reads=[b_A[i0][qd], b_AT[i0][qd]], writes=[bpAT])
                S.op("act", lambda e: e.copy(out=ATb[i1][:, q4, :].rearrange("p a b -> p (a b)"), in_=pAT[:]), reads=[bpAT], writes=[b_AT[i1][qd]])
                if not last:
                    pA_, bpA = bank()
                    for hh in range(4):
                        l16 = qd * 4 + hh
                        S.op("pe", lambda e: e.matmul(out=pA_[:, hh * 128:(hh + 1) * 128], lhsT=ATb[i0][:, l16, :], rhs=Ab[i0][:, l16, :], start=True, stop=True), reads=[b_A[i0][qd], b_AT[i0][qd]], writes=[bpA])
                    S.op("act", lambda e: e.copy(out=Ab[i1][:, q4, :].rearrange("p a b -> p (a b)"), in_=pA_[:]), reads=[bpA], writes=[b_A[i1][qd]])
            for qd in range(4):
                q4 = slice(qd * 4, qd * 4 + 4)
                pT, bpT = bank()
                for hh in range(4):
                    l16 = qd * 4 + hh
                    S.op("pe", lambda e: e.matmul(out=pT[:, hh * 128:(hh + 1) * 128], lhsT=ATb[i1][:, l16, :], rhs=Tb[i0][:, l16, :], start=True, stop=True), reads=[b_AT[i1][qd], b_T[i0][qd]], writes=[bpT])
                if last:
                    S.op("dve", lambda e: e.tensor_tensor(out=Tm[gi][:, q4, :], in0=pT[:].rearrange("p (a b) -> p a b", a=4), in1=Tb[i0][:, q4, :], op=ALU.add), reads=[bpT, b_T[i0][qd]], writes=[b_M3[gi][qd]])
                else:
                    S.op("dve", lambda e: e.tensor_tensor(out=Tb[i1][:, q4, :], in0=pT[:].rearrange("p (a b) -> p a b", a=4), in1=Tb[i0][:, q4, :], op=ALU.add), reads=[bpT, b_T[i0][qd]], writes=[b_T[i1][qd]])

        def state_stages(tt, p):
            gi = p
            t0 = tt * 128
            stages = []

            def stage_X():
                for g2 in range(2):
                    grp = 2 * p + g2
                    bo = b_op[grp]
                    bm = [b_M3[gi][2 * g2], b_M3[gi][2 * g2 + 1]]
                    pX, bpX = ps[6 + g2], bps[6 + g2]
                    for l8 in range(8):
                        h = grp * 8 + l8
                        l16 = g2 * 8 + l8
                        hp, par = h // 2, h % 2
                        pr = slice(par * 64, par * 64 + 64)
                        S.op("pe", lambda e: e.matmul(out=pX[:, l8 * 64:(l8 + 1) * 64], lhsT=ARt[pr, hp, 0, :], rhs=STb[pr, hp, :], start=True, stop=False), reads=[bo, b_ST[grp]], writes=[bpX])
                        S.op("pe", lambda e: e.matmul(out=pX[:, l8 * 64:(l8 + 1) * 64], lhsT=M3[gi][:, l16, 1, :], rhs=VT[:, h * 64:(h + 1) * 64], start=False, stop=True), reads=bm + [bo], writes=[bpX])
                    S.op("act", lambda e: e.copy(out=XTb[g2][:], in_=pX[:]), reads=[bpX], writes=[b_X[g2]])

            def stage_U():
                for g2 in range(2):
                    bm = [b_M3[gi][2 * g2], b_M3[gi][2 * g2 + 1]]
                    pU, bpU = ps[6 + g2], bps[6 + g2]
                    for l8 in range(8):
                        l16 = g2 * 8 + l8
                        S.op("pe", lambda e: e.matmul(out=pU[:, l8 * 64:(l8 + 1) * 64], lhsT=Tm[gi][:, l16, :], rhs=XTb[g2][:, l8 * 64:(l8 + 1) * 64], start=True, stop=True), reads=bm + [b_X[g2]], writes=[bpU])
                    S.op("act", lambda e: e.copy(out=UTb[g2][:], in_=pU[:]), reads=[bpU], writes=[b_U[g2]])

            def stage_S():
                for g2 in range(2):
                    grp = 2 * p + g2
                    bo = b_op[grp]
                    pS, bpS = ps[6 + g2], bps[6 + g2]
                    for l8 in range(8):
                        h = grp * 8 + l8
                        hp = h // 2
                        o = pS[:, l8 * 64:(l8 + 1) * 64]
                        S.op("pe", lambda e: e.matmul(out=o, lhsT=BhT[:, hp * 128:(hp + 1) * 128], rhs=UTb[g2][:, l8 * 64:(l8 + 1) * 64], start=True, stop=False), reads=[bo, b_U[g2]], writes=[bpS])
                        S.op("pe", lambda e: e.matmul(out=o, lhsT=KhT[:, hp * 128:(hp + 1) * 128], rhs=VT[:, h * 64:(h + 1) * 64], start=False, stop=True), reads=[bo], writes=[bpS])
                    hps = slice(grp * 4, grp * 4 + 4)
                    S.op("pool", lambda e: e.tensor_tensor(out=ST[:, hps, :], in0=ST[:, hps, :], in1=wc[:, hps].unsqueeze(2).to_broadcast([128, 4, 64]), op=ALU.mult), reads=[b_ST[grp], bo], writes=[b_ST[grp]])
                    for par in range(2):
                        pr = slice(par * 64, par * 64 + 64)
                        srcp = pS[pr, :].rearrange("p (a two b) -> p a two b", a=4, two=2)[:, :, par, :]
                        S.op("dve", lambda e: e.tensor_tensor(out=ST[pr, hps, :], in0=ST[pr, hps, :], in1=srcp, op=ALU.add), reads=[b_ST[grp], bpS], writes=[b_ST[grp]])
                    S.op("pool", lambda e: e.tensor_copy(out=STb[:, hps, :], in_=ST[:, hps, :]), reads=[b_ST[grp]], writes=[b_ST[grp]])

            def stage_Y():
                for g2 in range(2):
                    grp = 2 * p + g2
                    bo = b_op[grp]
                    bm = [b_M3[gi][2 * g2], b_M3[gi][2 * g2 + 1]]
                    pY, bpY = ps[6 + g2], bps[6 + g2]
                    for l8 in range(8):
                        h = grp * 8 + l8
                        l16 = g2 * 8 + l8
                        hp, par = h // 2, h % 2
                        pr = slice(par * 64, par * 64 + 64)
                        o = pY[:, l8 * 64:(l8 + 1) * 64]
                        S.op("pe", lambda e: e.matmul(out=o, lhsT=ARt[pr, hp, 1, :], rhs=STb[pr, hp, :], start=True, stop=False), reads=[bo, b_ST[grp]], writes=[bpY])
                        S.op("pe", lambda e: e.matmul(out=o, lhsT=M3[gi][:, l16, 0, :], rhs=UTb[g2][:, l8 * 64:(l8 + 1) * 64], start=False, stop=False), reads=bm + [b_U[g2]], writes=[bpY])
                        S.op("pe", lambda e: e.matmul(out=o, lhsT=M3[gi][:, l16, 2, :], rhs=VT[:, h * 64:(h + 1) * 64], start=False, stop=True), reads=bm + [bo], writes=[bpY])
                    S.op("act", lambda e: e.copy(out=yf[:].rearrange("p a b -> p (a b)"), in_=pY[:]), reads=[bpY], writes=[b_y])
                    S.op("pool", lambda e: e.tensor_tensor(out=ysq[:], in0=yf[:], in1=yf[:], op=ALU.mult), reads=[b_y], writes=[b_ysq])
                    g8 = slice(grp * 8, grp * 8 + 8)
                    S.op("dve", lambda e: e.tensor_reduce(out=gst[:, 0, g8], in_=yf[:], axis=AX.X, op=ALU.add), reads=[b_y], writes=[b_gst])
                    S.op("dve", lambda e: e.tensor_reduce(out=gst[:, 1, g8], in_=ysq[:], axis=AX.X, op=ALU.add), reads=[b_ysq], writes=[b_gst])
                    S.op("dve", lambda e: e.tensor_scalar(out=gst[:, 2, g8], in0=gst[:, 0, g8], scalar1=1.0 / 64, scalar2=None, op0=ALU.mult), reads=[b_gst], writes=[b_gst])
                    S.op("dve", lambda e: e.tensor_tensor(out=gst[:, 3, g8], in0=gst[:, 2, g8], in1=gst[:, 2, g8], op=ALU.mult), reads=[b_gst], writes=[b_gst])
                    S.op("dve", lambda e: e.scalar_tensor_tensor(out=gst[:, 4, g8], in0=gst[:, 1, g8], scalar=1.0 / 64, in1=gst[:, 3, g8], op0=ALU.mult, op1=ALU.subtract), reads=[b_gst], writes=[b_gst])
                    S.op("act", lambda e: e.activation(out=gst[:, 5, g8], in_=gst[:, 4, g8], func=AF.Sqrt, bias=C.cvals[:, 1:2], scale=1.0), reads=[b_gst, C.b_const], writes=[b_gst])
                    S.op("dve", lambda e: e.reciprocal(out=gst[:, 5, g8], in_=gst[:, 5, g8]), reads=[b_gst], writes=[b_gst])
                    S.op("dve", lambda e: e.tensor_tensor(out=yf[:], in0=yf[:], in1=gst[:, 2, g8].unsqueeze(2).to_broadcast([128, 8, 64]), op=ALU.subtract), reads=[b_y, b_gst], writes=[b_y])
                    S.op("pool", lambda e: e.tensor_tensor(out=yn[:, grp * 512:(grp + 1) * 512].rearrange("p (a b) -> p a b", a=8), in0=yf[:], in1=gst[:, 5, g8].unsqueeze(2).to_broadcast([128, 8, 64]), op=ALU.mult), reads=[b_y, b_gst], writes=[b_yn[grp]])

            return [stage_X, stage_U, stage_Y, stage_S]

        def F2a(tt):
            S.dma("sp", Gt[:], D["G_s"][tt], writes=[b_G])
            z_, bz_ = zb[tt % 2], b_zb[tt % 2]
            for hf in range(2):
                pz, bpz = bank()
                pzb = pz[:].bitcast(BF16)
                for j in range(8):
                    hp = hf * 8 + j
                    S.op("pe", lambda e: e.transpose(out=pzb[:, j * 128:(j + 1) * 128], in_=yn[:, hp * 128:(hp + 1) * 128], identity=C.ident[:]), reads=[b_yn[hp // 4], C.b_const], writes=[bpz])
                h8 = slice(hf * 8, hf * 8 + 8)
                lw_bc = pp[:, PLW + hf * 8:PLW + hf * 8 + 8].unsqueeze(2).to_broadcast([128, 8, 128])
                lb_bc = pp[:, PLB + hf * 8:PLB + hf * 8 + 8].unsqueeze(2).to_broadcast([128, 8, 128])
                S.op("dve", lambda e: e.tensor_tensor(out=zf[:], in0=pzb.rearrange("p (a b) -> p a b", a=8), in1=lw_bc, op=ALU.mult), reads=[bpz, b_pp], writes=[b_zf])
                S.op("pool", lambda e: e.tensor_tensor(out=zf[:], in0=zf[:], in1=lb_bc, op=ALU.add), reads=[b_zf, b_pp], writes=[b_zf])
                S.op("pool", lambda e: e.tensor_tensor(out=zf[:], in0=zf[:], in1=bonT[tt % 2][:, h8, :], op=ALU.add), reads=[b_zf] + b_op, writes=[b_zf])
                S.op("pool", lambda e: e.tensor_tensor(out=z_[:, h8, :], in0=zf[:], in1=Gt[:, h8, :], op=ALU.mult), reads=[b_zf, b_G], writes=[bz_])
            S.dma("pool", D["Z_s"][tt], z_[:], reads=[bz_], writes=[Buf()])

        passes = [(tt, p) for tt in range(NT) for p in range(2)]
        for g in (D_quarter(0, 0), D_quarter(0, 1)):
            for _ in g:
                pass
        for _ in G_pass(0, 0):
            pass
        for i, (tt, p) in enumerate(passes):
            gens = []
            if i + 1 < len(passes):
                tn, pn_ = passes[i + 1]
                gens = [D_quarter(tn, 2 * pn_), D_quarter(tn, 2 * pn_ + 1)]
            plan = {1: (0, 4), 2: (0, 3), 3: (0, 3), 4: (1, 4), 5: (1, 3), 6: (1, 3)}
            for lv in range(1, 7):
                N_level(p, lv)
                if gens:
                    gi_, nch = plan[lv]
                    for _ in range(nch):
                        next(gens[gi_], None)
            for g in gens:
                for _ in g:
                    pass
            gnext = G_pass(*passes[i + 1]) if i + 1 < len(passes) else iter(())
            for st_fn in state_stages(tt, p):
                st_fn()
                next(gnext, None)
            for _ in gnext:
                pass
            if p == 1:
                F2a(tt)
        S.run()


def phase_x(C):
    nc, S = C.nc, C.S
    NT = C.NT
    ps, bps = C.psum, C.b_ps
    D = C.dram
    with ExitStack() as px:
        sb = lambda n, s, d: px.enter_context(nc.sbuf_tensor(n, s, d))
        wo = sb("wo", [128, 16, 1024], BF16)
        gpost = sb("gpost", [128, 1024], F32)
        b_w = Buf("wres2")
        with ExitStack() as pl:
            wst = [pl.enter_context(nc.sbuf_tensor(f"wst3_{i}", [128, 2048], F32)) for i in range(2)]
            b_wst = [Buf(), Buf()]
            for c in range(8):
                S.dma("sp", wst[c % 2][:].rearrange("p (j n) -> p j n", j=2),
                      D["w_out0"][c * 256:(c + 1) * 256, :].rearrange("(j p) n -> p j n", p=128), writes=[b_wst[c % 2]])
                S.op("pool", lambda e: e.tensor_copy(out=wo[:, 2 * c:2 * c + 2, :], in_=wst[c % 2][:].rearrange("p (j n) -> p j n", j=2)),
                     reads=[b_wst[c % 2]], writes=[b_w])
            S.dma("sp", gpost[:], D["gpost0"].partition_broadcast(128), writes=[b_w])
            S.run()
        zin = [sb(f"zin{i}", [128, 16, 128], BF16) for i in range(2)]
        b_zin = [Buf(), Buf()]
        hin = [sb(f"hinX{i}", [128, 1024], F32) for i in range(2)]
        b_hin = [Buf(), Buf()]
        hout = [sb(f"houtX{i}", [128, 1024], F32) for i in range(2)]
        b_hout = [Buf(), Buf()]
        junk = sb("junkF", [128, 512], BF16)
        b_junk = Buf()
        pst = sb("pst", [128, 2, 4], F32)
        b_pst = [Buf(), Buf()]
        if getattr(C, "uT1", None) is not None:
            ub1 = [sb(f"ubX{i}", [128, 1024], BF16) for i in range(2)]
            b_ub1 = [Buf(), Buf()]
            junk1 = sb("junkX1", [128, 1024], BF16)
            b_junk1 = Buf()
            st1 = sb("ssX1", [128, NT], F32)
            rs1 = sb("rsX1", [128, NT], F32)
            b_st1 = [Buf() for _ in range(NT)]
        for tt in range(NT):
            t0 = tt * 128
            i2 = tt % 2
            S.dma("pool", zin[i2][:], D["Z_s"][tt], writes=[b_zin[i2]])
            S.dma("pool", hin[i2][:], D["h0"][t0:t0 + 128, :], writes=[b_hin[i2]])
            pm = [ps[2 * i2], ps[2 * i2 + 1]]
            bpm = [bps[2 * i2], bps[2 * i2 + 1]]
            for nh in range(2):
                for hp in range(16):
                    S.op("pe", lambda e: e.matmul(out=pm[nh][:], lhsT=zin[i2][:, hp, :], rhs=wo[:, hp, nh * 512:(nh + 1) * 512], start=(hp == 0), stop=(hp == 15)), reads=[b_zin[i2], b_w], writes=[bpm[nh]])
            st_ = pst[:, i2, :]
            S.op("pool", lambda e: e.memset(st_, 0.0), writes=[b_pst[i2]])
            for nh in range(2):
                S.op("act", lambda e: e.activation(out=junk[:], in_=pm[nh][:], func=AF.Square, bias=C.cvals[:, 2:3], scale=1.0, accum_out=st_[:, nh:nh + 1]), reads=[bpm[nh], C.b_const, b_junk], writes=[b_pst[i2]])
            S.op("dve", lambda e: e.tensor_tensor(out=st_[:, 2:3], in0=st_[:, 0:1], in1=st_[:, 1:2], op=ALU.add), reads=[b_pst[i2]], writes=[b_pst[i2]])
            S.op("act", lambda e: e.activation(out=st_[:, 3:4], in_=st_[:, 2:3], func=AF.Sqrt, bias=C.cvals[:, 0:1], scale=1.0 / 1024), reads=[b_pst[i2], C.b_const], writes=[b_pst[i2]])
            S.op("dve", lambda e: e.reciprocal(out=st_[:, 3:4], in_=st_[:, 3:4]), reads=[b_pst[i2]], writes=[b_pst[i2]])
            ho, bho = hout[i2], b_hout[i2]
            for nh in range(2):
                cs_ = slice(nh * 512, (nh + 1) * 512)
                S.op("dve", lambda e: e.scalar_tensor_tensor(out=ho[:, cs_], in0=pm[nh][:], scalar=st_[:, 3:4], in1=gpost[:, cs_], op0=ALU.mult, op1=ALU.mult), reads=[bpm[nh], b_pst[i2], b_w], writes=[bho])
            S.op("dve", lambda e: e.tensor_tensor(out=ho[:], in0=ho[:], in1=hin[i2][:], op=ALU.add), reads=[bho, b_hin[i2]], writes=[bho])
            S.dma("sp", D["H1"][t0:t0 + 128, :], ho[:], reads=[bho], writes=[Buf()])
            if getattr(C, "uT1", None) is not None:
                uT1, b_uT1, gpre1 = C.uT1, C.b_uT1, C.gpre1
                S.op("pool", lambda e: e.memset(st1[:, tt:tt + 1], 0.0), writes=[b_st1[tt]])
                rmsnorm_stats(C, ho[:], 1024, st1[:, tt:tt + 1], rs1[:, tt:tt + 1], junk1[:], [bho, b_junk1], b_st1[tt])
                u1, bu1 = ub1[i2], b_ub1[i2]
                S.op("dve", lambda e: e.scalar_tensor_tensor(out=u1[:], in0=ho[:], scalar=rs1[:, tt:tt + 1], in1=gpre1[:], op0=ALU.mult, op1=ALU.mult), reads=[bho, b_st1[tt], C.b_gpre1], writes=[bu1])
                pb = ps[4 + i2][:].bitcast(BF16)
                for c in range(8):
                    S.op("pe", lambda e: e.transpose(out=pb[:, c * 128:(c + 1) * 128], in_=u1[:, c * 128:(c + 1) * 128], identity=C.ident[:]), reads=[bu1, C.b_const], writes=[bps[4 + i2]])
                S.op("act", lambda e: e.copy(out=uT1[:, :, t0:t0 + 128], in_=pb.rearrange("p (c k) -> p c k", c=8)), reads=[bps[4 + i2]], writes=[b_uT1])
        S.run()


SCALE = 192.0 ** -0.5
TWO_PI = 2.0 * math.pi


def layer1(C):
    nc, S = C.nc, C.S
    NT = C.NT
    L = NT * 128
    STS = supertiles(NT)
    ps, bps = C.psum, C.b_ps
    D = C.dram
    cv = C.cvals
    bc = C.b_const
    with ExitStack() as l1:
        sb1 = lambda n, s, d: l1.enter_context(nc.sbuf_tensor(n, s, d))
        pp = sb1("pp1_sb", [128, 8], F32)
        b_pp = Buf("pp1")
        S.dma("sp", pp[:], D["pp1"], writes=[b_pp])
        KR = sb1("KR", [128, L], BF16)
        b_KR = Buf("KR")
        mx = sb1("mx", [128, 8], F32)
        b_mx = Buf("mx")
        S.op("pool", lambda e: e.memset(mx[:], 0.0), writes=[b_mx])
        onesb = sb1("onesb", [128, 128], BF16)
        S.op("pool", lambda e: e.memset(onesb[:], 1.0), writes=[bc])
        fr = sb1("fr", [64, 2], F32)
        fri = sb1("fri", [64, 2], I32)
        S.op("pool", lambda e: e.iota(out=fri[:, 0:1], pattern=[[0, 1]], base=0, channel_multiplier=1), writes=[bc])
        S.op("dve", lambda e: e.tensor_single_scalar(out=fri[:, 1:2], in_=fri[:, 0:1], scalar=31, op=ALU.bitwise_and), reads=[bc], writes=[bc])
        S.op("pool", lambda e: e.tensor_copy(out=fr[:, 0:1], in_=fri[:, 1:2]), reads=[bc], writes=[bc])
        S.op("act", lambda e: e.activation(out=fr[:, 1:2], in_=fr[:, 0:1], func=AF.Exp, bias=cv[0:64, 2:3], scale=-math.log(10000.0) / 32.0), reads=[bc], writes=[bc])
        S.run()

        with ExitStack() as pa:
            sa = lambda n, s, d: pa.enter_context(nc.sbuf_tensor(n, s, d))
            fusedA = getattr(C, "uT1", None) is not None
            if fusedA:
                uT, b_uT = C.uT1, C.b_uT1
            else:
                uT = sa("uT1", [128, 8, L], BF16)
                b_uT = Buf("uT1")
            with ExitStack() as pA:
                if fusedA:
                    NTA = 0
                else:
                    NTA = NT
                sA = lambda n, s, d: pA.enter_context(nc.sbuf_tensor(n, s, d))
                gpre = sA("gpre1_sb", [128, 1024], F32)
                b_g = Buf()
                S.dma("sp", gpre[:], D["gpre1"].partition_broadcast(128), writes=[b_g])
                xin = [sA(f"xin1_{i}", [128, 1024], F32) for i in range(2)]
                b_xin = [Buf() for _ in range(2)]
                ub = [sA(f"ub1_{i}", [128, 1024], BF16) for i in range(2)]
                b_ub = [Buf() for _ in range(2)]
                junk = sA("junkA1", [128, 1024], BF16)
                b_junk = Buf()
                st_ss = sA("ssA1", [128, NT], F32)
                st_rs = sA("rsA1", [128, NT], F32)
                b_st = [Buf() for _ in range(NT)]
                for tt in range(NTA):
                    x, bx = xin[tt % 2], b_xin[tt % 2]
                    u, bu = ub[tt % 2], b_ub[tt % 2]
                    S.dma("sp", x[:], D["H1"][tt * 128:(tt + 1) * 128, :], writes=[bx])
                    S.op("pool", lambda e: e.memset(st_ss[:, tt:tt + 1], 0.0), writes=[b_st[tt]])
                    rmsnorm_stats(C, x[:], 1024, st_ss[:, tt:tt + 1], st_rs[:, tt:tt + 1], junk[:], [bx, b_junk], b_st[tt])
                    S.op("dve", lambda e: e.scalar_tensor_tensor(out=u[:], in0=x[:], scalar=st_rs[:, tt:tt + 1], in1=gpre[:], op0=ALU.mult, op1=ALU.mult),
                         reads=[bx, b_st[tt], b_g], writes=[bu])
                    pb = ps[tt % 2][:].bitcast(BF16)
                    for c in range(8):
                        S.op("pe", lambda e: e.transpose(out=pb[:, c * 128:(c + 1) * 128], in_=u[:, c * 128:(c + 1) * 128], identity=C.ident[:]),
                             reads=[bu, bc], writes=[bps[tt % 2]])
                    dst = uT[:, :, tt * 128:(tt + 1) * 128]
                    src = pb.rearrange("p (c k) -> p c k", c=8)
                    if tt % 2 == 0:
                        S.op("act", lambda e: e.copy(out=dst, in_=src), reads=[bps[tt % 2]], writes=[b_uT])
                    else:
                        S.op("dve", lambda e: e.tensor_copy(out=dst, in_=src), reads=[bps[tt % 2]], writes=[b_uT])
                S.run()

            with ExitStack() as pB:
                sB = lambda n, s, d: pB.enter_context(nc.sbuf_tensor(n, s, d))
                wst = [sB(f"wstB{i}", [128, 2048], F32) for i in range(2)]
                b_wst = [Buf(), Buf()]
                wcnt = [0]

                def load_w(dst_ap, src_ap, rows, cols, bdst):
                    i = wcnt[0] % 2
                    wcnt[0] += 1
                    S.dma("sp", wst[i][0:rows, 0:cols], src_ap, writes=[b_wst[i]])
                    if i == 0:
                        S.op("pool", lambda e: e.tensor_copy(out=dst_ap, in_=wst[i][0:rows, 0:cols]), reads=[b_wst[i]], writes=[bdst])
                    else:
                        S.op("act", lambda e: e.copy(out=dst_ap, in_=wst[i][0:rows, 0:cols]), reads=[b_wst[i]], writes=[bdst])

                stg = [sB(f"stgB{i}", [128, 512], BF16) for i in range(4)]
                b_stg = [Buf() for _ in range(4)]
                sqb_all = [sB(f"sqB{i}", [128, 512], BF16) for i in range(4)]
                b_sqb_all = [Buf() for _ in range(4)]
                sqb, b_sqb = sqb_all[0:2], b_sqb_all[0:2]
                nmc = [0]
                red = sB("redB", [128, 4], F32)
                b_red = Buf()
                cnt = [0]

                pend_norm = []

                def flush_norm():
                    while pend_norm:
                        a_, c_ = pend_norm.pop(0)
                        norm_max(a_, c_)

                def norm_max(sq_list, col):
                    nmc[0] += 1
                    pn, bpn = ps[6 + nmc[0] % 2], bps[6 + nmc[0] % 2]
                    n = sq_list[0][0].shape[-1]
                    for i, (ap, K, b) in enumerate(sq_list):
                        S.op("pe", lambda e: e.matmul(out=pn[:, :n], lhsT=onesb[0:K, :], rhs=ap, start=(i == 0), stop=(i == len(sq_list) - 1)), reads=[b, bc], writes=[bpn])
                    S.op("dve", lambda e: e.tensor_reduce(out=red[:, 0:1], in_=pn[:, :n], axis=AX.X, op=ALU.max), reads=[bpn], writes=[b_red])
                    S.op("dve", lambda e: e.tensor_tensor(out=mx[:, col:col + 1], in0=mx[:, col:col + 1], in1=red[:, 0:1], op=ALU.max), reads=[b_red, b_mx], writes=[b_mx])

                def feat_rmsnorm(src, bsrc, nch, n, gcol0, dst, bdst, nfeat):
                    pn, bpn = ps[6], bps[6]
                    for c in range(nch):
                        sq, bsq = sqb[c % 2], b_sqb[c % 2]
                        S.op("act", lambda e: e.activation(out=sq[:, :n], in_=src[:, c, :n], func=AF.Square, bias=cv[:, 2:3], scale=1.0), reads=[bsrc, bc], writes=[bsq])
                        S.op("pe", lambda e: e.matmul(out=pn[:, :n], lhsT=onesb[:], rhs=sq[:, :n], start=(c == 0), stop=(c == nch - 1)), reads=[bsq, bc], writes=[bpn])
                    rs, brs = rstd, b_rstd
                    S.op("act", lambda e: e.activation(out=rs[:, :n], in_=pn[:, :n], func=AF.Sqrt, bias=cv[:, 0:1], scale=1.0 / nfeat), reads=[bpn, bc], writes=[brs])
                    S.op("dve", lambda e: e.reciprocal(out=rs[:, :n], in_=rs[:, :n]), reads=[brs], writes=[brs])
                    for c in range(nch):
                        S.op("dve", lambda e: e.scalar_tensor_tensor(out=dst[:, c, :n], in0=src[:, c, :n], scalar=pp[:, gcol0 + c:gcol0 + c + 1], in1=rs[:, :n], op0=ALU.mult, op1=ALU.mult),
                             reads=[bsrc, brs, b_pp], writes=[bdst])

                rstd = sB("rstdB", [128, 512], F32)
                b_rstd = Buf()
                cosT = sB("cosT", [64, 512], F32)
                sinT = sB("sinT", [64, 512], F32)
                b_tab = Buf()
                angi = sB("angi", [64, 512], I32)
                angf = sB("angf", [64, 512], F32)
                angn = sB("angn", [64, 512], F32)
                angq = sB("angq", [64, 512], I32)

                def rope_tables(t0, n):
                    S.op("pool", lambda e: e.iota(out=angi[:, :n], pattern=[[1, n]], base=t0, channel_multiplier=0), writes=[b_tab])
                    S.op("pool", lambda e: e.tensor_copy(out=angf[:, :n], in_=angi[:, :n]), reads=[b_tab], writes=[b_tab])
                    S.op("dve", lambda e: e.tensor_scalar(out=angf[:, :n], in0=angf[:, :n], scalar1=fr[:, 1:2], scalar2=1.0 / TWO_PI, op0=ALU.mult, op1=ALU.mult), reads=[b_tab, bc], writes=[b_tab])
                    for which, dst in ((0, sinT), (1, cosT)):
                        if which == 1:
                            S.op("dve", lambda e: e.tensor_scalar(out=angf[:, :n], in0=angf[:, :n], scalar1=0.25, scalar2=None, op0=ALU.add), reads=[b_tab], writes=[b_tab])
                        S.op("dve", lambda e: e.tensor_copy(out=angq[:, :n], in_=angf[:, :n]), reads=[b_tab], writes=[b_tab])
                        S.op("dve", lambda e: e.tensor_copy(out=angn[:, :n], in_=angq[:, :n]), reads=[b_tab], writes=[b_tab])
                        S.op("dve", lambda e: e.tensor_tensor(out=angn[:, :n], in0=angf[:, :n], in1=angn[:, :n], op=ALU.subtract), reads=[b_tab], writes=[b_tab])
                        S.op("act", lambda e: e.activation(out=dst[:, :n], in_=angn[:, :n], func=AF.Sin, bias=cv[0:64, 2:3], scale=TWO_PI), reads=[b_tab, bc], writes=[b_tab])
                    S.op("dve", lambda e: e.tensor_scalar(out=sinT[0:32, :n], in0=sinT[0:32, :n], scalar1=-1.0, scalar2=None, op0=ALU.mult), reads=[b_tab], writes=[b_tab])

                rtmp = [sB(f"rtmp{i}", [64, 512], F32) for i in range(2)]
                b_rtmp = [Buf(), Buf()]

                def rope_rot(p_main, b_main, p_sw, b_sw, n, dst_ap, bdst):
                    S.op("dve", lambda e: e.tensor_tensor(out=rtmp[0][:, :n], in0=p_main, in1=cosT[:, :n], op=ALU.mult), reads=[b_main, b_tab], writes=[b_rtmp[0]])
                    S.op("dve", lambda e: e.tensor_tensor(out=rtmp[1][:, :n], in0=p_sw, in1=sinT[:, :n], op=ALU.mult), reads=[b_sw, b_tab], writes=[b_rtmp[1]])
                    S.op("dve", lambda e: e.tensor_tensor(out=dst_ap, in0=rtmp[0][:, :n], in1=rtmp[1][:, :n], op=ALU.add), reads=[b_rtmp[0], b_rtmp[1]], writes=[bdst])

                with ExitStack() as pB1:
                    s1 = lambda n, s, d: pB1.enter_context(nc.sbuf_tensor(n, s, d))
                    w1q = s1("w1q", [128, 8, 512], BF16)
                    wqn = s1("wqn", [128, 4, 2048], BF16)
                    wqr = s1("wqr", [128, 4, 1024], BF16)
                    wqrs = s1("wqrs", [128, 4, 1024], BF16)
                    b_wq = Buf()
                    for c in range(8):
                        load_w(w1q[:, c, :], D["w_in1"][c * 128:(c + 1) * 128, 0:512], 128, 512, b_wq)
                    for c in range(4):
                        load_w(wqn[:, c, :], D["wq_n"][c * 128:(c + 1) * 128, :], 128, 2048, b_wq)
                        load_w(wqr[:, c, :], D["wq_r"][c * 128:(c + 1) * 128, :], 128, 1024, b_wq)
                        load_w(wqrs[:, c, :], D["wq_rs"][c * 128:(c + 1) * 128, :], 128, 1024, b_wq)
                    cq = s1("cq", [128, 4, 512], F32)
                    b_cq = Buf()
                    qn = s1("qn", [128, 4, 512], BF16)
                    b_qn = Buf()
                    for si, (tt0, nt) in enumerate(STS):
                        t0, n = tt0 * 128, nt * 128
                        rope_tables(t0, n)
                        for c4 in range(4):
                            p_, bp_ = ps[c4 % 2], bps[c4 % 2]
                            for c in range(8):
                                S.op("pe", lambda e: e.matmul(out=p_[:, :n], lhsT=w1q[:, c, c4 * 128:(c4 + 1) * 128], rhs=uT[:, c, t0:t0 + n], start=(c == 0), stop=(c == 7)), reads=[b_wq, b_uT], writes=[bp_])
                            S.op("act", lambda e: e.copy(out=cq[:, c4, :n], in_=p_[:, :n]), reads=[bp_], writes=[b_cq])
                        feat_rmsnorm(cq, b_cq, 4, n, 0, qn, b_qn, 512)
                        for h in range(16):
                            k = cnt[0]
                            cnt[0] += 1
                            sqb, b_sqb = sqb_all[2 * (k % 2):2 * (k % 2) + 2], b_sqb_all[2 * (k % 2):2 * (k % 2) + 2]
                            p_, bp_ = ps[k % 2], bps[k % 2]
                            s_, bs_ = stg[k % 4], b_stg[k % 4]
                            for c in range(4):
                                S.op("pe", lambda e: e.matmul(out=p_[:, :n], lhsT=wqn[:, c, h * 128:(h + 1) * 128], rhs=qn[:, c, :n], start=(c == 0), stop=(c == 3)), reads=[b_wq, b_qn], writes=[bp_])
                            S.op("act", lambda e: e.copy(out=s_[:, :n], in_=p_[:, :n]), reads=[bp_], writes=[bs_])
                            S.dma("sp", D["QN_s"][h, :, t0:t0 + n], s_[:, :n], reads=[bs_], writes=[Buf()])
                            pr, bpr = ps[2 + (k % 2) * 2], bps[2 + (k % 2) * 2]
                            prs, bprs = ps[3 + (k % 2) * 2], bps[3 + (k % 2) * 2]
                            for c in range(4):
                                S.op("pe", lambda e: e.matmul(out=pr[0:64, :n], lhsT=wqr[:, c, h * 64:(h + 1) * 64], rhs=qn[:, c, :n], start=(c == 0), stop=(c == 3)), reads=[b_wq, b_qn], writes=[bpr])
                            for c in range(4):
                                S.op("pe", lambda e: e.matmul(out=prs[0:64, :n], lhsT=wqrs[:, c, h * 64:(h + 1) * 64], rhs=qn[:, c, :n], start=(c == 0), stop=(c == 3)), reads=[b_wq, b_qn], writes=[bprs])
                            flush_norm()
                            s2, bs2 = stg[(k + 2) % 4], b_stg[(k + 2) % 4]
                            rope_rot(pr[0:64, :n], bpr, prs[0:64, :n], bprs, n, s2[0:64, :n], bs2)
                            S.dma("sp", D["QR_s"][h, :, t0:t0 + n], s2[0:64, :n], reads=[bs2], writes=[Buf()])
                            S.op("act", lambda e: e.activation(out=sqb[0][:, :n], in_=s_[:, :n], func=AF.Square, bias=cv[0:128, 2:3], scale=1.0), reads=[bs_], writes=[b_sqb[0]])
                            S.op("act", lambda e: e.activation(out=sqb[1][0:64, :n], in_=s2[0:64, :n], func=AF.Square, bias=cv[0:64, 2:3], scale=1.0), reads=[bs2], writes=[b_sqb[1]])
                            pend_norm.append(([(sqb[0][:, :n], 128, b_sqb[0]), (sqb[1][0:64, :n], 64, b_sqb[1])], 0))
                    flush_norm()
                    S.run()

                with ExitStack() as pB2:
                    s2_ = lambda n, s, d: pB2.enter_context(nc.sbuf_tensor(n, s, d))
                    w1k = s2_("w1k", [128, 8, 384], BF16)
                    wkk = s2_("wkk", [128, 2, 2048], BF16)
                    wkv = s2_("wkv", [128, 2, 2048], BF16)
                    b_wk = Buf()
                    for c in range(8):
                        load_w(w1k[:, c, 0:320], D["w_in1"][c * 128:(c + 1) * 128, 512:832], 128, 320, b_wk)
                        load_w(w1k[:, c, 320:384], D["w_in1"][c * 128:(c + 1) * 128, 2880:2944], 128, 64, b_wk)
                    for c in range(2):
                        load_w(wkk[:, c, :], D["wkv_k"][c * 128:(c + 1) * 128, :], 128, 2048, b_wk)
                        load_w(wkv[:, c, :], D["wkv_v"][c * 128:(c + 1) * 128, :], 128, 2048, b_wk)
                    ckv = s2_("ckv", [128, 2, 512], F32)
                    b_ckv = Buf()
                    kvn = s2_("kvn", [128, 2, 512], BF16)
                    b_kvn = Buf()
                    vst = [s2_(f"vst{i}", [128, 16, 129], BF16) for i in range(2)]
                    b_vst = [Buf(), Buf()]
                    for i in range(2):
                        S.op("pool", lambda e: e.memset(vst[i][:, :, 128:129], 1.0), writes=[b_vst[i]])
                    vc = 0
                    for si, (tt0, nt) in enumerate(STS):
                        t0, n = tt0 * 128, nt * 128
                        rope_tables(t0, n)
                        for c2 in range(2):
                            p_, bp_ = ps[c2 % 2], bps[c2 % 2]
                            for c in range(8):
                                S.op("pe", lambda e: e.matmul(out=p_[:, :n], lhsT=w1k[:, c, c2 * 128:(c2 + 1) * 128], rhs=uT[:, c, t0:t0 + n], start=(c == 0), stop=(c == 7)), reads=[b_wk, b_uT], writes=[bp_])
                            S.op("act", lambda e: e.copy(out=ckv[:, c2, :n], in_=p_[:, :n]), reads=[bp_], writes=[b_ckv])
                        feat_rmsnorm(ckv, b_ckv, 2, n, 4, kvn, b_kvn, 256)
                        pr, bpr = ps[2], bps[2]
                        prs, bprs = ps[3], bps[3]
                        for c in range(8):
                            S.op("pe", lambda e: e.matmul(out=pr[0:64, :n], lhsT=w1k[:, c, 256:320], rhs=uT[:, c, t0:t0 + n], start=(c == 0), stop=(c == 7)), reads=[b_wk, b_uT], writes=[bpr])
                        for c in range(8):
                            S.op("pe", lambda e: e.matmul(out=prs[0:64, :n], lhsT=w1k[:, c, 320:384], rhs=uT[:, c, t0:t0 + n], start=(c == 0), stop=(c == 7)), reads=[b_wk, b_uT], writes=[bprs])
                        rope_rot(pr[0:64, :n], bpr, prs[0:64, :n], bprs, n, KR[0:64, t0:t0 + n], b_KR)
                        S.op("act", lambda e: e.activation(out=sqb[1][0:64, :n], in_=KR[0:64, t0:t0 + n], func=AF.Square, bias=cv[0:64, 2:3], scale=1.0), reads=[b_KR], writes=[b_sqb[1]])
                        norm_max([(sqb[1][0:64, :n], 64, b_sqb[1])], 2)
                        for h in range(16):
                            k = cnt[0]
                            cnt[0] += 1
                            sqb, b_sqb = sqb_all[2 * (k % 2):2 * (k % 2) + 2], b_sqb_all[2 * (k % 2):2 * (k % 2) + 2]
                            p_, bp_ = ps[k % 2], bps[k % 2]
                            s_, bs_ = stg[k % 4], b_stg[k % 4]
                            for c in range(2):
                                S.op("pe", lambda e: e.matmul(out=p_[:, :n], lhsT=wkk[:, c, h * 128:(h + 1) * 128], rhs=kvn[:, c, :n], start=(c == 0), stop=(c == 1)), reads=[b_wk, b_kvn], writes=[bp_])
                            flush_norm()
                            S.op("act", lambda e: e.copy(out=s_[:, :n], in_=p_[:, :n]), reads=[bp_], writes=[bs_])
                            S.dma("sp", D["KN_s"][h, :, t0:t0 + n], s_[:, :n], reads=[bs_], writes=[Buf()])
                            S.op("act", lambda e: e.activation(out=sqb[0][:, :n], in_=s_[:, :n], func=AF.Square, bias=cv[0:128, 2:3], scale=1.0), reads=[bs_], writes=[b_sqb[0]])
                            pend_norm.append(([(sqb[0][:, :n], 128, b_sqb[0])], 1))
                        flush_norm()
                        for j in range(nt):
                            vs, bvs = vst[vc % 2], b_vst[vc % 2]
                            vc += 1
                            for n4 in range(4):
                                p_, bp_ = ps[4 + n4 % 2], bps[4 + n4 % 2]
                                for c in range(2):
                                    S.op("pe", lambda e: e.matmul(out=p_[:], lhsT=kvn[:, c, j * 128:(j + 1) * 128], rhs=wkv[:, c, n4 * 512:(n4 + 1) * 512], start=(c == 0), stop=(c == 1)), reads=[b_wk, b_kvn], writes=[bp_])
                                if n4 % 2 == 0:
                                    S.op("act", lambda e: e.copy(out=vs[:, 4 * n4:4 * n4 + 4, 0:128], in_=p_[:].rearrange("p (a b) -> p a b", a=4)), reads=[bp_], writes=[bvs])
                                else:
                                    S.op("dve", lambda e: e.tensor_copy(out=vs[:, 4 * n4:4 * n4 + 4, 0:128], in_=p_[:].rearrange("p (a b) -> p a b", a=4)), reads=[bp_], writes=[bvs])
                            S.dma("sp", D["V1_s"][tt0 + j], vs[:], reads=[bvs], writes=[Buf()])
                    flush_norm()
                    S.run()

                with ExitStack() as pB3:
                    s3 = lambda n, s, d: pB3.enter_context(nc.sbuf_tensor(n, s, d))
                    w1g = s3("w1g", [128, 8, 2048], BF16)
                    b_wg = Buf()
                    for c in range(8):
                        load_w(w1g[:, c, :], D["w_in1"][c * 128:(c + 1) * 128, 832:2880], 128, 2048, b_wg)
                    for si, (tt0, nt) in enumerate(STS):
                        t0, n = tt0 * 128, nt * 128
                        for oc in range(16):
                            k = cnt[0]
                            cnt[0] += 1
                            p_, bp_ = ps[k % 4], bps[k % 4]
                            s_, bs_ = stg[k % 4], b_stg[k % 4]
                            for c in range(8):
                                S.op("pe", lambda e: e.matmul(out=p_[:, :n], lhsT=w1g[:, c, oc * 128:(oc + 1) * 128], rhs=uT[:, c, t0:t0 + n], start=(c == 0), stop=(c == 7)), reads=[b_wg, b_uT], writes=[bp_])
                            S.op("act", lambda e: e.activation(out=s_[:, :n], in_=p_[:, :n], func=AF.Silu, bias=cv[:, 2:3], scale=1.0), reads=[bp_, bc], writes=[bs_])
                            S.dma("sp", D["G_s"][tt0:tt0 + nt, :, oc, :].rearrange("t p k -> p t k"), s_[:, :n].rearrange("p (t k) -> p t k", t=nt), reads=[bs_], writes=[Buf()])
                    S.run()

        if getattr(C, "stop_after", None) == "l1b":
            raise StopBuild()
        S.dma("sp", KR[64:128, :], KR[0:64, :], reads=[b_KR], writes=[b_KR])
        S.op("dve", lambda e: e.tensor_tensor(out=mx[:, 3:4], in0=mx[:, 1:2], in1=mx[:, 2:3], op=ALU.add), reads=[b_mx], writes=[b_mx])
        S.op("dve", lambda e: e.tensor_tensor(out=mx[:, 3:4], in0=mx[:, 3:4], in1=mx[:, 0:1], op=ALU.mult), reads=[b_mx], writes=[b_mx])
        S.op("act", lambda e: e.activation(out=mx[:, 4:5], in_=mx[:, 3:4], func=AF.Sqrt, bias=cv[:, 2:3], scale=1.0), reads=[b_mx, bc], writes=[b_mx])
        S.op("dve", lambda e: e.tensor_scalar(out=mx[:, 4:5], in0=mx[:, 4:5], scalar1=-SCALE, scalar2=None, op0=ALU.mult), reads=[b_mx], writes=[b_mx])
        S.run()

        with ExitStack() as pC:
            sC = lambda n, s, d: pC.enter_context(nc.sbuf_tensor(n, s, d))
            QN = [sC(f"QN{i}", [128, L], BF16) for i in range(2)]
            QR = [sC(f"QR{i}", [128, L], BF16) for i in range(2)]
            KN = [sC(f"KN{i}", [128, L], BF16) for i in range(2)]
            VA = [sC(f"VA{i}", [128, NT, 129], BF16) for i in range(2)]
            b_hd = [[Buf() for _ in range(5)] for _ in range(2)]
            NPT = 6
            SB = (0, 1, 6, 7)
            PT = [sC(f"PT{i}", [128, 512], BF16) for i in range(NPT)]
            b_PT = [Buf() for _ in range(NPT)]
            ost = [sC(f"ost{i}", [128, 128], BF16) for i in range(4)]
            b_ost = [Buf() for _ in range(4)]
            rl = sC("rl", [128, 8], F32)
            b_rl = Buf()
            mUib = sC("mUib", [128, 128], BF16)
            S.op("pool", lambda e: e.tensor_copy(out=mUib[:], in_=C.mUi[:]), reads=[bc], writes=[bc])
            kblk = 0
            oc_ = 0
            for h in range(16):
                hb = h % 2
                bh = b_hd[hb]
                S.dma("pool", QN[hb][:], D["QN_s"][h], writes=[bh[0]])
                S.dma("pool", QR[hb][0:64, :], D["QR_s"][h], writes=[bh[1]])
                S.dma("pool", QR[hb][64:128, :], D["QR_s"][h], writes=[bh[4]])
                S.dma("pool", KN[hb][:], D["KN_s"][h], writes=[bh[2]])
                S.dma("pool", VA[hb][:], D["V1_s"][:, :, h, :].rearrange("t p k -> p t k"), writes=[bh[3]])
                for (tt0, nt) in STS:
                    q0 = tt0 * 128
                    pO = [ps[2 + j] for j in range(nt)]
                    bpO = [bps[2 + j] for j in range(nt)]
                    blocks = []
                    for kt in range(tt0 + nt):
                        j0 = max(0, kt - tt0)
                        blocks.append((kt, j0, (nt - j0) * 128, q0 + j0 * 128))
                    state = {}
                    sinfo = {}

                    def emit_S(i, part="all", rg=0):
                        nonlocal kblk
                        if part == "rope":
                            kt, j0, ncol, qc0 = blocks[i]
                            pS, bpS, pt_, bpt = sinfo[i]
                            rs_ = slice(rg * 64, rg * 64 + 64)
                            S.op("pe", lambda e: e.matmul(out=pS[:, :ncol], lhsT=KR[rs_, kt * 128:(kt + 1) * 128], rhs=QR[hb][rs_, qc0:qc0 + ncol], start=False, stop=True), reads=[bh[1], bh[4], b_KR], writes=[bpS])
                            S.op("act", lambda e: e.activation(out=pt_[:, :ncol], in_=pS[:, :ncol], func=AF.Exp, bias=mx[:, 4:5], scale=SCALE), reads=[bpS, b_mx], writes=[bpt])
                            if kt >= tt0:
                                S.op("dve", lambda e: e.tensor_tensor(out=pt_[:, 0:128], in0=pt_[:, 0:128], in1=mUib[:], op=ALU.mult), reads=[bpt, bc], writes=[bpt])
                            return
                        if part == "nope":
                            kt, j0, ncol, qc0 = blocks[i]
                            pS, bpS = ps[SB[kblk % 4]], bps[SB[kblk % 4]]
                            pt_, bpt = PT[kblk % NPT], b_PT[kblk % NPT]
                            kblk += 1
                            state[i] = (pt_, bpt)
                            sinfo[i] = (pS, bpS, pt_, bpt)
                            S.op("pe", lambda e: e.matmul(out=pS[:, :ncol], lhsT=KN[hb][:, kt * 128:(kt + 1) * 128], rhs=QN[hb][:, qc0:qc0 + ncol], start=True, stop=False), reads=[bh[0], bh[2]], writes=[bpS])
                            return
                        kt, j0, ncol, qc0 = blocks[i]
                        pS, bpS = ps[SB[kblk % 4]], bps[SB[kblk % 4]]
                        pt_, bpt = PT[kblk % NPT], b_PT[kblk % NPT]
                        kblk += 1
                        state[i] = (pt_, bpt)
                        S.op("pe", lambda e: e.matmul(out=pS[:, :ncol], lhsT=KN[hb][:, kt * 128:(kt + 1) * 128], rhs=QN[hb][:, qc0:qc0 + ncol], start=True, stop=False), reads=[bh[0], bh[2]], writes=[bpS])
                        S.op("pe", lambda e: e.matmul(out=pS[:, :ncol], lhsT=KR[:, kt * 128:(kt + 1) * 128], rhs=QR[hb][:, qc0:qc0 + ncol], start=False, stop=True), reads=[bh[1], b_KR], writes=[bpS])
                        S.op("act", lambda e: e.activation(out=pt_[:, :ncol], in_=pS[:, :ncol], func=AF.Exp, bias=mx[:, 4:5], scale=SCALE), reads=[bpS, b_mx], writes=[bpt])
                        if kt >= tt0:
                            S.op("dve", lambda e: e.tensor_tensor(out=pt_[:, 0:128], in0=pt_[:, 0:128], in1=mUib[:], op=ALU.mult), reads=[bpt, bc], writes=[bpt])

                    def emit_PV(i):
                        kt, j0, ncol, qc0 = blocks[i]
                        pt_, bpt = state.pop(i)
                        for j in range(j0, nt):
                            cj = (j - j0) * 128
                            S.op("pe", lambda e: e.matmul(out=pO[j][:, 0:129], lhsT=pt_[:, cj:cj + 128], rhs=VA[hb][:, kt, :], start=(kt == 0), stop=(kt == tt0 + j)), reads=[bpt, bh[3]], writes=[bpO[j]])

                    nb = len(blocks)

                    def emit_pair(i):
                        if i + 1 < nb:
                            emit_S(i, "nope")
                            emit_S(i + 1, "nope")
                            emit_S(i, "rope", 0)
                            emit_S(i + 1, "rope", 1)
                        elif i < nb:
                            emit_S(i, "nope")
                            emit_S(i, "rope", 0)
                    emit_pair(0)
                    for i in range(0, nb, 2):
                        emit_pair(i + 2)
                        emit_PV(i)
                        if i + 1 < nb:
                            emit_PV(i + 1)
                    for j in range(nt):
                        S.op("dve", lambda e: e.reciprocal(out=rl[:, j:j + 1], in_=pO[j][:, 128:129]), reads=[bpO[j]], writes=[b_rl])
                        o_, bo_ = ost[oc_ % 4], b_ost[oc_ % 4]
                        oc_ += 1
                        S.op("dve", lambda e: e.tensor_scalar(out=o_[:], in0=pO[j][:, 0:128], scalar1=rl[:, j:j + 1], scalar2=None, op0=ALU.mult), reads=[bpO[j], b_rl], writes=[bo_])
                        S.dma("sp", D["O_s"][tt0 + j, :, h * 128:(h + 1) * 128], o_[:], reads=[bo_], writes=[Buf()])
            S.run()

        if getattr(C, "stop_after", None) == "l1c":
            raise StopBuild()
        with ExitStack() as pD:
            sD = lambda n, s, d: pD.enter_context(nc.sbuf_tensor(n, s, d))
            wo = sD("wo1", [128, 16, 1024], BF16)
            gpost = sD("gpost1_sb", [128, 1024], F32)
            b_w = Buf()
            with ExitStack() as pl:
                wst = [pl.enter_context(nc.sbuf_tensor(f"wstD{i}", [128, 2048], F32)) for i in range(2)]
                b_wst = [Buf(), Buf()]
                for c in range(8):
                    S.dma("sp", wst[c % 2][:].rearrange("p (j n) -> p j n", j=2),
                          D["w_out1"][c * 256:(c + 1) * 256, :].rearrange("(j p) n -> p j n", p=128), writes=[b_wst[c % 2]])
                    S.op("pool", lambda e: e.tensor_copy(out=wo[:, 2 * c:2 * c + 2, :], in_=wst[c % 2][:].rearrange("p (j n) -> p j n", j=2)),
                         reads=[b_wst[c % 2]], writes=[b_w])
                S.dma("sp", gpost[:], D["gpost1"].partition_broadcast(128), writes=[b_w])
                S.run()
            ot = [sD(f"ot{i}", [128, 2048], BF16) for i in range(2)]
            b_ot = [Buf(), Buf()]
            Gt = [sD(f"Gt1_{i}", [128, 16, 128], BF16) for i in range(2)]
            b_G = [Buf(), Buf()]
            hin = [sD(f"hin1_{i}", [128, 1024], F32) for i in range(2)]
            b_hin = [Buf(), Buf()]
            hout = [sD(f"hout1_{i}", [128, 1024], F32) for i in range(2)]
            b_hout = [Buf(), Buf()]
            zbs = [sD(f"zb1_{i}", [128, 16, 128], BF16) for i in range(2)]
            b_zbs = [Buf(), Buf()]
            junk = sD("junkD1", [128, 512], BF16)
            b_junk = Buf()
            pst_all = sD("pst1", [128, 8], F32)
            b_pst_all = [Buf(), Buf()]
            NOUT = C.NOUT
            for tt in range(NT):
                t0 = tt * 128
                pst = pst_all[:, 4 * (tt % 2):4 * (tt % 2) + 4]
                b_pst = b_pst_all[tt % 2]
                o, bo = ot[tt % 2], b_ot[tt % 2]
                G, bG = Gt[tt % 2], b_G[tt % 2]
                S.dma("pool", o[:], D["O_s"][tt], writes=[bo])
                S.dma("pool", G[:], D["G_s"][tt], writes=[bG])
                S.dma("pool", hin[tt % 2][:], D["H1"][t0:t0 + 128, :], writes=[b_hin[tt % 2]])
                zb, b_zb = zbs[tt % 2], b_zbs[tt % 2]
                for hf in range(2):
                    pz, bpz = ps[hf + 6 * (tt % 2)], bps[hf + 6 * (tt % 2)]
                    pzb = pz[:].bitcast(BF16)
                    for j in range(8):
                        hp = hf * 8 + j
                        S.op("pe", lambda e: e.transpose(out=pzb[:, j * 128:(j + 1) * 128], in_=o[:, hp * 128:(hp + 1) * 128], identity=C.ident[:]), reads=[bo, bc], writes=[bpz])
                    h8 = slice(hf * 8, hf * 8 + 8)
                    S.op("dve", lambda e: e.tensor_tensor(out=zb[:, h8, :], in0=pzb.rearrange("p (a b) -> p a b", a=8), in1=G[:, h8, :], op=ALU.mult), reads=[bpz, bG], writes=[b_zb])
                pm = [ps[2 + 2 * (tt % 2)], ps[3 + 2 * (tt % 2)]]
                bpm = [bps[2 + 2 * (tt % 2)], bps[3 + 2 * (tt % 2)]]
                for nh in range(2):
                    for hp in range(16):
                        S.op("pe", lambda e: e.matmul(out=pm[nh][:], lhsT=zb[:, hp, :], rhs=wo[:, hp, nh * 512:(nh + 1) * 512], start=(hp == 0), stop=(hp == 15)), reads=[b_zb, b_w], writes=[bpm[nh]])
                S.op("pool", lambda e: e.memset(pst, 0.0), writes=[b_pst])
                for nh in range(2):
                    S.op("act", lambda e: e.activation(out=junk[:], in_=pm[nh][:], func=AF.Square, bias=cv[:, 2:3], scale=1.0, accum_out=pst[:, nh:nh + 1]), reads=[bpm[nh], bc, b_junk], writes=[b_pst])
                S.op("dve", lambda e: e.tensor_tensor(out=pst[:, 2:3], in0=pst[:, 0:1], in1=pst[:, 1:2], op=ALU.add), reads=[b_pst], writes=[b_pst])
                S.op("act", lambda e: e.activation(out=pst[:, 3:4], in_=pst[:, 2:3], func=AF.Sqrt, bias=cv[:, 0:1], scale=1.0 / 1024), reads=[b_pst, bc], writes=[b_pst])
                S.op("dve", lambda e: e.reciprocal(out=pst[:, 3:4], in_=pst[:, 3:4]), reads=[b_pst], writes=[b_pst])
                ho, bho = hout[tt % 2], b_hout[tt % 2]
                for nh in range(2):
                    cs_ = slice(nh * 512, (nh + 1) * 512)
                    S.op("dve", lambda e: e.scalar_tensor_tensor(out=ho[:, cs_], in0=pm[nh][:], scalar=pst[:, 3:4], in1=gpost[:, cs_], op0=ALU.mult, op1=ALU.mult), reads=[bpm[nh], b_pst, b_w], writes=[bho])
                S.op("dve", lambda e: e.tensor_tensor(out=ho[:], in0=ho[:], in1=hin[tt % 2][:], op=ALU.add), reads=[bho, b_hin[tt % 2]], writes=[bho])
                r0 = t0 - 16
                lo = max(0, -r0)
                hi = min(128, NOUT - r0)
                if hi > lo:
                    S.dma("sp", D["out"][r0 + lo:r0 + hi, :], ho[lo:hi, :], reads=[bho], writes=[Buf()])
            S.run()


N_CORES = 8
SEQ = 4096
N_META = 16
STOP_AFTER = None
NT_FULL = 33


class Ctx:
    pass


def host_l0(inp, b, NT):
    L = NT * 128
    hfull = np.concatenate([inp["meta_tokens"], inp["x"][b]], axis=0)
    h0 = np.zeros((L, 1024), np.float32)
    n = min(L, hfull.shape[0])
    h0[:n] = hfull[:n]
    mu = inp["rwkv_mu"][0]
    pp = np.zeros((128, 160), np.float32)
    pp[:, :48] = mu.reshape(6, 8, 128).transpose(2, 0, 1).reshape(128, 48)
    for j, nm in enumerate(["rwkv_w0", "rwkv_a0", "rwkv_k_k", "rwkv_k_a", "rwkv_r_k", "rwkv_ln_w", "rwkv_ln_b"]):
        pp[:, 48 + 16 * j: 48 + 16 * (j + 1)] = inp[nm][0].reshape(16, 128).T
    return {"h0": h0, "gpre0": inp["norm_pre"][0], "gpost0": inp["norm_post"][0], "pp0": pp,
            "w_in0": inp["rwkv_w_in"][0], "w2": inp["rwkv_w2"][0], "a2": inp["rwkv_a2"][0], "w_out0": inp["rwkv_w_out"][0]}


def host_l1(inp):
    w_in = inp["mla_w_in"][0]
    kr = w_in[:, 768:832]
    w_in1 = np.concatenate([w_in, kr[:, 32:64], kr[:, 0:32]], axis=1)
    pp1 = np.zeros((128, 8), np.float32)
    pp1[:, 0:4] = inp["mla_q_norm"][0].reshape(4, 128).T
    pp1[:, 4:6] = inp["mla_kv_norm"][0].reshape(2, 128).T
    wq = inp["mla_w_q_up"][0].reshape(512, 16, 192)
    wq_n = np.ascontiguousarray(wq[:, :, 0:128]).reshape(512, 2048)
    wq_r = np.ascontiguousarray(wq[:, :, 128:192]).reshape(512, 1024)
    wq_rs = np.ascontiguousarray(np.concatenate([wq[:, :, 160:192], wq[:, :, 128:160]], axis=2)).reshape(512, 1024)
    wkv = inp["mla_w_kv_up"][0].reshape(256, 16, 256)
    wkv_k = np.ascontiguousarray(wkv[:, :, 0:128]).reshape(256, 2048)
    wkv_v = np.ascontiguousarray(wkv[:, :, 128:256]).reshape(256, 2048)
    return {"w_in1": np.ascontiguousarray(w_in1), "pp1": pp1, "wq_n": wq_n, "wq_r": wq_r, "wq_rs": wq_rs,
            "wkv_k": wkv_k, "wkv_v": wkv_v, "w_out1": inp["mla_w_out"][0],
            "gpre1": inp["norm_pre"][1], "gpost1": inp["norm_post"][1]}


def build(NT, NOUT):
    nc = bass.Bass("TRN2", target_bir_lowering=False)
    L = NT * 128
    D = {}

    def din(name, shape, dt=F32):
        D[name] = nc.dram_tensor(name, shape, dt, kind="ExternalInput").ap()

    def dsc(name, shape, dt=F32):
        D[name] = nc.dram_tensor(name, shape, dt, kind="Internal").ap()
    din("h0", [L, 1024]); din("gpre0", [1024]); din("gpost0", [1024]); din("pp0", [128, 160])
    din("w_in0", [1024, 8320]); din("w2", [64, 2048]); din("a2", [64, 2048]); din("w_out0", [2048, 1024])
    din("w_in1", [1024, 2944]); din("pp1", [128, 8]); din("wq_n", [512, 2048]); din("wq_r", [512, 1024]); din("wq_rs", [512, 1024])
    din("wkv_k", [256, 2048]); din("wkv_v", [256, 2048]); din("w_out1", [2048, 1024]); din("gpre1", [1024]); din("gpost1", [1024])
    for nm in ("R_s", "K_s", "V_s"):
        dsc(nm, [NT, 128, 16, 128])
    dsc("G_s", [NT, 128, 16, 128], BF16); dsc("Z_s", [NT, 128, 16, 128], BF16)
    dsc("H1", [L, 1024])
    dsc("QN_s", [16, 128, L], BF16); dsc("QR_s", [16, 64, L], BF16); dsc("KN_s", [16, 128, L], BF16)
    dsc("V1_s", [NT, 128, 16, 129], BF16); dsc("O_s", [NT, 128, 2048], BF16)
    D["out"] = nc.dram_tensor("out", [NOUT, 1024], F32, kind="ExternalOutput").ap()
    with ExitStack() as st:
        C = Ctx()
        C.nc = nc; C.stack = st; C.S = Sched(nc, st); C.NT = NT; C.dram = D; C.dbg_barrier = 0; C.NOUT = NOUT
        C.stop_after = STOP_AFTER
        consts(C)
        C.S.run()
        try:
            layer0(C)
            with ExitStack() as mid:
                C.uT1 = mid.enter_context(nc.sbuf_tensor("uT1", [128, 8, L], BF16))
                C.b_uT1 = Buf("uT1")
                C.gpre1 = mid.enter_context(nc.sbuf_tensor("gpre1_x", [128, 1024], F32))
                C.b_gpre1 = Buf("gpre1")
                C.S.dma("sp", C.gpre1[:], D["gpre1"].partition_broadcast(128), writes=[C.b_gpre1])
                phase_x(C)
                layer1(C)
        except StopBuild:
            C.S.ops = {e: [] for e in C.S.ENGS}
        C.S.run()
    return nc


def kernel(**inputs):
    inp = {k: np.asarray(v) for k, v in inputs.items()}
    B = inp["x"].shape[0]
    nc = build(NT_FULL, SEQ)
    shared = host_l1(inp)
    in_maps = []
    for b in range(B):
        m = host_l0(inp, b, NT_FULL)
        m.update(shared)
        in_maps.append({k: np.ascontiguousarray(v, dtype=np.float32) for k, v in m.items()})
    res = run_bass_kernel_spmd(nc, in_maps, core_ids=list(range(B)))
    out = np.stack([np.asarray(res.results[b]["out"]) for b in range(B)], axis=0)
    return out.astype(np.float32)
```
